# Optimizing a Trainium2 kernel written in Bass

```python
import jax, jax.numpy as jnp
from jax import lax
import numpy as np

D_MODEL = 1024
BATCH = 4
SEQ = 8192
DEPTH = 2
DEC_BATCH = 4
DEC_SEQ = 4096
PAST_LEN = 128

D_MIX = D_MODEL
CONV_WIDTH = D_MIX // 4
CONV_KERNEL = 31
CONV_PAD = CONV_KERNEL // 2
SGU_WIDTH = D_MIX // 4
SGU_HEADS = 4
SGU_HEAD_DIM = SGU_WIDTH // SGU_HEADS
SGU_CHUNK = 128
HGRN_WIDTH = D_MIX // 2
HGRN_EXPAND = 128
HGRN_HEADS = HGRN_WIDTH // HGRN_EXPAND
HGRN_HEAD_V = HGRN_WIDTH // HGRN_HEADS
HGRN_CHUNK = 64
SPLIT_SIZES = (CONV_WIDTH, CONV_WIDTH, CONV_WIDTH,
               SGU_WIDTH, SGU_WIDTH, SGU_WIDTH,
               HGRN_WIDTH, HGRN_WIDTH, HGRN_WIDTH, HGRN_WIDTH, HGRN_WIDTH)
SPLIT_OFFSETS = tuple(int(o) for o in np.cumsum(SPLIT_SIZES)[:-1])
D_IN = int(sum(SPLIT_SIZES))
EPS = 1e-6

kernel_name = "hymba_conv_sgu_hgrn2_bidir_encoder"


def rmsnorm(x, g):
    xf = x.astype(jnp.float32)
    y = xf * lax.rsqrt(jnp.mean(xf * xf, axis=-1, keepdims=True) + EPS)
    return (y * g.astype(jnp.float32)).astype(x.dtype)


def layernorm(x, g, b):
    xf = x.astype(jnp.float32)
    mu = jnp.mean(xf, axis=-1, keepdims=True)
    xc = xf - mu
    y = xc * lax.rsqrt(jnp.mean(xc * xc, axis=-1, keepdims=True) + EPS)
    return (y * g.astype(jnp.float32) + b.astype(jnp.float32)).astype(x.dtype)


def conv_branch(a_val, a_glu, a_gate, conv_w, conv_b, ln_g, ln_b, pw):
    y = a_val * jax.nn.sigmoid(a_glu)
    y = lax.conv_general_dilated(
        y, conv_w[:, None, :], window_strides=(1,), padding=[(CONV_PAD, CONV_PAD)],
        dimension_numbers=('NWC', 'WIO', 'NWC'), feature_group_count=CONV_WIDTH) + conv_b
    y = jax.nn.silu(layernorm(y, ln_g, ln_b))
    y = y @ pw
    return y * jax.nn.silu(a_gate)


def sgu_branch(b_u, b_v, b_gate, ln_g, ln_b, w_s, b_s):
    bsz, seq, _ = b_u.shape
    u = jax.nn.gelu(b_u, approximate=False)
    v = layernorm(jax.nn.gelu(b_v, approximate=False), ln_g, ln_b)
    v = v.reshape(bsz, seq // SGU_CHUNK, SGU_CHUNK, SGU_HEADS, SGU_HEAD_DIM)
    s = jnp.einsum('hts,bnshd->bnthd', w_s, v) + b_s.T[:, :, None]
    s = s.reshape(bsz, seq, SGU_WIDTH)
    return u * s * jax.nn.silu(b_gate)


def hgrn2_scan(q, k, logf, v):
    bsz, seq, nh, dk = q.shape
    dv = v.shape[-1]
    n = seq // HGRN_CHUNK

    def chunks(t):
        return t.reshape(bsz, n, HGRN_CHUNK, nh, t.shape[-1]).transpose(1, 0, 3, 2, 4)

    qc, kc, vc = chunks(q), chunks(k), chunks(v)
    bc = jnp.cumsum(chunks(logf), axis=3)
    mask = jnp.tril(jnp.ones((HGRN_CHUNK, HGRN_CHUNK), dtype=bool))

    def step(state, inp):
        qt, kt, vt, bt = inp
        diff = bt[:, :, :, None, :] - bt[:, :, None, :, :]
        decay = jnp.exp(jnp.where(mask[:, :, None], diff, -jnp.inf))
        scores = jnp.einsum('bhtk,bhsk,bhtsk->bhts', qt, kt, decay)
        o = (jnp.einsum('bhts,bhsv->bhtv', scores, vt)
             + jnp.einsum('bhtk,bhkv->bhtv', qt * jnp.exp(bt), state))
        b_last = bt[:, :, -1:, :]
        state = (jnp.exp(b_last[:, :, 0, :, None]) * state
                 + jnp.einsum('bhsk,bhsv->bhkv', kt * jnp.exp(b_last - bt), vt))
        return state, o

    s0 = jnp.zeros((bsz, nh, dk, dv), jnp.float32)
    _, o = lax.scan(step, s0, (qc, kc, vc, bc))
    return o.transpose(1, 0, 3, 2, 4).reshape(bsz, seq, nh, dv)


def hgrn_branch(c_q, c_i, c_ff, c_fb, c_gate, lb_f, lb_b, norm_g):
    bsz, seq, _ = c_q.shape

    def heads(t):
        return t.astype(jnp.float32).reshape(bsz, seq, HGRN_HEADS, -1)

    q, v = heads(c_q), heads(c_i)

    def forget(z, lb):
        lb = lb.astype(jnp.float32).reshape(HGRN_HEADS, HGRN_EXPAND)
        logf = jnp.logaddexp(jnp.log(lb), jnp.log1p(-lb) + jax.nn.log_sigmoid(z))
        k = (1.0 - lb) * jax.nn.sigmoid(-z)
        return k, logf

    k_f, logf_f = forget(heads(c_ff), lb_f)
    k_b, logf_b = forget(heads(c_fb), lb_b)
    flip = lambda t: jnp.flip(t, axis=1)
    o_f = hgrn2_scan(q, k_f, logf_f, v)
    o_b = flip(hgrn2_scan(flip(q), flip(k_b), flip(logf_b), flip(v)))
    o = rmsnorm(o_f + o_b, norm_g)
    o = o.reshape(bsz, seq, HGRN_WIDTH).astype(c_gate.dtype)
    return o * jax.nn.silu(c_gate)


def trunk(x, norm_g, w_in, conv_w, conv_b, conv_ln_g, conv_ln_b, conv_pw,
          sgu_ln_g, sgu_ln_b, sgu_w, sgu_b, hgrn_lb_fwd, hgrn_lb_bwd, hgrn_norm_g,
          w_out, final_norm_g):
    def lower_bounds(p):
        lb = jnp.cumsum(jax.nn.softmax(p.astype(jnp.float32), axis=0), axis=0)
        return lb - lb[0:1]

    lbf_all = lower_bounds(hgrn_lb_fwd)
    lbb_all = lower_bounds(hgrn_lb_bwd)
    for l in range(DEPTH):
        h = rmsnorm(x, norm_g[l])
        p = h @ w_in[l]
        (a_val, a_glu, a_gate, b_u, b_v, b_gate,
         c_q, c_i, c_ff, c_fb, c_gate) = jnp.split(p, SPLIT_OFFSETS, axis=-1)
        y_a = conv_branch(a_val, a_glu, a_gate, conv_w[l], conv_b[l],
                          conv_ln_g[l], conv_ln_b[l], conv_pw[l])
        y_b = sgu_branch(b_u, b_v, b_gate, sgu_ln_g[l], sgu_ln_b[l], sgu_w[l], sgu_b[l])
        y_c = hgrn_branch(c_q, c_i, c_ff, c_fb, c_gate, lbf_all[l], lbb_all[l], hgrn_norm_g[l])
        y = jnp.concatenate([y_a, y_b, y_c], axis=-1)
        x = x + y @ w_out[l]
    return rmsnorm(x, final_norm_g)


def setup_inputs(seed: int = 0) -> dict:
    key = jax.random.key(seed)
    ks = jax.random.split(key, 20)
    f32 = jnp.float32
    nrm = lambda k, shape, scale: (jax.random.normal(k, shape, f32) * scale).astype(f32)
    return {
        "x_prompt": nrm(ks[0], (BATCH, SEQ, D_MODEL), 1.0),
        "x_sample": nrm(ks[1], (DEC_BATCH, DEC_SEQ, D_MODEL), 1.0),
        "norm_g": 1.0 + nrm(ks[2], (DEPTH, D_MODEL), 0.02),
        "w_in": nrm(ks[3], (DEPTH, D_MODEL, D_IN), D_MODEL ** -0.5),
        "conv_w": nrm(ks[4], (DEPTH, CONV_KERNEL, CONV_WIDTH), CONV_KERNEL ** -0.5),
        "conv_b": nrm(ks[5], (DEPTH, CONV_WIDTH), 0.02),
        "conv_ln_g": 1.0 + nrm(ks[6], (DEPTH, CONV_WIDTH), 0.02),
        "conv_ln_b": nrm(ks[7], (DEPTH, CONV_WIDTH), 0.02),
        "conv_pw": nrm(ks[8], (DEPTH, CONV_WIDTH, CONV_WIDTH), CONV_WIDTH ** -0.5),
        "sgu_ln_g": 1.0 + nrm(ks[9], (DEPTH, SGU_WIDTH), 0.02),
        "sgu_ln_b": nrm(ks[10], (DEPTH, SGU_WIDTH), 0.02),
        "sgu_w": nrm(ks[11], (DEPTH, SGU_HEADS, SGU_CHUNK, SGU_CHUNK), SGU_CHUNK ** -0.5),
        "sgu_b": 1.0 + nrm(ks[12], (DEPTH, SGU_HEADS, SGU_CHUNK), 0.02),
        "hgrn_lb_fwd": nrm(ks[13], (DEPTH, HGRN_WIDTH), 0.5),
        "hgrn_lb_bwd": nrm(ks[14], (DEPTH, HGRN_WIDTH), 0.5),
        "hgrn_norm_g": 1.0 + nrm(ks[15], (DEPTH, HGRN_HEAD_V), 0.02),
        "w_out": nrm(ks[16], (DEPTH, D_MIX, D_MODEL), (2.0 * D_MIX) ** -0.5),
        "final_norm_g": 1.0 + nrm(ks[17], (D_MODEL,), 0.02),
    }


def reference(x_prompt, x_sample, norm_g, w_in, conv_w, conv_b, conv_ln_g, conv_ln_b, conv_pw,
              sgu_ln_g, sgu_ln_b, sgu_w, sgu_b, hgrn_lb_fwd, hgrn_lb_bwd, hgrn_norm_g,
              w_out, final_norm_g):
    y_prompt = trunk(x_prompt, norm_g, w_in, conv_w, conv_b, conv_ln_g, conv_ln_b, conv_pw,
                     sgu_ln_g, sgu_ln_b, sgu_w, sgu_b, hgrn_lb_fwd, hgrn_lb_bwd, hgrn_norm_g,
                     w_out, final_norm_g)
    y_sample = trunk(x_sample, norm_g, w_in, conv_w, conv_b, conv_ln_g, conv_ln_b, conv_pw,
                     sgu_ln_g, sgu_ln_b, sgu_w, sgu_b, hgrn_lb_fwd, hgrn_lb_bwd, hgrn_norm_g,
                     w_out, final_norm_g)
    return (y_prompt, y_sample)
```

```python
import numpy as np
import ml_dtypes
from contextlib import ExitStack
import concourse.bass as bass
import concourse.mybir as mybir
from concourse.bass_utils import run_bass_kernel_spmd

F32 = mybir.dt.float32
BF16 = mybir.dt.bfloat16
AF = mybir.ActivationFunctionType
ALU = mybir.AluOpType

D = 1024
DIN = 4096
T = 512
EPS = 1e-6
SEQ_P = 8192
SEQ_S = 4096

C_AVAL, C_AGLU, C_AGATE, C_BU, C_BV, C_BGATE, C_Q, C_I, C_FF, C_FB, C_GATE = (
    0, 256, 512, 768, 1024, 1280, 1536, 2048, 2560, 3072, 3584)


class StopBuild(Exception):
    pass


class Sched:
    def __init__(self, nc, es):
        self.nc = nc
        self.es = es
        self.engs = {"pe": nc.tensor, "act": nc.scalar, "dve": nc.vector, "pool": nc.gpsimd, "sp": nc.sync}
        self.sems = {}
        self.cnt = {}
        for e in ("pe", "act", "dve", "pool"):
            self.sems[e] = es.enter_context(nc.semaphore("s_" + e))
            self.cnt[e] = 0
        self.known = {e: {} for e in self.engs}
        self.last_w = {}
        self.readers = {}
        self.nwaits = 0
        self.nops = 0
        self.limit = 10 ** 9
        self.log = []

    def _deps(self, reads, writes):
        deps = {}

        def add(tok):
            if tok is None:
                return
            k, v = tok
            if deps.get(k, 0) < v:
                deps[k] = v

        for r in reads:
            add(self.last_w.get(r))
        for w in writes:
            add(self.last_w.get(w))
            for k, v in self.readers.get(w, {}).items():
                add((k, v))
        return deps

    def _emit_waits(self, eng, deps):
        kn = self.known[eng]
        for k, v in deps.items():
            if eng == "pe" and k == "pe":
                continue
            if kn.get(k, 0) >= v:
                continue
            self.engs[eng].wait_ge(self.sems[k], v)
            kn[k] = v
            self.nwaits += 1

    def _record(self, tok, reads, writes):
        for w in writes:
            self.last_w[w] = tok
            self.readers[w] = {}
        k, v = tok
        for r in reads:
            d = self.readers.setdefault(r, {})
            if d.get(k, 0) < v:
                d[k] = v

    def op(self, eng, fn, reads=(), writes=()):
        if self.nops >= self.limit:
            raise StopBuild()
        self.log.append((self.nops, eng, reads, writes))
        deps = self._deps(reads, writes)
        self._emit_waits(eng, deps)
        inst = fn()
        self.cnt[eng] += 1
        inst.then_inc(self.sems[eng], 1)
        self._record((eng, self.cnt[eng]), reads, writes)
        self.nops += 1

    def dma(self, key, fn, reads=(), writes=(), queue="sp"):
        if self.nops >= self.limit:
            raise StopBuild()
        self.log.append((self.nops, "dma", reads, writes))
        if key not in self.sems:
            self.sems[key] = self.es.enter_context(self.nc.semaphore("d_" + str(key).replace(" ", "")))
            self.cnt[key] = 0
        deps = self._deps(reads, writes)
        self._emit_waits(queue, deps)
        inst = fn()
        self.cnt[key] += 16
        inst.then_inc(self.sems[key], 16)
        self._record((key, self.cnt[key]), reads, writes)
        self.nops += 1

    def finish(self, keys):
        for k in keys:
            if k in self.cnt and self.cnt[k] > 0:
                self.engs["sp"].wait_ge(self.sems[k], self.cnt[k])


def build(L, debug=False, stage=99, limit=None):
    NT = L // T
    nc = bass.Bass("TRN2", target_bir_lowering=False, dynamic_dma_scratch_size=1024)
    es = ExitStack()
    es.enter_context(nc.allow_non_contiguous_dma(reason="small param vectors"))
    es.enter_context(nc.allow_low_precision(reason="bf16 matmul operands, fp32 accumulation"))

    def din(name, shape, dt=F32):
        return nc.dram_tensor(name, list(shape), dt, kind="ExternalInput").ap()

    x_d = din("x", [L, D])
    norm_g_d = din("norm_g", [2, D])
    w_in_d = din("w_in", [2, D, DIN])
    conv_w_d = din("conv_w", [2, 31, 256])
    conv_b_d = din("conv_b", [2, 256])
    cln_g_d = din("conv_ln_g", [2, 256])
    cln_b_d = din("conv_ln_b", [2, 256])
    conv_pw_d = din("conv_pw", [2, 256, 256])
    sln_g_d = din("sgu_ln_g", [2, 256])
    sln_b_d = din("sgu_ln_b", [2, 256])
    sgu_w_d = din("sgu_w", [2, 4, 128, 128])
    sgu_b_d = din("sgu_b", [2, 4, 128])
    lbf_d = din("hgrn_lb_fwd", [2, 512])
    lbb_d = din("hgrn_lb_bwd", [2, 512])
    hng_d = din("hgrn_norm_g", [2, 128])
    w_out_d = din("w_out", [2, D, D])
    fng_d = din("final_norm_g", [D])
    cbf_d = din("cbf", [128, 640], BF16)
    csm_d = din("csm", [128, 512])
    y_d = nc.dram_tensor("y", [L, D], F32, kind="ExternalOutput").ap()
    x1_d = nc.dram_tensor("x1s", [L, D], F32, kind="Internal").ap()
    sbnd_d = nc.dram_tensor("sbnd", [NT, 128, 512], F32, kind="Internal").ap()
    dbg_outs = {}

    def sb(name, shape, dt):
        return es.enter_context(nc.sbuf_tensor("sb_" + name, list(shape), dt))

    Wi = sb("Wi", [128, 8, DIN], BF16)
    Wo = sb("Wo", [128, 8, D], BF16)
    Wpw = sb("Wpw", [128, 2, 256], BF16)
    Wcd = sb("Wcd", [128, 2, 31, 128], BF16)
    Wsg = sb("Wsg", [128, 4, 128], BF16)
    cbf = sb("cbf", [128, 640], BF16)
    IDB = cbf[:, 0:128]
    MASKN = [cbf[:, 128:256], cbf[:, 256:384]]
    ONES128 = cbf[:, 384:512]
    ONES256 = cbf[:, 512:640]
    csm = sb("csm", [128, 512], F32)
    identF = sb("identF", [128, 128], F32)
    fng = sb("fng", [128, D], F32)
    sgg = sb("sgg", [128, 2, 256], F32)
    sgb = sb("sgb", [128, 2, 256], F32)
    sbias = sb("sbias", [128, 2, 2, 128], F32)
    sv = sb("sv", [128, 64], F32)
    lbc = sb("lbc", [128, 2, 2, 2, 4], F32)
    cw = sb("cw", [128, 2, 31], F32)
    SV_NG, SV_CB, SV_CLG, SV_CLB, SV_NCLG, SV_NCLB, SV_GV, SV_LBF, SV_LBB, SV_EPS, SV_M05, SV_ONE = (
        0, 16, 20, 24, 28, 32, 36, 38, 46, 54, 55, 56)
    xin = sb("xin", [128, 2, D], F32)
    xs = sb("xs", [128, D], BF16)
    hT = sb("hT", [128, 8, T], BF16)
    ybuf = sb("ybuf", [128, 2, 544], BF16)
    ag = sb("ag", [128, 2, 528], BF16)
    tA = sb("tA", [128, 3, T], F32)
    ub = sb("ub", [128, 2, T], BF16)
    bg = sb("bg", [128, 2, T], BF16)
    gvb = sb("gvb", [128, 1, 256], F32)
    vsn = sb("vsn", [128, 4, 256], BF16)
    vtok = sb("vtok", [128, 4, T], BF16)
    gs = sb("gs", [128, 4, T], BF16)
    Ff = sb("Ff", [128, T], F32)
    Fb = sb("Fb", [128, T + 1], F32)
    CPf2 = sb("CPf", [128, 2, T], F32)
    Eb = sb("Eb", [128, T], F32)
    RC = sb("RC", [128, 1, T], F32)
    QK2 = sb("QK", [128, 2, 4, T], BF16)
    kT = sb("kT", [128, 2, T], BF16)
    Am = sb("Am", [128, 2, T], BF16)
    Sbf = sb("Sbf", [128, 2, 8, 128], BF16)
    Gf2 = sb("Gf", [128, 4, 2, 128], F32)
    Sb2 = sb("Sb", [128, 4, 2, 128], F32)
    dcar = sb("dcar", [128, 4], F32)
    DB2 = sb("DB", [128, 2, 8], F32)
    DB4 = sb("DB4", [128, 4, 8], F32)
    big = sb("big", [128, 4, T], F32)
    osq = sb("osq", [128, 4, T], BF16)
    rstd = sb("rstd", [128, T], F32)
    yT = sb("yT", [128, 8, 528], BF16)
    cn = big
    cnb = osq
    csq = osq[:, 2:4]
    xres = sb("xres", [128, 2, D], F32)
    st6 = sb("st6", [128, 4, 6], F32)
    mv = sb("mv", [128, 4, 2], F32)
    sm = sb("sm", [128, 32], F32)

    ps = [es.enter_context(nc.psum_tensor("ps%d" % i, [128, 512], F32)) for i in range(8)]

    S = Sched(nc, es)
    if limit is not None:
        S.limit = limit
    build.S = S
    free_banks = list(range(8))

    def ps_alloc():
        assert free_banks, "out of PSUM banks"
        return free_banks.pop(0)

    def ps_free(b):
        free_banks.append(b)

    def PS(b):
        return ("ps", b)

    out_keys = []

    def rowres(prefix, r0, n):
        res = []
        for g in range(r0 // 128, (r0 + n - 1) // 128 + 1):
            lo0, hi0, end = g * 128, g * 128 + 112, (g + 1) * 128
            if r0 < hi0 and r0 + n > lo0:
                res.append((prefix, g, "lo"))
            if r0 < end and r0 + n > hi0:
                res.append((prefix, g, "hi"))
        return tuple(res)

    def setup():
        loads = []

        def ld(out, in_, res):
            k = ("setup", len(loads))
            S.dma(k, lambda: nc.sync.dma_start(out=out, in_=in_), reads=(), writes=(("setup_r", len(loads)),))
            loads.append((k, res))

        ld(cbf[:], cbf_d[:, :], "cbf")
        ld(csm[:], csm_d[:, :], "csm")
        ld(fng[:], fng_d.partition_broadcast(128), "fng")
        for l in range(2):
            ld(sgg[:, l, :], sln_g_d[l, :].partition_broadcast(128), "sgg")
            ld(sgb[:, l, :], sln_b_d[l, :].partition_broadcast(128), "sgb")
            for h in range(4):
                ld(sbias[(h % 2) * 64:(h % 2) * 64 + 64, l, h // 2, :],
                   sgu_b_d[l, h, :].partition_broadcast(64), "sbias")
        ld(sv[:, SV_NG:SV_NG + 16].rearrange("p (l k) -> p l k", l=2),
           norm_g_d.rearrange("l (k p) -> p l k", p=128), "sv")
        for col, src in ((SV_CB, conv_b_d), (SV_CLG, cln_g_d), (SV_CLB, cln_b_d)):
            ld(sv[:, col:col + 4].rearrange("p (l j) -> p l j", l=2),
               src.rearrange("l (j p) -> p l j", p=128), "sv")
        ld(sv[:, SV_GV:SV_GV + 2], hng_d.rearrange("l p -> p l"), "sv")
        ld(sv[:, SV_LBF:SV_LBF + 8].rearrange("p (l h) -> p l h", l=2),
           lbf_d.rearrange("l (h p) -> p l h", p=128), "sv")
        ld(sv[:, SV_LBB:SV_LBB + 8].rearrange("p (l h) -> p l h", l=2),
           lbb_d.rearrange("l (h p) -> p l h", p=128), "sv")
        for eng in ("pe", "act", "dve", "pool", "sp"):
            for k, _ in loads:
                S.engs[eng].wait_ge(S.sems[k], 16)
                S.known[eng][k] = 16
        S.op("dve", lambda: nc.vector.memset(sv[:, SV_EPS:SV_EPS + 1], EPS), writes=("sv_c",))
        S.op("dve", lambda: nc.vector.memset(sv[:, SV_M05:SV_M05 + 1], -0.5), writes=("sv_c",))
        S.op("dve", lambda: nc.vector.memset(sv[:, SV_ONE:SV_ONE + 1], 1.0), writes=("sv_c",))
        S.op("pool", lambda: nc.gpsimd.memset(Fb[:, 0:1], 1.0), writes=("Fb",))
        for s_ in range(2):
            S.op("pool", lambda: nc.gpsimd.memset(xres[:, s_, :], 0.0), writes=(("xres", s_),))
        S.op("dve", lambda: nc.vector.tensor_copy(out=identF[:], in_=IDB), reads=("cbf",), writes=("identF",))
        S.op("dve", lambda: nc.vector.tensor_scalar(out=sv[:, SV_NCLG:SV_NCLG + 8], in0=sv[:, SV_CLG:SV_CLG + 8],
                                                    scalar1=-1.0, scalar2=None, op0=ALU.mult),
             reads=("sv",), writes=("sv_d",))
        S.op("dve", lambda: nc.vector.tensor_scalar(out=sv[:, SV_GV:SV_GV + 2], in0=sv[:, SV_GV:SV_GV + 2],
                                                    scalar1=0.5, scalar2=None, op0=ALU.mult),
             reads=("sv",), writes=("sv",))
        S.op("dve", lambda: nc.vector.memset(lbc[:, 0], 0.5), writes=("lbc",))
        for d_, col in ((0, SV_LBF), (1, SV_LBB)):
            S.op("dve", lambda: nc.vector.tensor_tensor(out=sm[:, 0:4], in0=sv[:, col + 4:col + 8],
                                                        in1=sv[:, col:col + 4], op=ALU.subtract),
                 reads=("sv",), writes=("sm",))
            S.op("act", lambda: nc.scalar.activation(out=sm[:, 4:8], in_=sm[:, 0:4], func=AF.Tanh, scale=0.5),
                 reads=("sm",), writes=("sm2",))
            S.op("dve", lambda: nc.vector.tensor_scalar(out=lbc[:, 1, d_, 0, :], in0=sm[:, 4:8], scalar1=-0.25,
                                                        scalar2=0.25, op0=ALU.mult, op1=ALU.add),
                 reads=("sm2",), writes=("lbc",))
            S.op("dve", lambda: nc.vector.tensor_scalar(out=lbc[:, 1, d_, 1, :], in0=sm[:, 4:8], scalar1=0.25,
                                                        scalar2=0.75, op0=ALU.mult, op1=ALU.add),
                 reads=("sm2",), writes=("lbc",))

    def load_weights(l):
        stg = [(xin[:, 0, :], ("xin", 0)), (xin[:, 1, :], ("xin", 1)), (xres[:, 0, :], ("xres", 0)),
               (xres[:, 1, :], ("xres", 1))]
        slot = [0]

        def nslot4():
            s_ = slot[0]
            slot[0] = (slot[0] + 1) % 4
            return s_

        def nslot():
            s_ = slot[0] % 2
            slot[0] = (slot[0] + 1) % 4
            return s_

        first = {"act": True, "dve": True, "pool": True}
        n = 0
        for kc in range(8):
            for q in range(4):
                buf, bkey = stg[nslot4()]
                S.dma(bkey, lambda: nc.sync.dma_start(
                    out=buf, in_=w_in_d[l, kc * 128:(kc + 1) * 128, q * 1024:(q + 1) * 1024]), writes=(bkey,))
                eng = "act" if n % 2 == 0 else "dve"
                wr = ("Wi", ("Wi", n)) if first[eng] else (("Wi", n),)
                first[eng] = False
                if eng == "act":
                    S.op("act", lambda: nc.scalar.activation(
                        out=Wi[:, kc, q * 1024:(q + 1) * 1024], in_=buf, func=AF.Copy,
                        scale=sv[:, SV_NG + l * 8 + kc:SV_NG + l * 8 + kc + 1]),
                        reads=(bkey, "sv"), writes=wr)
                else:
                    S.op("dve", lambda: nc.vector.tensor_scalar(
                        out=Wi[:, kc, q * 1024:(q + 1) * 1024], in0=buf,
                        scalar1=sv[:, SV_NG + l * 8 + kc:SV_NG + l * 8 + kc + 1], scalar2=None, op0=ALU.mult),
                        reads=(bkey, "sv"), writes=wr)
                n += 1
        for mc in range(8):
            buf, bkey = stg[nslot4()]
            S.dma(bkey, lambda: nc.sync.dma_start(out=buf, in_=w_out_d[l, mc * 128:(mc + 1) * 128, :]), writes=(bkey,))
            wr = ("Wo", ("Wo", mc)) if first["pool"] else (("Wo", mc),)
            first["pool"] = False
            S.op("pool", lambda: nc.gpsimd.tensor_copy(out=Wo[:, mc, :], in_=buf), reads=(bkey,), writes=wr)
        S._emit_waits("pe", {"act": S.cnt["act"], "dve": S.cnt["dve"], "pool": S.cnt["pool"]})
        slot[0] = 0
        s_ = nslot()
        S.dma(("xin", s_), lambda: nc.sync.dma_start(
            out=xin[:, s_, 0:512].rearrange("p (j c) -> p j c", j=2),
            in_=conv_pw_d[l].rearrange("(j p) c -> p j c", p=128)), writes=(("xin", s_),))
        S.op("dve", lambda: nc.vector.tensor_scalar(
            out=Wpw[:].rearrange("p j c -> p (j c)"), in0=xin[:, s_, 0:512], scalar1=0.5, scalar2=None,
            op0=ALU.mult), reads=(("xin", s_),), writes=("Wpw",))
        s_ = nslot()
        S.dma(("xin", s_), lambda: nc.sync.dma_start(out=xin[0:31, s_, 0:256], in_=conv_w_d[l]), writes=(("xin", s_),))
        b = ps_alloc()
        for j in range(2):
            S.op("pe", lambda: nc.tensor.transpose(out=ps[b][:, j * 32:j * 32 + 31],
                                                   in_=xin[0:31, s_, j * 128:(j + 1) * 128],
                                                   identity=identF[0:31, 0:31]),
                 reads=(("xin", s_), "identF"), writes=(PS(b),))
        S.op("dve", lambda: nc.vector.tensor_copy(
            out=cw[:], in_=ps[b][:, 0:64].rearrange("p (j t) -> p j t", j=2)[:, :, 0:31]),
            reads=(PS(b),), writes=("cw",))
        ps_free(b)
        for j in range(2):
            for tp in range(31):
                S.op("pool", lambda: nc.gpsimd.tensor_scalar(
                    out=Wcd[:, j, tp, :], in0=IDB, scalar1=cw[:, j, tp:tp + 1], scalar2=0.5,
                    op0=ALU.mult, op1=ALU.mult), reads=("cw", "cbf"), writes=("Wcd",))
        s_ = nslot()
        S.dma(("xin", s_), lambda: nc.sync.dma_start(
            out=xin[:, s_, 0:512].rearrange("p (h s) -> p h s", h=4),
            in_=sgu_w_d[l].rearrange("h t s -> t h s")), writes=(("xin", s_),))
        b = ps_alloc()
        for h in range(4):
            S.op("pe", lambda: nc.tensor.transpose(out=ps[b][:, h * 128:(h + 1) * 128],
                                                   in_=xin[:, s_, h * 128:(h + 1) * 128], identity=identF[:]),
                 reads=(("xin", s_), "identF"), writes=(PS(b),))
        S.op("dve", lambda: nc.vector.tensor_copy(out=Wsg[:].rearrange("p h t -> p (h t)"), in_=ps[b][:]),
             reads=(PS(b),), writes=("Wsg",))
        ps_free(b)

    def proj_fm(fc):
        b = ps_alloc()

        def f():
            i = None
            for kc in range(8):
                i = nc.tensor.matmul(ps[b][:], lhsT=Wi[:, kc, fc * 128:(fc + 1) * 128], rhs=hT[:, kc, :],
                                     start=(kc == 0), stop=(kc == 7))
            return i

        S.op("pe", f, reads=("Wi", "hT"), writes=(PS(b),))
        return b

    def proj_tm(tg, col0, ncols):
        b = ps_alloc()

        def f():
            i = None
            for kc in range(8):
                i = nc.tensor.matmul(ps[b][:, 0:ncols], lhsT=hT[:, kc, tg * 128:(tg + 1) * 128],
                                     rhs=Wi[:, kc, col0:col0 + ncols], start=(kc == 0), stop=(kc == 7))
            return i

        S.op("pe", f, reads=("Wi", "hT"), writes=(PS(b),))
        return b

    xin_slot = [0]

    def prep_hT(src_d, i, tgs=(0, 1, 2, 3)):
        t0 = i * T
        for tg in tgs:
            s_ = xin_slot[0]
            xin_slot[0] ^= 1
            r0 = t0 + tg * 128
            S.dma(("xin", s_), lambda: nc.sync.dma_start(out=xin[:, s_, :], in_=src_d[r0:r0 + 128, :]),
                  reads=rowres("xrow", r0, 128), writes=(("xin", s_),))
            S.op("act", lambda: nc.scalar.activation(out=xs[:], in_=xin[:, s_, :], func=AF.Square,
                                                     accum_out=sm[:, 8 + tg:9 + tg]),
                 reads=(("xin", s_),), writes=("xs", ("ssq", tg)))
            S.op("pool", lambda: nc.gpsimd.tensor_scalar(out=sm[:, 12 + tg:13 + tg], in0=sm[:, 8 + tg:9 + tg],
                                                        scalar1=1.0 / D, scalar2=EPS, op0=ALU.mult, op1=ALU.add),
                 reads=(("ssq", tg),), writes=(("ms", tg),))
            S.op("pool", lambda: nc.gpsimd.tensor_tensor(out=sm[:, 16 + tg:17 + tg], in0=sm[:, 12 + tg:13 + tg],
                                                        in1=sv[:, SV_M05:SV_M05 + 1], op=ALU.pow),
                 reads=(("ms", tg), "sv_c"), writes=(("rs", tg),))
            S.op("act", lambda: nc.scalar.activation(out=xs[:], in_=xin[:, s_, :], func=AF.Copy,
                                                     scale=sm[:, 16 + tg:17 + tg]),
                 reads=(("xin", s_), ("rs", tg)), writes=("xs",))
            b = ps_alloc()
            pbf = ps[b][:].bitcast(BF16)

            def f():
                i_ = None
                for kc in range(8):
                    i_ = nc.tensor.transpose(out=pbf[:, kc * 128:(kc + 1) * 128], in_=xs[:, kc * 128:(kc + 1) * 128],
                                             identity=IDB)
                return i_

            S.op("pe", f, reads=("xs", "cbf"), writes=(PS(b),))
            eng = "act"
            if eng == "act":
                S.op("act", lambda: nc.scalar.copy(out=hT[:, :, tg * 128:(tg + 1) * 128],
                                                   in_=pbf.rearrange("p (k t) -> p k t", k=8)),
                     reads=(PS(b),), writes=("hT",))
            else:
                S.op("dve", lambda: nc.vector.tensor_copy(out=hT[:, :, tg * 128:(tg + 1) * 128],
                                                          in_=pbf.rearrange("p (k t) -> p k t", k=8)),
                     reads=(PS(b),), writes=("hT",))
            ps_free(b)

    def tanh_gate(bank, dst, slot):
        S.op("act", lambda: nc.scalar.activation(out=tA[:, slot, :], in_=ps[bank][:], func=AF.Tanh, scale=0.5),
             reads=(PS(bank),), writes=(("tA", slot),))
        S.op("dve", lambda: nc.vector.scalar_tensor_tensor(out=dst, in0=tA[:, slot, :], scalar=1.0, in1=ps[bank][:],
                                                          op0=ALU.add, op1=ALU.mult),
             reads=(("tA", slot), PS(bank)), writes=())

    def hgrn_dir_elem(l, h, d_, Pz, Pq, need_q, sl):
        a_ap = lbc[:, l, d_, 0, h:h + 1]
        c_ap = lbc[:, l, d_, 1, h:h + 1]
        QK = QK2[:, sl]
        CPf = CPf2[:, sl]
        DB = DB2[:, sl]
        if Pz is not None:
            S.op("act", lambda: nc.scalar.activation(out=tA[:, d_, :], in_=ps[Pz][:], func=AF.Tanh, scale=0.5),
                 reads=(PS(Pz),), writes=(("tA", d_),))
            ps_free(Pz)
        if d_ == 0:
            S.op("pool", lambda: nc.gpsimd.tensor_scalar(out=Ff[:], in0=tA[:, 0, :], scalar1=a_ap, scalar2=c_ap,
                                                        op0=ALU.mult, op1=ALU.add),
                 reads=(("tA", 0), "lbc"), writes=("Ff",))
            S.op("dve", lambda: nc.vector.tensor_tensor_scan(out=CPf, data0=csm[:], data1=Ff[:], initial=1.0,
                                                            op0=ALU.max, op1=ALU.mult),
                 reads=("Ff", "csm"), writes=(("CPf", sl),))
            if need_q:
                S.op("dve", lambda: nc.vector.tensor_tensor(out=QK[:, 0, :], in0=ps[Pq][:], in1=CPf, op=ALU.mult),
                     reads=(PS(Pq), ("CPf", sl)), writes=(("QK", sl, 0),))
            S.op("dve", lambda: nc.vector.reciprocal(out=RC[:, 0, :], in_=CPf), reads=(("CPf", sl),),
                 writes=(("RC", 0),))
            S.op("pool", lambda: nc.gpsimd.tensor_scalar(out=tA[:, 0, :], in0=Ff[:], scalar1=-1.0, scalar2=1.0,
                                                        op0=ALU.add, op1=ALU.mult),
                 reads=("Ff",), writes=(("tA", 0),))
            S.op("pool", lambda: nc.gpsimd.tensor_tensor(out=QK[:, 1, :], in0=tA[:, 0, :], in1=RC[:, 0, :], op=ALU.mult),
                 reads=(("tA", 0), ("RC", 0)), writes=(("QK", sl, 1),))
        else:
            S.op("pool", lambda: nc.gpsimd.tensor_scalar(out=Fb[:, 1:T + 1], in0=tA[:, 1, :], scalar1=a_ap,
                                                        scalar2=c_ap, op0=ALU.mult, op1=ALU.add),
                 reads=(("tA", 1), "lbc"), writes=("Fb",))
            S.op("dve", lambda: nc.vector.tensor_tensor_scan(out=Eb[:], data0=Fb[:, 0:T], data1=csm[:], initial=1.0,
                                                            op0=ALU.mult, op1=ALU.max),
                 reads=("Fb", "csm"), writes=("Eb",))
            if need_q:
                S.op("dve", lambda: nc.vector.reciprocal(out=RC[:, 0, :], in_=Eb[:]), reads=("Eb",),
                     writes=(("RC", 0),))
                S.op("dve", lambda: nc.vector.tensor_tensor(out=QK[:, 2, :], in0=ps[Pq][:], in1=RC[:, 0, :],
                                                           op=ALU.mult),
                     reads=(PS(Pq), ("RC", 0)), writes=(("QK", sl, 2),))
            S.op("pool", lambda: nc.gpsimd.tensor_scalar(out=tA[:, 1, :], in0=Fb[:, 1:T + 1], scalar1=-1.0, scalar2=1.0,
                                                        op0=ALU.add, op1=ALU.mult),
                 reads=("Fb",), writes=(("tA", 1),))
            S.op("pool", lambda: nc.gpsimd.tensor_tensor(out=QK[:, 3, :], in0=tA[:, 1, :], in1=Eb[:], op=ALU.mult),
                 reads=(("tA", 1), "Eb"), writes=(("QK", sl, 3),))
            S.op("dve", lambda: nc.vector.tensor_tensor(
                out=DB, in0=Eb[:].rearrange("p (c t) -> p c t", t=64)[:, :, 63],
                in1=Fb[:, 1:T + 1].rearrange("p (c t) -> p c t", t=64)[:, :, 63], op=ALU.mult),
                reads=("Eb", "Fb"), writes=(("DB", sl),))

    def hgrn_elem2(l, h, Pq, sl):
        QK = QK2[:, sl]
        CPf = CPf2[:, sl]
        DB = DB2[:, sl]
        for d_, dst in ((0, Ff[:]), (1, Fb[:, 1:T + 1])):
            S.op("pool", lambda: nc.gpsimd.tensor_scalar(out=dst, in0=tA[:, d_, :], scalar1=lbc[:, l, d_, 0, h:h + 1],
                                                        scalar2=lbc[:, l, d_, 1, h:h + 1], op0=ALU.mult, op1=ALU.add),
                 reads=(("tA", d_), "lbc"), writes=("Ff" if d_ == 0 else "Fb",))
        S.op("dve", lambda: nc.vector.tensor_tensor_scan(out=CPf, data0=csm[:], data1=Ff[:], initial=1.0,
                                                        op0=ALU.max, op1=ALU.mult),
             reads=("Ff", "csm"), writes=(("CPf", sl),))
        S.op("dve", lambda: nc.vector.tensor_tensor_scan(out=Eb[:], data0=Fb[:, 0:T], data1=csm[:], initial=1.0,
                                                        op0=ALU.mult, op1=ALU.max),
             reads=("Fb", "csm"), writes=("Eb",))
        S.op("dve", lambda: nc.vector.tensor_tensor(out=QK[:, 0, :], in0=ps[Pq][:], in1=CPf, op=ALU.mult),
             reads=(PS(Pq), ("CPf", sl)), writes=(("QK", sl, 0),))
        S.op("dve", lambda: nc.vector.reciprocal(out=RC[:, 0, :], in_=CPf), reads=(("CPf", sl),), writes=(("RC", 0),))
        S.op("pool", lambda: nc.gpsimd.tensor_scalar(out=tA[:, 1, :], in0=Fb[:, 1:T + 1], scalar1=-1.0, scalar2=1.0,
                                                    op0=ALU.add, op1=ALU.mult),
             reads=("Fb",), writes=(("tA", 1),))
        S.op("pool", lambda: nc.gpsimd.tensor_tensor(out=QK[:, 3, :], in0=tA[:, 1, :], in1=Eb[:], op=ALU.mult),
             reads=(("tA", 1), "Eb"), writes=(("QK", sl, 3),))
        S.op("dve", lambda: nc.vector.tensor_tensor(
            out=DB, in0=Eb[:].rearrange("p (c t) -> p c t", t=64)[:, :, 63],
            in1=Fb[:, 1:T + 1].rearrange("p (c t) -> p c t", t=64)[:, :, 63], op=ALU.mult),
            reads=("Eb", "Fb"), writes=(("DB", sl),))
        S.op("pool", lambda: nc.gpsimd.tensor_scalar(out=tA[:, 0, :], in0=Ff[:], scalar1=-1.0, scalar2=1.0,
                                                    op0=ALU.add, op1=ALU.mult),
             reads=("Ff",), writes=(("tA", 0),))
        S.op("pool", lambda: nc.gpsimd.tensor_tensor(out=QK[:, 1, :], in0=tA[:, 0, :], in1=RC[:, 0, :], op=ALU.mult),
             reads=(("tA", 0), ("RC", 0)), writes=(("QK", sl, 1),))
        S.op("dve", lambda: nc.vector.reciprocal(out=Eb[:], in_=Eb[:]), reads=("Eb",), writes=("Eb",))
        S.op("dve", lambda: nc.vector.tensor_tensor(out=QK[:, 2, :], in0=ps[Pq][:], in1=Eb[:], op=ALU.mult),
             reads=(PS(Pq), "Eb"), writes=(("QK", sl, 2),))

    def hgrn_state_mm(h, d_, sl):
        b = ps_alloc()
        pbf = ps[b][:].bitcast(BF16)

        def f():
            i_ = None
            for pr in range(4):
                i_ = nc.tensor.transpose(out=pbf[:, pr * 128:(pr + 1) * 128],
                                         in_=QK2[:, sl, 1 + 2 * d_, pr * 128:(pr + 1) * 128], identity=IDB)
            return i_

        S.op("pe", f, reads=(("QK", sl, 1 + 2 * d_), "cbf"), writes=(PS(b),))
        S.op("act", lambda: nc.scalar.copy(out=kT[:, d_, :], in_=pbf[:, 0:T]), reads=(PS(b),), writes=(("kT", d_),))
        ps_free(b)
        banks = []
        for half in range(2):
            bb = ps_alloc()

            def g():
                i_ = None
                for cc in range(4):
                    c = cc * 2 + half
                    pr, hf = c // 2, c % 2
                    i_ = nc.tensor.matmul(ps[bb][:, cc * 128:(cc + 1) * 128],
                                          lhsT=kT[hf * 64:(hf + 1) * 64, d_, pr * 128:(pr + 1) * 128],
                                          rhs=vtok[hf * 64:(hf + 1) * 64, pr, h * 128:(h + 1) * 128],
                                          start=True, stop=True)
                return i_

            S.op("pe", g, reads=(("kT", d_), "vtok"), writes=(PS(bb),))
            banks.append(bb)
        return banks

    def Pchunk(banks, c):
        return ps[banks[c % 2]][:, (c // 2) * 128:(c // 2 + 1) * 128], PS(banks[c % 2])

    def hgrn_bwd_recur(h, banks, snapshots, sl):
        DB = DB2[:, sl]
        par = 0
        for c in range(7, -1, -1):
            pc, pres = Pchunk(banks, c)
            src, dst = par, 1 - par
            if snapshots:
                S.op("pool", lambda: nc.gpsimd.tensor_scalar(out=Sbf[:, 1, c, :], in0=Sb2[:, h, src, :],
                                                            scalar1=DB[:, c:c + 1], scalar2=1.0, op0=ALU.mult,
                                                            op1=ALU.mult),
                     reads=(("Sb", h, src), ("DB", sl)), writes=(("Sbf", 1, c),))
            S.op("dve", lambda: nc.vector.scalar_tensor_tensor(out=Sb2[:, h, dst, :], in0=Sb2[:, h, src, :],
                                                              scalar=DB[:, c:c + 1], in1=pc, op0=ALU.mult,
                                                              op1=ALU.subtract),
                 reads=(("Sb", h, src), ("DB", sl), pres), writes=(("Sb", h, dst),))
            par = dst
        assert par == 0
        for bb in banks:
            ps_free(bb)

    def hgrn_fwd_recur(h, banks, sl):
        CPf = CPf2[:, sl]
        S.op("pool", lambda: nc.gpsimd.tensor_scalar(out=Sbf[:, 0, 0, :], in0=Gf2[:, h, 0, :], scalar1=dcar[:, h:h + 1],
                                                    scalar2=1.0, op0=ALU.mult, op1=ALU.mult),
             reads=(("Gf", h, 0), ("dcar", h)), writes=(("Sbf", 0, 0),))
        par = 0
        for c in range(8):
            pc, pres = Pchunk(banks, c)
            src, dst = par, 1 - par
            dprev = dcar[:, h:h + 1] if c == 0 else CPf[:, c * 64 - 1:c * 64]
            S.op("dve", lambda: nc.vector.scalar_tensor_tensor(out=Gf2[:, h, dst, :], in0=Gf2[:, h, src, :], scalar=dprev,
                                                              in1=pc, op0=ALU.mult, op1=ALU.subtract),
                 reads=(("Gf", h, src), ("dcar", h), ("CPf", sl), pres), writes=(("Gf", h, dst),))
            par = dst
            if c < 7:
                S.op("pool", lambda: nc.gpsimd.tensor_scalar(out=Sbf[:, 0, c + 1, :], in0=Gf2[:, h, dst, :],
                                                            scalar1=CPf[:, c * 64 + 63:c * 64 + 64], scalar2=1.0,
                                                            op0=ALU.mult, op1=ALU.mult),
                     reads=(("Gf", h, dst), ("CPf", sl)), writes=(("Sbf", 0, c + 1),))
        assert par == 0
        S.op("dve", lambda: nc.vector.tensor_copy(out=dcar[:, h:h + 1], in_=CPf[:, T - 1:T]),
             reads=(("CPf", sl),), writes=(("dcar", h),))
        for bb in banks:
            ps_free(bb)

    def hgrn_recur2(h, bf_, bb_, sl):
        CPf = CPf2[:, sl]
        DB = DB2[:, sl]
        S.op("pool", lambda: nc.gpsimd.tensor_scalar(out=Sbf[:, 0, 0, :], in0=Gf2[:, h, 0, :], scalar1=dcar[:, h:h + 1],
                                                    scalar2=1.0, op0=ALU.mult, op1=ALU.mult),
             reads=(("Gf", h, 0), ("dcar", h)), writes=(("Sbf", 0, 0),))
        pf = pb = 0
        for k in range(8):
            c = k
            pc, pres = Pchunk(bf_, c)
            src, dst = pf, 1 - pf
            dprev = dcar[:, h:h + 1] if c == 0 else CPf[:, c * 64 - 1:c * 64]
            S.op("dve", lambda: nc.vector.scalar_tensor_tensor(out=Gf2[:, h, dst, :], in0=Gf2[:, h, src, :], scalar=dprev,
                                                              in1=pc, op0=ALU.mult, op1=ALU.subtract),
                 reads=(("Gf", h, src), ("dcar", h), ("CPf", sl), pres), writes=(("Gf", h, dst),))
            pf = dst
            if c < 7:
                S.op("pool", lambda: nc.gpsimd.tensor_scalar(out=Sbf[:, 0, c + 1, :], in0=Gf2[:, h, dst, :],
                                                            scalar1=CPf[:, c * 64 + 63:c * 64 + 64], scalar2=1.0,
                                                            op0=ALU.mult, op1=ALU.mult),
                     reads=(("Gf", h, dst), ("CPf", sl)), writes=(("Sbf", 0, c + 1),))
            c = 7 - k
            pc, pres = Pchunk(bb_, c)
            src, dst = pb, 1 - pb
            S.op("pool", lambda: nc.gpsimd.tensor_scalar(out=Sbf[:, 1, c, :], in0=Sb2[:, h, src, :],
                                                        scalar1=DB[:, c:c + 1], scalar2=1.0, op0=ALU.mult, op1=ALU.mult),
                 reads=(("Sb", h, src), ("DB", sl)), writes=(("Sbf", 1, c),))
            S.op("dve", lambda: nc.vector.scalar_tensor_tensor(out=Sb2[:, h, dst, :], in0=Sb2[:, h, src, :],
                                                              scalar=DB[:, c:c + 1], in1=pc, op0=ALU.mult,
                                                              op1=ALU.subtract),
                 reads=(("Sb", h, src), ("DB", sl), pres), writes=(("Sb", h, dst),))
            pb = dst
        assert pf == 0 and pb == 0
        S.op("dve", lambda: nc.vector.tensor_copy(out=dcar[:, h:h + 1], in_=CPf[:, T - 1:T]),
             reads=(("CPf", sl),), writes=(("dcar", h),))
        for bb in bf_ + bb_:
            ps_free(bb)

    def conv_mm(W):
        cb = []
        for j in range(2):
            b = ps_alloc()

            def f():
                i_ = None
                for tp in range(31):
                    i_ = nc.tensor.matmul(ps[b][:, 0:W], lhsT=Wcd[:, j, tp, :], rhs=ybuf[:, j, tp:tp + W],
                                          start=(tp == 0), stop=(tp == 30))
                return i_

            S.op("pe", f, reads=("Wcd", "ybuf"), writes=(PS(b),))
            cb.append(b)
        return cb

    def conv_tail(l, W, cb=None, hooks=None):
        if cb is None:
            cb = conv_mm(W)
        for j in range(2):
            b = cb[j]
            cbias = sv[:, SV_CB + l * 2 + j:SV_CB + l * 2 + j + 1]
            S.op("act", lambda: nc.scalar.activation(out=xres[:, j, 0:W], in_=ps[b][:, 0:W], func=AF.Identity,
                                                     bias=cbias, scale=1.0),
                 reads=(PS(b), "sv"), writes=(("xres", j),))
            S.op("act", lambda: nc.scalar.activation(out=cnb[:, j, 0:W], in_=ps[b][:, 0:W], func=AF.Identity,
                                                     bias=cbias, scale=1.0),
                 reads=(PS(b), "sv"), writes=(("osq", j),))
            S.op("act", lambda: nc.scalar.activation(out=csq[:, j, 0:W], in_=ps[b][:, 0:W], func=AF.Square,
                                                     bias=cbias, scale=1.0),
                 reads=(PS(b), "sv"), writes=(("osq", 2 + j),))
            ps_free(b)
        if hooks:
            hooks["e"]()
        bm = ps_alloc()
        bq = ps_alloc()

        def fm():
            i_ = None
            for j in range(2):
                i_ = nc.tensor.matmul(ps[bm][:, 0:W], lhsT=ONES256, rhs=cnb[:, j, 0:W], start=(j == 0), stop=(j == 1))
            return i_

        def fq():
            i_ = None
            for j in range(2):
                i_ = nc.tensor.matmul(ps[bq][:, 0:W], lhsT=ONES256, rhs=csq[:, j, 0:W], start=(j == 0), stop=(j == 1))
            return i_

        S.op("pe", fm, reads=("cbf", ("osq", 0), ("osq", 1)), writes=(PS(bm),))
        S.op("pe", fq, reads=("cbf", ("osq", 2), ("osq", 3)), writes=(PS(bq),))
        if hooks:
            mean, mkey, var, vkey = big[:, 0, 0:W], ("big", 0), big[:, 1, 0:W], ("big", 1)
        else:
            mean, mkey, var, vkey = tA[:, 0, 0:W], ("tA", 0), tA[:, 1, 0:W], ("tA", 1)
        S.op("act", lambda: nc.scalar.copy(out=mean, in_=ps[bm][:, 0:W]), reads=(PS(bm),), writes=(mkey,))
        ps_free(bm)
        if hooks:
            hooks["a"]()
        S.op("dve", lambda: nc.vector.scalar_tensor_tensor(out=var, in0=mean, scalar=-1.0, in1=mean,
                                                          op0=ALU.mult, op1=ALU.mult),
             reads=(mkey,), writes=(vkey,))
        S.op("dve", lambda: nc.vector.tensor_tensor(out=var, in0=ps[bq][:, 0:W], in1=var, op=ALU.add),
             reads=(PS(bq), vkey), writes=(vkey,))
        ps_free(bq)
        S.op("act", lambda: nc.scalar.activation(out=rstd[:, 0:W], in_=var, func=AF.Ln,
                                                 bias=sv[:, SV_EPS:SV_EPS + 1], scale=1.0),
             reads=(vkey, "sv_c"), writes=("rstd",))
        S.op("act", lambda: nc.scalar.activation(out=rstd[:, 0:W], in_=rstd[:, 0:W], func=AF.Exp, scale=-0.5),
             reads=("rstd",), writes=("rstd",))
        if hooks:
            hooks["b"]()
        exb = [(tA[:, 2, :], ("tA", 2)), (rstd[:], "rstd")]
        for j in range(2):
            S.op("dve", lambda: nc.vector.tensor_tensor(out=xres[:, j, 0:W], in0=xres[:, j, 0:W], in1=mean, op=ALU.subtract),
                 reads=(("xres", j), mkey), writes=(("xres", j),))
            S.op("dve", lambda: nc.vector.tensor_tensor(out=xres[:, j, 0:W], in0=xres[:, j, 0:W], in1=rstd[:, 0:W],
                                                       op=ALU.mult),
                 reads=(("xres", j), "rstd"), writes=(("xres", j),))
        for j in range(2):
            g_ap = sv[:, SV_CLG + l * 2 + j:SV_CLG + l * 2 + j + 1]
            b_ap = sv[:, SV_CLB + l * 2 + j:SV_CLB + l * 2 + j + 1]
            ng_ap = sv[:, SV_NCLG + l * 2 + j:SV_NCLG + l * 2 + j + 1]
            nb_ap = sv[:, SV_NCLB + l * 2 + j:SV_NCLB + l * 2 + j + 1]
            ex, ekey = exb[j]
            S.op("act", lambda: nc.scalar.activation(out=ex[:, 0:W], in_=xres[:, j, 0:W], func=AF.Exp,
                                                     bias=nb_ap, scale=ng_ap),
                 reads=(("xres", j), "sv_d"), writes=(ekey,))
            S.op("pool", lambda: nc.gpsimd.tensor_scalar(out=xres[:, j, 0:W], in0=xres[:, j, 0:W], scalar1=g_ap,
                                                        scalar2=b_ap, op0=ALU.mult, op1=ALU.add),
                 reads=(("xres", j), "sv", ekey), writes=(("xres", j),))
            S.op("act", lambda: nc.scalar.activation(out=ex[:, 0:W], in_=ex[:, 0:W], func=AF.Ln,
                                                     bias=sv[:, SV_ONE:SV_ONE + 1], scale=1.0),
                 reads=(ekey, "sv_c"), writes=(ekey,))
            S.op("act", lambda: nc.scalar.activation(out=ex[:, 0:W], in_=ex[:, 0:W], func=AF.Exp, scale=-1.0),
                 reads=(ekey,), writes=(ekey,))
        for j in range(2):
            ex, ekey = exb[j]
            S.op("dve", lambda: nc.vector.tensor_tensor(out=cnb[:, j, 0:W], in0=xres[:, j, 0:W], in1=ex[:, 0:W],
                                                       op=ALU.mult),
                 reads=(("xres", j), ekey), writes=(("osq", j),))
        for jo in range(2):
            b = ps_alloc()

            def f():
                i_ = None
                for ji in range(2):
                    i_ = nc.tensor.matmul(ps[b][:, 0:W], lhsT=Wpw[:, ji, jo * 128:(jo + 1) * 128], rhs=cnb[:, ji, 0:W],
                                          start=(ji == 0), stop=(ji == 1))
                return i_

            S.op("pe", f, reads=("Wpw", ("osq", 0), ("osq", 1)), writes=(PS(b),))
            S.op("dve", lambda: nc.vector.tensor_tensor(out=yT[:, jo, 0:W], in0=ps[b][:, 0:W], in1=ag[:, jo, 0:W],
                                                       op=ALU.mult),
                 reads=(PS(b), "ag"), writes=(("yT", jo),))
            ps_free(b)

    xres_slot = [0]

    def out_proj(l, src_d, dst_d, tok0, ncols, col0):
        skip = max(0, -tok0)
        nv = ncols - skip
        s_ = xres_slot[0]
        xres_slot[0] ^= 1
        r0 = tok0 + skip
        S.dma(("xres", s_), lambda: nc.sync.dma_start(out=xres[skip:ncols, s_, :], in_=src_d[r0:r0 + nv, :]),
              reads=rowres("xrow", r0, nv), writes=(("xres", s_),))
        for dh in range(2):
            b = ps_alloc()

            def f():
                i_ = None
                for mc in range(8):
                    i_ = nc.tensor.matmul(ps[b][0:ncols, :], lhsT=yT[:, mc, col0:col0 + ncols],
                                          rhs=Wo[:, mc, dh * 512:(dh + 1) * 512], start=(mc == 0), stop=(mc == 7))
                return i_

            S.op("pe", f, reads=("Wo",) + tuple(("yT", m) for m in range(8)), writes=(PS(b),))
            S.op("dve", lambda: nc.vector.tensor_tensor(out=xres[0:ncols, s_, dh * 512:(dh + 1) * 512],
                                                       in0=ps[b][0:ncols, :], in1=xres[0:ncols, s_, dh * 512:(dh + 1) * 512],
                                                       op=ALU.add),
                 reads=(PS(b), ("xres", s_)), writes=(("xres", s_),))
            ps_free(b)
        if l == 1:
            S.op("act", lambda: nc.scalar.activation(out=xs[0:ncols, :], in_=xres[0:ncols, s_, :], func=AF.Square,
                                                     accum_out=sm[0:ncols, 20:21]),
                 reads=(("xres", s_),), writes=("xs", "fssq"))
            S.op("pool", lambda: nc.gpsimd.tensor_scalar(out=sm[0:ncols, 21:22], in0=sm[0:ncols, 20:21],
                                                        scalar1=1.0 / D, scalar2=EPS, op0=ALU.mult, op1=ALU.add),
                 reads=("fssq",), writes=("fms",))
            S.op("pool", lambda: nc.gpsimd.tensor_tensor(out=sm[0:ncols, 22:23], in0=sm[0:ncols, 21:22],
                                                        in1=sv[0:ncols, SV_M05:SV_M05 + 1], op=ALU.pow),
                 reads=("fms", "sv_c"), writes=("frs",))
            S.op("dve", lambda: nc.vector.scalar_tensor_tensor(out=xres[0:ncols, s_, :], in0=xres[0:ncols, s_, :],
                                                              scalar=sm[0:ncols, 22:23], in1=fng[0:ncols, :],
                                                              op0=ALU.mult, op1=ALU.mult),
                 reads=(("xres", s_), "frs", "fng"), writes=(("xres", s_),))
        key = ("out", s_)
        if key not in out_keys:
            out_keys.append(key)
        wr = rowres("xrow1", r0, nv) if l == 0 else ()
        S.dma(key, lambda: nc.sync.dma_start(out=dst_d[r0:r0 + nv, :], in_=xres[skip:ncols, s_, :]),
              reads=(("xres", s_),), writes=wr)

    def p1_elem(l, h, th, thkey):
        qdst = QK2[:, h % 2, 1 + 2 * (h // 2), :]
        qkey = ("QK", h % 2, 1 + 2 * (h // 2))
        S.op("pool", lambda: nc.gpsimd.tensor_scalar(out=Fb[:, 1:T + 1], in0=th, scalar1=lbc[:, l, 1, 0, h:h + 1],
                                                    scalar2=lbc[:, l, 1, 1, h:h + 1], op0=ALU.mult, op1=ALU.add),
             reads=(thkey, "lbc"), writes=("Fb",))
        S.op("dve", lambda: nc.vector.tensor_tensor_scan(out=Eb[:], data0=Fb[:, 0:T], data1=csm[:], initial=1.0,
                                                        op0=ALU.mult, op1=ALU.max),
             reads=("Fb", "csm"), writes=("Eb",))
        S.op("pool", lambda: nc.gpsimd.tensor_scalar(out=th, in0=Fb[:, 1:T + 1], scalar1=-1.0, scalar2=1.0,
                                                    op0=ALU.add, op1=ALU.mult),
             reads=("Fb",), writes=(thkey,))
        S.op("pool", lambda: nc.gpsimd.tensor_tensor(out=qdst, in0=th, in1=Eb[:], op=ALU.mult),
             reads=(thkey, "Eb"), writes=(qkey,))
        S.op("dve", lambda: nc.vector.tensor_tensor(
            out=DB4[:, h, :], in0=Eb[:].rearrange("p (c t) -> p c t", t=64)[:, :, 63],
            in1=Fb[:, 1:T + 1].rearrange("p (c t) -> p c t", t=64)[:, :, 63], op=ALU.mult),
            reads=("Eb", "Fb"), writes=(("DB4", h),))

    def p1_recur_pair(heads, banks):
        par = {h: 0 for h in heads}
        for c in range(7, -1, -1):
            for h in heads:
                pc, pres = Pchunk(banks[h], c)
                src, dst = par[h], 1 - par[h]
                S.op("dve", lambda: nc.vector.scalar_tensor_tensor(out=Sb2[:, h, dst, :], in0=Sb2[:, h, src, :],
                                                                  scalar=DB4[:, h, c:c + 1], in1=pc, op0=ALU.mult,
                                                                  op1=ALU.subtract),
                     reads=(("Sb", h, src), ("DB4", h), pres), writes=(("Sb", h, dst),))
                par[h] = dst
        for h in heads:
            assert par[h] == 0
            for bb in banks[h]:
                ps_free(bb)

    def pass1_tile(l, src_d, i):
        S.dma("sbst", lambda: nc.sync.dma_start(out=sbnd_d[i].rearrange("p (h v) -> p h v", h=4), in_=Sb2[:, :, 0, :]),
              reads=tuple(("Sb", h, 0) for h in range(4)), writes=(("sbnd", i),))
        if i == NT - 1:
            prep_hT(src_d, i)
        for tg in range(4):
            b = proj_tm(tg, C_I, 512)
            S.op("act", lambda: nc.scalar.copy(out=vtok[:, tg, :], in_=ps[b][:]), reads=(PS(b),), writes=("vtok",))
            ps_free(b)
        thb = [(tA[:, 0, :], ("tA", 0)), (tA[:, 1, :], ("tA", 1)), (tA[:, 2, :], ("tA", 2)), (rstd[:], "rstd")]
        for h in range(4):
            pz = proj_fm(C_FB // 128 + h)
            S.op("act", lambda: nc.scalar.activation(out=thb[h][0], in_=ps[pz][:], func=AF.Tanh, scale=0.5),
                 reads=(PS(pz),), writes=(thb[h][1],))
            ps_free(pz)
        nxt = (lambda tg: prep_hT(src_d, i - 1, (tg,))) if i > 0 else (lambda tg: None)
        banks = {}
        p1_elem(l, 0, *thb[0])
        p1_elem(l, 1, *thb[1])
        banks[0] = hgrn_state_mm(0, 0, 0)
        banks[1] = hgrn_state_mm(1, 0, 1)
        nxt(0)
        p1_elem(l, 2, *thb[2])
        p1_elem(l, 3, *thb[3])
        p1_recur_pair((0, 1), banks)
        nxt(1)
        banks[2] = hgrn_state_mm(2, 1, 0)
        banks[3] = hgrn_state_mm(3, 1, 1)
        nxt(2)
        p1_recur_pair((2, 3), banks)
        nxt(3)

    started = set()
    pending = []

    def e_proj(l, h):
        Pq = proj_fm(C_Q // 128 + h)
        for d_, c0 in ((0, C_FF), (1, C_FB)):
            Pz = proj_fm(c0 // 128 + h)
            S.op("act", lambda: nc.scalar.activation(out=tA[:, d_, :], in_=ps[Pz][:], func=AF.Tanh, scale=0.5),
                 reads=(PS(Pz),), writes=(("tA", d_),))
            ps_free(Pz)
        return Pq

    def e_elem(l, h, Pq):
        hgrn_elem2(l, h, Pq, h % 2)
        ps_free(Pq)

    def tile_start(l, i):
        started.add(i)
        S.dma("sbld", lambda: nc.sync.dma_start(out=Sb2[:, :, 0, :], in_=sbnd_d[i].rearrange("p (h v) -> p h v", h=4)),
              reads=(("sbnd", i),), writes=tuple(("Sb", h, 0) for h in range(4)))
        for tg in range(4):
            b = proj_tm(tg, C_I, 512)
            S.op("act", lambda: nc.scalar.copy(out=vtok[:, tg, :], in_=ps[b][:]), reads=(PS(b),), writes=("vtok",))
            ps_free(b)
        pq = e_proj(l, 0)
        e_elem(l, 0, pq)

    def pass2_tile(l, src_d, dst_d, i):
        t0 = i * T
        if i not in started:
            tile_start(l, i)
        def conv_in(j):
            Pval = proj_fm(C_AVAL // 128 + j)
            Pglu = proj_fm(C_AGLU // 128 + j)
            S.op("act", lambda: nc.scalar.activation(out=tA[:, 2, :], in_=ps[Pglu][:], func=AF.Tanh, scale=0.5),
                 reads=(PS(Pglu),), writes=(("tA", 2),))
            ps_free(Pglu)
            S.op("dve", lambda: nc.vector.scalar_tensor_tensor(out=ybuf[:, j, 31:31 + T], in0=tA[:, 2, :], scalar=1.0,
                                                              in1=ps[Pval][:], op0=ALU.add, op1=ALU.mult),
                 reads=(("tA", 2), PS(Pval)), writes=("ybuf",))
            ps_free(Pval)
            Pg = proj_fm(C_AGATE // 128 + j)
            S.op("act", lambda: nc.scalar.activation(out=tA[:, 2, :], in_=ps[Pg][:], func=AF.Tanh, scale=0.5),
                 reads=(PS(Pg),), writes=(("tA", 2),))
            S.op("dve", lambda: nc.vector.scalar_tensor_tensor(out=ag[:, j, 16:16 + T], in0=tA[:, 2, :], scalar=1.0,
                                                              in1=ps[Pg][:], op0=ALU.add, op1=ALU.mult),
                 reads=(("tA", 2), PS(Pg)), writes=("ag",))
            ps_free(Pg)
        def sgu_in(j):
            Pu = proj_fm(C_BU // 128 + j)
            S.op("act", lambda: nc.scalar.activation(out=ub[:, j, :], in_=ps[Pu][:], func=AF.Gelu),
                 reads=(PS(Pu),), writes=(("ub", j),))
            ps_free(Pu)
            Pb = proj_fm(C_BGATE // 128 + j)
            S.op("act", lambda: nc.scalar.activation(out=tA[:, 2, :], in_=ps[Pb][:], func=AF.Tanh, scale=0.5),
                 reads=(PS(Pb),), writes=(("tA", 2),))
            S.op("dve", lambda: nc.vector.scalar_tensor_tensor(out=bg[:, j, :], in0=tA[:, 2, :], scalar=1.0,
                                                              in1=ps[Pb][:], op0=ALU.add, op1=ALU.mult),
                 reads=(("tA", 2), PS(Pb)), writes=(("bg", j),))
            ps_free(Pb)
            S.op("pool", lambda: nc.gpsimd.tensor_tensor(out=ub[:, j, :], in0=ub[:, j, :], in1=bg[:, j, :], op=ALU.mult),
                 reads=(("ub", j), ("bg", j)), writes=(("ub", j),))
        def sgu_bv(tg):
            Pv = proj_tm(tg, C_BV, 256)
            gsl = 0
            S.op("act", lambda: nc.scalar.activation(out=gvb[:, gsl, :], in_=ps[Pv][:, 0:256], func=AF.Gelu),
                 reads=(PS(Pv),), writes=(("gvb", gsl),))
            ps_free(Pv)
            S.op("dve", lambda: nc.vector.bn_stats(out=st6[:, tg, :], in_=gvb[:, gsl, :]),
                 reads=(("gvb", gsl),), writes=(("st6", tg),))
            S.op("dve", lambda: nc.vector.bn_aggr(out=mv[:, tg, :], in_=st6[:, tg, :]),
                 reads=(("st6", tg),), writes=(("mv", tg),))
            S.op("pool", lambda: nc.gpsimd.tensor_scalar(out=sm[:, 24 + tg:25 + tg], in0=mv[:, tg, 1:2], scalar1=1.0,
                                                        scalar2=EPS, op0=ALU.mult, op1=ALU.add),
                 reads=(("mv", tg),), writes=(("sve", tg),))
            S.op("pool", lambda: nc.gpsimd.tensor_tensor(out=sm[:, 28 + tg:29 + tg], in0=sm[:, 24 + tg:25 + tg],
                                                        in1=sv[:, SV_M05:SV_M05 + 1], op=ALU.pow),
                 reads=(("sve", tg), "sv_c"), writes=(("srs", tg),))
            S.op("dve", lambda: nc.vector.tensor_scalar(out=gvb[:, gsl, :], in0=gvb[:, gsl, :], scalar1=mv[:, tg, 0:1],
                                                       scalar2=sm[:, 28 + tg:29 + tg], op0=ALU.subtract, op1=ALU.mult),
                 reads=(("gvb", gsl), ("mv", tg), ("srs", tg)), writes=(("gvb", gsl),))
            S.op("pool", lambda: nc.gpsimd.tensor_tensor(out=gvb[:, gsl, :], in0=gvb[:, gsl, :], in1=sgg[:, l, :],
                                                        op=ALU.mult),
                 reads=(("gvb", gsl), "sgg"), writes=(("gvb", gsl),))
            S.op("pool", lambda: nc.gpsimd.tensor_tensor(out=vsn[:, tg, :], in0=gvb[:, gsl, :], in1=sgb[:, l, :],
                                                        op=ALU.add),
                 reads=(("gvb", gsl), "sgb"), writes=(("vsn", tg),))
        def sgu_out(g2):
            b = ps_alloc()

            def f():
                i_ = None
                for tg in range(4):
                    for hh in range(2):
                        h = g2 * 2 + hh
                        i_ = nc.tensor.matmul(ps[b][hh * 64:(hh + 1) * 64, tg * 128:(tg + 1) * 128],
                                              lhsT=vsn[:, tg, h * 64:(h + 1) * 64], rhs=Wsg[:, h, :],
                                              start=True, stop=True)
                return i_

            S.op("pe", f, reads=tuple(("vsn", tg) for tg in range(4)) + ("Wsg",), writes=(PS(b),))
            S.op("dve", lambda: nc.vector.tensor_tensor(
                out=tA[:, 2, :].rearrange("p (g t) -> p g t", g=4), in0=ps[b][:].rearrange("p (g t) -> p g t", g=4),
                in1=sbias[:, l, g2, :].unsqueeze(1).to_broadcast([128, 4, 128]), op=ALU.add),
                reads=(PS(b), "sbias"), writes=(("tA", 2),))
            ps_free(b)
            S.op("dve", lambda: nc.vector.scalar_tensor_tensor(out=yT[:, 2 + g2, 16:16 + T], in0=tA[:, 2, :], scalar=0.5,
                                                              in1=ub[:, g2, :], op0=ALU.mult, op1=ALU.mult),
                 reads=(("tA", 2), ("ub", g2)), writes=(("yT", 2 + g2),))
        for h in range(4):
            Pgt = proj_fm(C_GATE // 128 + h)
            S.op("act", lambda: nc.scalar.activation(out=tA[:, 2, :], in_=ps[Pgt][:], func=AF.Tanh, scale=0.5),
                 reads=(PS(Pgt),), writes=(("tA", 2),))
            S.op("dve", lambda: nc.vector.scalar_tensor_tensor(out=gs[:, h, :], in0=tA[:, 2, :], scalar=1.0,
                                                              in1=ps[Pgt][:], op0=ALU.add, op1=ALU.mult),
                 reads=(("tA", 2), PS(Pgt)), writes=(("gs", h),))
            ps_free(Pgt)

        def m_head(h, mid=None):
            sl = h % 2
            QK = QK2[:, sl]
            bf_ = hgrn_state_mm(h, 0, sl)
            bb_ = hgrn_state_mm(h, 1, sl)
            hgrn_recur2(h, bf_, bb_, sl)
            for d_ in range(2):
                b = ps_alloc()

                def f():
                    i_ = None
                    for pr in range(4):
                        i_ = nc.tensor.matmul(ps[b][:, pr * 128:(pr + 1) * 128],
                                              lhsT=QK[:, 1 + 2 * d_, pr * 128:(pr + 1) * 128],
                                              rhs=QK[:, 2 * d_, pr * 128:(pr + 1) * 128], start=True, stop=True)
                    return i_

                S.op("pe", f, reads=(("QK", sl, 2 * d_), ("QK", sl, 1 + 2 * d_)), writes=(PS(b),))
                S.op("dve", lambda: nc.vector.tensor_tensor(
                    out=Am[:, d_, :].rearrange("p (g t) -> p g t", g=4), in0=ps[b][:].rearrange("p (g t) -> p g t", g=4),
                    in1=MASKN[d_].unsqueeze(1).to_broadcast([128, 4, 128]), op=ALU.mult),
                    reads=(PS(b), "cbf"), writes=(("Am", d_),))
                ps_free(b)
            if mid is not None:
                mid()
            b = ps_alloc()

            def fo():
                i_ = None
                for pr in range(4):
                    nc.tensor.matmul(ps[b][:, pr * 128:(pr + 1) * 128], lhsT=vtok[:, pr, h * 128:(h + 1) * 128],
                                     rhs=Am[:, 0, pr * 128:(pr + 1) * 128], start=True, stop=False)
                    nc.tensor.matmul(ps[b][:, pr * 128:(pr + 1) * 128], lhsT=vtok[:, pr, h * 128:(h + 1) * 128],
                                     rhs=Am[:, 1, pr * 128:(pr + 1) * 128], start=False, stop=False)
                    for hf in range(2):
                        c = pr * 2 + hf
                        for d_ in range(2):
                            i_ = nc.tensor.matmul(ps[b][:, c * 64:(c + 1) * 64], lhsT=Sbf[:, d_, c, :],
                                                  rhs=QK[:, 2 * d_, c * 64:(c + 1) * 64], start=False,
                                                  stop=(hf == 1 and d_ == 1))
                return i_

            S.op("pe", fo, reads=("vtok", ("Am", 0), ("Am", 1), ("QK", sl, 0), ("QK", sl, 2)) +
                 tuple(("Sbf", d_, c) for d_ in range(2) for c in range(8)), writes=(PS(b),))
            S.op("act", lambda: nc.scalar.activation(out=osq[:, h, :], in_=ps[b][:], func=AF.Square),
                 reads=(PS(b),), writes=(("osq", h),))
            S.op("act", lambda: nc.scalar.activation(out=big[:, h, :], in_=ps[b][:], func=AF.Copy,
                                                     scale=sv[:, SV_GV + l:SV_GV + l + 1]),
                 reads=(PS(b), "sv"), writes=(("big", h),))
            ps_free(b)
            S.op("pool", lambda: nc.gpsimd.tensor_tensor(out=big[:, h, :], in0=big[:, h, :], in1=gs[:, h, :], op=ALU.mult),
                 reads=(("big", h), ("gs", h)), writes=(("big", h),))

        nxt = (lambda tg: prep_hT(src_d, i + 1, (tg,))) if i + 1 < NT else (lambda tg: None)
        pq1 = e_proj(l, 1)
        conv_in(0)
        e_elem(l, 1, pq1)
        pq2 = e_proj(l, 2)
        conv_in(1)
        prev_out = pending.pop() if pending else None

        def mid0():
            sgu_in(0)
            if prev_out:
                prev_out((0, 1), False)

        def mid1():
            sgu_bv(2)
            if prev_out:
                prev_out((2, 3), True)

        m_head(0, mid=mid0)
        sgu_in(1)
        e_elem(l, 2, pq2)
        pq3 = e_proj(l, 3)
        sgu_bv(0)
        sgu_bv(1)
        m_head(1, mid=mid1)
        sgu_bv(3)
        e_elem(l, 3, pq3)
        nxt(0)
        sgu_out(0)
        m_head(2, mid=lambda: nxt(1))
        sgu_out(1)
        nxt(2)
        m_head(3, mid=lambda: nxt(3))
        rbuf = [(RC[:, 0, :], ("RC", 0)), (Ff[:], "Ff"), (Eb[:], "Eb"), (Fb[:, 1:T + 1], "Fb")]
        mb = []
        for h in range(4):
            b = ps_alloc()
            S.op("pe", lambda: nc.tensor.matmul(ps[b][:], lhsT=ONES128, rhs=osq[:, h, :], start=True, stop=True),
                 reads=("cbf", ("osq", h)), writes=(PS(b),))
            mb.append(b)

        cbanks = conv_mm(T)
        if True:
            for h in range(4):
                rb, rk = rbuf[h]
                S.op("act", lambda: nc.scalar.activation(out=rb, in_=ps[mb[h]][:], func=AF.Ln,
                                                         bias=sv[:, SV_EPS:SV_EPS + 1], scale=1.0),
                     reads=(PS(mb[h]), "sv_c"), writes=(rk,))
                ps_free(mb[h])
            for h in range(4):
                rb, rk = rbuf[h]
                S.op("act", lambda: nc.scalar.activation(out=rb, in_=rb, func=AF.Exp, scale=-0.5),
                     reads=(rk,), writes=(rk,))

        def hook_b():
            for h in range(4):
                rb, rk = rbuf[h]
                if h % 2 == 0:
                    S.op("dve", lambda: nc.vector.tensor_tensor(out=yT[:, 4 + h, 16:16 + T], in0=big[:, h, :], in1=rb,
                                                               op=ALU.mult),
                         reads=(("big", h), rk), writes=(("yT", 4 + h),))
                else:
                    S.op("pool", lambda: nc.gpsimd.tensor_tensor(out=yT[:, 4 + h, 16:16 + T], in0=big[:, h, :], in1=rb,
                                                                op=ALU.mult),
                         reads=(("big", h), rk), writes=(("yT", 4 + h),))

        def hook_e():
            hook_b()
            if i + 1 < NT:
                tile_start(l, i + 1)

        conv_tail(l, T, cbanks, hooks={"e": hook_e, "a": (lambda: None), "b": (lambda: None)})
        def do_out(wgs, carry):
            for wg in wgs:
                out_proj(l, src_d, dst_d, t0 - 16 + wg * 128, 128, wg * 128)
            if carry:
                S.op("pool", lambda: nc.gpsimd.tensor_copy(out=yT[:, 2:8, 0:16], in_=yT[:, 2:8, T:T + 16]),
                     reads=tuple(("yT", m) for m in range(2, 8)), writes=tuple(("yT", m) for m in range(2, 8)))

        S.op("pool", lambda: nc.gpsimd.tensor_copy(out=ybuf[:, :, 0:31], in_=ybuf[:, :, T:T + 31]),
             reads=("ybuf",), writes=("ybuf",))
        S.op("pool", lambda: nc.gpsimd.tensor_copy(out=ag[:, :, 0:16], in_=ag[:, :, T:T + 16]),
             reads=("ag",), writes=("ag",))
        if i + 1 < NT:
            pending.append(do_out)
        else:
            do_out((0, 1, 2, 3), True)

    def epilogue(l, src_d, dst_d):
        S.op("pool", lambda: nc.gpsimd.memset(ybuf[:, :, 31:64], 0.0), reads=(), writes=("ybuf",))
        conv_tail(l, 16)
        out_proj(l, src_d, dst_d, L - 16, 16, 0)

    def layer(l, src_d, dst_d):
        load_weights(l)
        S.op("dve", lambda: nc.vector.memset(Sb2[:], 0.0), writes=tuple(("Sb", h, p_) for h in range(4) for p_ in range(2)))
        for i in range(NT - 1, -1, -1):
            pass1_tile(l, src_d, i)
        S.op("dve", lambda: nc.vector.memset(Gf2[:], 0.0), writes=tuple(("Gf", h, p_) for h in range(4) for p_ in range(2)))
        S.op("dve", lambda: nc.vector.memset(dcar[:], 1.0), writes=tuple(("dcar", h) for h in range(4)))
        S.op("pool", lambda: nc.gpsimd.memset(ybuf[:], 0.0), writes=("ybuf",))
        S.op("pool", lambda: nc.gpsimd.memset(ag[:], 0.0), writes=("ag",))
        S.op("pool", lambda: nc.gpsimd.memset(yT[:], 0.0), writes=tuple(("yT", m) for m in range(8)))
        prep_hT(src_d, 0)
        started.clear()
        for i in range(NT):
            pass2_tile(l, src_d, dst_d, i)
        epilogue(l, src_d, dst_d)

    class _Stop(Exception):
        pass

    def layer_staged(l, src_d, dst_d):
        if stage < 1:
            raise _Stop()
        load_weights(l)
        if stage < 2:
            raise _Stop()
        S.op("dve", lambda: nc.vector.memset(Sb2[:], 0.0), writes=tuple(("Sb", h, p_) for h in range(4) for p_ in range(2)))
        for i in range(NT - 1, -1, -1):
            pass1_tile(l, src_d, i)
            if stage < 3:
                raise _Stop()
        if stage < 4:
            raise _Stop()
        S.op("dve", lambda: nc.vector.memset(Gf2[:], 0.0), writes=tuple(("Gf", h, p_) for h in range(4) for p_ in range(2)))
        S.op("dve", lambda: nc.vector.memset(dcar[:], 1.0), writes=tuple(("dcar", h) for h in range(4)))
        S.op("pool", lambda: nc.gpsimd.memset(ybuf[:], 0.0), writes=("ybuf",))
        S.op("pool", lambda: nc.gpsimd.memset(ag[:], 0.0), writes=("ag",))
        S.op("pool", lambda: nc.gpsimd.memset(yT[:], 0.0), writes=tuple(("yT", m) for m in range(8)))
        prep_hT(src_d, 0)
        started.clear()
        for i in range(NT):
            pass2_tile(l, src_d, dst_d, i)
            if stage < 5:
                raise _Stop()
        if stage < 6:
            raise _Stop()
        epilogue(l, src_d, dst_d)
        if stage < 7:
            raise _Stop()

    if stage < 99:
        setup()
        try:
            layer_staged(0, x_d, y_d)
        except (_Stop, StopBuild):
            pass
        for e in ("pe", "act", "dve", "pool"):
            if S.cnt[e] > 0:
                nc.sync.wait_ge(S.sems[e], S.cnt[e])
        S.finish([k for k in S.cnt if k not in ("pe", "act", "dve", "pool")])
        es.close()
        build.stats = (S.nops, S.nwaits)
        return nc

    setup()
    layer(0, x_d, x1_d)
    for key_ in list(S.last_w.keys()):
        if isinstance(key_, tuple) and key_[0] == "xrow1":
            S.last_w[("xrow",) + key_[1:]] = S.last_w[key_]
    layer(1, x1_d, y_d)
    S.finish(out_keys)
    es.close()
    build.stats = (S.nops, S.nwaits)
    return nc


def make_consts():
    c = np.zeros((128, 640), np.float32)
    c[:, 0:128] = np.eye(128)
    s = np.arange(128)[:, None]
    t = np.arange(128)[None, :]
    same = (s // 64) == (t // 64)
    c[:, 128:256] = -(same & (s <= t)).astype(np.float32)
    c[:, 256:384] = -(same & (s >= t)).astype(np.float32)
    c[:, 384:512] = 1.0 / 128
    c[:, 512:640] = 1.0 / 256
    csm = np.full((128, 512), 1e-35, np.float32)
    csm[:, ::64] = 1.0
    return c.astype(ml_dtypes.bfloat16), csm


_WNAMES = ["norm_g", "w_in", "conv_w", "conv_b", "conv_ln_g", "conv_ln_b", "conv_pw", "sgu_ln_g", "sgu_ln_b",
           "sgu_w", "sgu_b", "hgrn_lb_fwd", "hgrn_lb_bwd", "hgrn_norm_g", "w_out", "final_norm_g"]


def kernel(x_prompt, x_sample, **w):
    L = SEQ_P
    nc = build(L)
    cbf, csm = make_consts()
    base = {k: np.ascontiguousarray(np.asarray(w[k], dtype=np.float32)) for k in _WNAMES}
    base["cbf"] = cbf
    base["csm"] = csm
    xp = np.asarray(x_prompt, dtype=np.float32)
    xsm = np.asarray(x_sample, dtype=np.float32)
    in_maps = []
    for c in range(8):
        m = dict(base)
        if c < 4:
            m["x"] = np.ascontiguousarray(xp[c])
        else:
            pad = np.zeros((L, D), np.float32)
            pad[:SEQ_S] = xsm[c - 4]
            m["x"] = pad
        in_maps.append(m)
    res = run_bass_kernel_spmd(nc, in_maps, core_ids=list(range(8)))
    yp = np.stack([np.asarray(res.results[c]["y"], dtype=np.float32) for c in range(4)], axis=0)
    ys = np.stack([np.asarray(res.results[c]["y"], dtype=np.float32)[:SEQ_S] for c in range(4, 8)], axis=0)
    return (yp, ys)
```

```python
import numpy as np
import ml_dtypes
from contextlib import ExitStack
import concourse.bass as bass
import concourse.mybir as mybir
from concourse.bass_utils import run_bass_kernel_spmd

F32 = mybir.dt.float32
BF16 = mybir.dt.bfloat16
AF = mybir.ActivationFunctionType
ALU = mybir.AluOpType

D = 1024
DIN = 4096
T = 512
EPS = 1e-6
SEQ_P = 8192
SEQ_S = 4096

C_AVAL, C_AGLU, C_AGATE, C_BU, C_BV, C_BGATE, C_Q, C_I, C_FF, C_FB, C_GATE = (
    0, 256, 512, 768, 1024, 1280, 1536, 2048, 2560, 3072, 3584)


class StopBuild(Exception):
    pass


class Sched:
    def __init__(self, nc, es):
        self.nc = nc
        self.es = es
        self.engs = {"pe": nc.tensor, "act": nc.scalar, "dve": nc.vector, "pool": nc.gpsimd, "sp": nc.sync}
        self.sems = {}
        self.cnt = {}
        for e in ("pe", "act", "dve", "pool"):
            self.sems[e] = es.enter_context(nc.semaphore("s_" + e))
            self.cnt[e] = 0
        self.known = {e: {} for e in self.engs}
        self.last_w = {}
        self.readers = {}
        self.nwaits = 0
        self.nops = 0
        self.limit = 10 ** 9
        self.log = []

    def _deps(self, reads, writes):
        deps = {}

        def add(tok):
            if tok is None:
                return
            k, v = tok
            if deps.get(k, 0) < v:
                deps[k] = v

        for r in reads:
            add(self.last_w.get(r))
        for w in writes:
            add(self.last_w.get(w))
            for k, v in self.readers.get(w, {}).items():
                add((k, v))
        return deps

    def _emit_waits(self, eng, deps):
        kn = self.known[eng]
        for k, v in deps.items():
            if eng == "pe" and k == "pe":
                continue
            if kn.get(k, 0) >= v:
                continue
            self.engs[eng].wait_ge(self.sems[k], v)
            kn[k] = v
            self.nwaits += 1

    def _record(self, tok, reads, writes):
        for w in writes:
            self.last_w[w] = tok
            self.readers[w] = {}
        k, v = tok
        for r in reads:
            d = self.readers.setdefault(r, {})
            if d.get(k, 0) < v:
                d[k] = v

    def op(self, eng, fn, reads=(), writes=()):
        if self.nops >= self.limit:
            raise StopBuild()
        self.log.append((self.nops, eng, reads, writes))
        deps = self._deps(reads, writes)
        self._emit_waits(eng, deps)
        inst = fn()
        self.cnt[eng] += 1
        inst.then_inc(self.sems[eng], 1)
        self._record((eng, self.cnt[eng]), reads, writes)
        self.nops += 1

    def dma(self, key, fn, reads=(), writes=(), queue="sp"):
        if self.nops >= self.limit:
            raise StopBuild()
        self.log.append((self.nops, "dma", reads, writes))
        if key not in self.sems:
            self.sems[key] = self.es.enter_context(self.nc.semaphore("d_" + str(key).replace(" ", "")))
            self.cnt[key] = 0
        deps = self._deps(reads, writes)
        self._emit_waits(queue, deps)
        inst = fn()
        self.cnt[key] += 16
        inst.then_inc(self.sems[key], 16)
        self._record((key, self.cnt[key]), reads, writes)
        self.nops += 1

    def finish(self, keys):
        for k in keys:
            if k in self.cnt and self.cnt[k] > 0:
                self.engs["sp"].wait_ge(self.sems[k], self.cnt[k])


def build(L, debug=False, stage=99, limit=None):
    NT = L // T
    nc = bass.Bass("TRN2", target_bir_lowering=False, dynamic_dma_scratch_size=1024)
    es = ExitStack()
    es.enter_context(nc.allow_non_contiguous_dma(reason="small param vectors"))
    es.enter_context(nc.allow_low_precision(reason="bf16 matmul operands, fp32 accumulation"))

    def din(name, shape, dt=F32):
        return nc.dram_tensor(name, list(shape), dt, kind="ExternalInput").ap()

    x_d = din("x", [L, D])
    norm_g_d = din("norm_g", [2, D])
    w_in_d = din("w_in", [2, D, DIN])
    conv_w_d = din("conv_w", [2, 31, 256])
    conv_b_d = din("conv_b", [2, 256])
    cln_g_d = din("conv_ln_g", [2, 256])
    cln_b_d = din("conv_ln_b", [2, 256])
    conv_pw_d = din("conv_pw", [2, 256, 256])
    sln_g_d = din("sgu_ln_g", [2, 256])
    sln_b_d = din("sgu_ln_b", [2, 256])
    sgu_w_d = din("sgu_w", [2, 4, 128, 128])
    sgu_b_d = din("sgu_b", [2, 4, 128])
    lbf_d = din("hgrn_lb_fwd", [2, 512])
    lbb_d = din("hgrn_lb_bwd", [2, 512])
    hng_d = din("hgrn_norm_g", [2, 128])
    w_out_d = din("w_out", [2, D, D])
    fng_d = din("final_norm_g", [D])
    cbf_d = din("cbf", [128, 640], BF16)
    csm_d = din("csm", [128, 512])
    y_d = nc.dram_tensor("y", [L, D], F32, kind="ExternalOutput").ap()
    x1_d = nc.dram_tensor("x1s", [L, D], F32, kind="Internal").ap()
    sbnd_d = nc.dram_tensor("sbnd", [NT, 128, 512], F32, kind="Internal").ap()
    dbg_outs = {}

    def sb(name, shape, dt):
        return es.enter_context(nc.sbuf_tensor("sb_" + name, list(shape), dt))

    Wi = sb("Wi", [128, 8, DIN], BF16)
    Wo = sb("Wo", [128, 8, D], BF16)
    Wpw = sb("Wpw", [128, 2, 256], BF16)
    Wcd = sb("Wcd", [128, 2, 31, 128], BF16)
    Wsg = sb("Wsg", [128, 4, 128], BF16)
    cbf = sb("cbf", [128, 640], BF16)
    IDB = cbf[:, 0:128]
    MASKN = [cbf[:, 128:256], cbf[:, 256:384]]
    ONES128 = cbf[:, 384:512]
    ONES256 = cbf[:, 512:640]
    csm = sb("csm", [128, 512], F32)
    identF = sb("identF", [128, 128], F32)
    fng = sb("fng", [128, D], F32)
    sgg = sb("sgg", [128, 2, 256], F32)
    sgb = sb("sgb", [128, 2, 256], F32)
    sbias = sb("sbias", [128, 2, 2, 128], F32)
    sv = sb("sv", [128, 64], F32)
    lbc = sb("lbc", [128, 2, 2, 2, 4], F32)
    cw = sb("cw", [128, 2, 31], F32)
    SV_NG, SV_CB, SV_CLG, SV_CLB, SV_NCLG, SV_NCLB, SV_GV, SV_LBF, SV_LBB, SV_EPS, SV_M05, SV_ONE = (
        0, 16, 20, 24, 28, 32, 36, 38, 46, 54, 55, 56)
    xin = sb("xin", [128, 2, D], F32)
    xs = sb("xs", [128, D], BF16)
    hT = sb("hT", [128, 8, T], BF16)
    ybuf = sb("ybuf", [128, 2, 544], BF16)
    ag = sb("ag", [128, 2, 528], BF16)
    tA = sb("tA", [128, 3, T], F32)
    ub = sb("ub", [128, 2, T], BF16)
    bg = sb("bg", [128, 2, T], BF16)
    gvb = sb("gvb", [128, 1, 256], F32)
    vsn = sb("vsn", [128, 4, 256], BF16)
    vtok = sb("vtok", [128, 4, T], BF16)
    gs = sb("gs", [128, 4, T], BF16)
    Ff = sb("Ff", [128, T], F32)
    Fb = sb("Fb", [128, T + 1], F32)
    CPf2 = sb("CPf", [128, 2, T], F32)
    Eb = sb("Eb", [128, T], F32)
    RC = sb("RC", [128, 1, T], F32)
    QK2 = sb("QK", [128, 2, 4, T], BF16)
    kT = sb("kT", [128, 2, T], BF16)
    Am = sb("Am", [128, 2, T], BF16)
    Sbf = sb("Sbf", [128, 2, 8, 128], BF16)
    Gf2 = sb("Gf", [128, 4, 2, 128], F32)
    Sb2 = sb("Sb", [128, 4, 2, 128], F32)
    dcar = sb("dcar", [128, 4], F32)
    DB2 = sb("DB", [128, 2, 8], F32)
    DB4 = sb("DB4", [128, 4, 8], F32)
    big = sb("big", [128, 4, T], F32)
    osq = sb("osq", [128, 4, T], BF16)
    rstd = sb("rstd", [128, T], F32)
    yT = sb("yT", [128, 8, 528], BF16)
    cn = big
    cnb = osq
    csq = osq[:, 2:4]
    xres = sb("xres", [128, 2, D], F32)
    st6 = sb("st6", [128, 4, 6], F32)
    mv = sb("mv", [128, 4, 2], F32)
    sm = sb("sm", [128, 32], F32)

    ps = [es.enter_context(nc.psum_tensor("ps%d" % i, [128, 512], F32)) for i in range(8)]

    S = Sched(nc, es)
    if limit is not None:
        S.limit = limit
    build.S = S
    free_banks = list(range(8))

    def ps_alloc():
        assert free_banks, "out of PSUM banks"
        return free_banks.pop(0)

    def ps_free(b):
        free_banks.append(b)

    def PS(b):
        return ("ps", b)

    out_keys = []

    def rowres(prefix, r0, n):
        res = []
        for g in range(r0 // 128, (r0 + n - 1) // 128 + 1):
            lo0, hi0, end = g * 128, g * 128 + 112, (g + 1) * 128
            if r0 < hi0 and r0 + n > lo0:
                res.append((prefix, g, "lo"))
            if r0 < end and r0 + n > hi0:
                res.append((prefix, g, "hi"))
        return tuple(res)

    def setup():
        loads = []

        def ld(out, in_, res):
            k = ("setup", len(loads))
            S.dma(k, lambda: nc.sync.dma_start(out=out, in_=in_), reads=(), writes=(("setup_r", len(loads)),))
            loads.append((k, res))

        ld(cbf[:], cbf_d[:, :], "cbf")
        ld(csm[:], csm_d[:, :], "csm")
        ld(fng[:], fng_d.partition_broadcast(128), "fng")
        for l in range(2):
            ld(sgg[:, l, :], sln_g_d[l, :].partition_broadcast(128), "sgg")
            ld(sgb[:, l, :], sln_b_d[l, :].partition_broadcast(128), "sgb")
            for h in range(4):
                ld(sbias[(h % 2) * 64:(h % 2) * 64 + 64, l, h // 2, :],
                   sgu_b_d[l, h, :].partition_broadcast(64), "sbias")
        ld(sv[:, SV_NG:SV_NG + 16].rearrange("p (l k) -> p l k", l=2),
           norm_g_d.rearrange("l (k p) -> p l k", p=128), "sv")
        for col, src in ((SV_CB, conv_b_d), (SV_CLG, cln_g_d), (SV_CLB, cln_b_d)):
            ld(sv[:, col:col + 4].rearrange("p (l j) -> p l j", l=2),
               src.rearrange("l (j p) -> p l j", p=128), "sv")
        ld(sv[:, SV_GV:SV_GV + 2], hng_d.rearrange("l p -> p l"), "sv")
        ld(sv[:, SV_LBF:SV_LBF + 8].rearrange("p (l h) -> p l h", l=2),
           lbf_d.rearrange("l (h p) -> p l h", p=128), "sv")
        ld(sv[:, SV_LBB:SV_LBB + 8].rearrange("p (l h) -> p l h", l=2),
           lbb_d.rearrange("l (h p) -> p l h", p=128), "sv")
        for eng in ("pe", "act", "dve", "pool", "sp"):
            for k, _ in loads:
                S.engs[eng].wait_ge(S.sems[k], 16)
                S.known[eng][k] = 16
        S.op("dve", lambda: nc.vector.memset(sv[:, SV_EPS:SV_EPS + 1], EPS), writes=("sv_c",))
        S.op("dve", lambda: nc.vector.memset(sv[:, SV_M05:SV_M05 + 1], -0.5), writes=("sv_c",))
        S.op("dve", lambda: nc.vector.memset(sv[:, SV_ONE:SV_ONE + 1], 1.0), writes=("sv_c",))
        S.op("pool", lambda: nc.gpsimd.memset(Fb[:, 0:1], 1.0), writes=("Fb",))
        for s_ in range(2):
            S.op("pool", lambda: nc.gpsimd.memset(xres[:, s_, :], 0.0), writes=(("xres", s_),))
        S.op("dve", lambda: nc.vector.tensor_copy(out=identF[:], in_=IDB), reads=("cbf",), writes=("identF",))
        S.op("dve", lambda: nc.vector.tensor_scalar(out=sv[:, SV_NCLG:SV_NCLG + 8], in0=sv[:, SV_CLG:SV_CLG + 8],
                                                    scalar1=-1.0, scalar2=None, op0=ALU.mult),
             reads=("sv",), writes=("sv_d",))
        S.op("dve", lambda: nc.vector.tensor_scalar(out=sv[:, SV_GV:SV_GV + 2], in0=sv[:, SV_GV:SV_GV + 2],
                                                    scalar1=0.5, scalar2=None, op0=ALU.mult),
             reads=("sv",), writes=("sv",))
        S.op("dve", lambda: nc.vector.memset(lbc[:, 0], 0.5), writes=("lbc",))
        for d_, col in ((0, SV_LBF), (1, SV_LBB)):
            S.op("dve", lambda: nc.vector.tensor_tensor(out=sm[:, 0:4], in0=sv[:, col + 4:col + 8],
                                                        in1=sv[:, col:col + 4], op=ALU.subtract),
                 reads=("sv",), writes=("sm",))
            S.op("act", lambda: nc.scalar.activation(out=sm[:, 4:8], in_=sm[:, 0:4], func=AF.Tanh, scale=0.5),
                 reads=("sm",), writes=("sm2",))
            S.op("dve", lambda: nc.vector.tensor_scalar(out=lbc[:, 1, d_, 0, :], in0=sm[:, 4:8], scalar1=-0.25,
                                                        scalar2=0.25, op0=ALU.mult, op1=ALU.add),
                 reads=("sm2",), writes=("lbc",))
            S.op("dve", lambda: nc.vector.tensor_scalar(out=lbc[:, 1, d_, 1, :], in0=sm[:, 4:8], scalar1=0.25,
                                                        scalar2=0.75, op0=ALU.mult, op1=ALU.add),
                 reads=("sm2",), writes=("lbc",))

    def load_weights(l):
        stg = [(xin[:, 0, :], ("xin", 0)), (xin[:, 1, :], ("xin", 1)), (xres[:, 0, :], ("xres", 0)),
               (xres[:, 1, :], ("xres", 1))]
        slot = [0]

        def nslot4():
            s_ = slot[0]
            slot[0] = (slot[0] + 1) % 4
            return s_

        def nslot():
            s_ = slot[0] % 2
            slot[0] = (slot[0] + 1) % 4
            return s_

        first = {"act": True, "dve": True, "pool": True}
        n = 0
        for kc in range(8):
            for q in range(4):
                buf, bkey = stg[nslot4()]
                S.dma(bkey, lambda: nc.sync.dma_start(
                    out=buf, in_=w_in_d[l, kc * 128:(kc + 1) * 128, q * 1024:(q + 1) * 1024]), writes=(bkey,))
                eng = "act" if n % 2 == 0 else "dve"
                wr = ("Wi", ("Wi", n)) if first[eng] else (("Wi", n),)
                first[eng] = False
                if eng == "act":
                    S.op("act", lambda: nc.scalar.activation(
                        out=Wi[:, kc, q * 1024:(q + 1) * 1024], in_=buf, func=AF.Copy,
                        scale=sv[:, SV_NG + l * 8 + kc:SV_NG + l * 8 + kc + 1]),
                        reads=(bkey, "sv"), writes=wr)
                else:
                    S.op("dve", lambda: nc.vector.tensor_scalar(
                        out=Wi[:, kc, q * 1024:(q + 1) * 1024], in0=buf,
                        scalar1=sv[:, SV_NG + l * 8 + kc:SV_NG + l * 8 + kc + 1], scalar2=None, op0=ALU.mult),
                        reads=(bkey, "sv"), writes=wr)
                n += 1
        for mc in range(8):
            buf, bkey = stg[nslot4()]
            S.dma(bkey, lambda: nc.sync.dma_start(out=buf, in_=w_out_d[l, mc * 128:(mc + 1) * 128, :]), writes=(bkey,))
            wr = ("Wo", ("Wo", mc)) if first["pool"] else (("Wo", mc),)
            first["pool"] = False
            S.op("pool", lambda: nc.gpsimd.tensor_copy(out=Wo[:, mc, :], in_=buf), reads=(bkey,), writes=wr)
        S._emit_waits("pe", {"act": S.cnt["act"], "dve": S.cnt["dve"], "pool": S.cnt["pool"]})
        slot[0] = 0
        s_ = nslot()
        S.dma(("xin", s_), lambda: nc.sync.dma_start(
            out=xin[:, s_, 0:512].rearrange("p (j c) -> p j c", j=2),
            in_=conv_pw_d[l].rearrange("(j p) c -> p j c", p=128)), writes=(("xin", s_),))
        S.op("dve", lambda: nc.vector.tensor_scalar(
            out=Wpw[:].rearrange("p j c -> p (j c)"), in0=xin[:, s_, 0:512], scalar1=0.5, scalar2=None,
            op0=ALU.mult), reads=(("xin", s_),), writes=("Wpw",))
        s_ = nslot()
        S.dma(("xin", s_), lambda: nc.sync.dma_start(out=xin[0:31, s_, 0:256], in_=conv_w_d[l]), writes=(("xin", s_),))
        b = ps_alloc()
        for j in range(2):
            S.op("pe", lambda: nc.tensor.transpose(out=ps[b][:, j * 32:j * 32 + 31],
                                                   in_=xin[0:31, s_, j * 128:(j + 1) * 128],
                                                   identity=identF[0:31, 0:31]),
                 reads=(("xin", s_), "identF"), writes=(PS(b),))
        S.op("dve", lambda: nc.vector.tensor_copy(
            out=cw[:], in_=ps[b][:, 0:64].rearrange("p (j t) -> p j t", j=2)[:, :, 0:31]),
            reads=(PS(b),), writes=("cw",))
        ps_free(b)
        for j in range(2):
            for tp in range(31):
                S.op("pool", lambda: nc.gpsimd.tensor_scalar(
                    out=Wcd[:, j, tp, :], in0=IDB, scalar1=cw[:, j, tp:tp + 1], scalar2=0.5,
                    op0=ALU.mult, op1=ALU.mult), reads=("cw", "cbf"), writes=("Wcd",))
        s_ = nslot()
        S.dma(("xin", s_), lambda: nc.sync.dma_start(
            out=xin[:, s_, 0:512].rearrange("p (h s) -> p h s", h=4),
            in_=sgu_w_d[l].rearrange("h t s -> t h s")), writes=(("xin", s_),))
        b = ps_alloc()
        for h in range(4):
            S.op("pe", lambda: nc.tensor.transpose(out=ps[b][:, h * 128:(h + 1) * 128],
                                                   in_=xin[:, s_, h * 128:(h + 1) * 128], identity=identF[:]),
                 reads=(("xin", s_), "identF"), writes=(PS(b),))
        S.op("dve", lambda: nc.vector.tensor_copy(out=Wsg[:].rearrange("p h t -> p (h t)"), in_=ps[b][:]),
             reads=(PS(b),), writes=("Wsg",))
        ps_free(b)

    def proj_fm(fc):
        b = ps_alloc()

        def f():
            i = None
            for kc in range(8):
                i = nc.tensor.matmul(ps[b][:], lhsT=Wi[:, kc, fc * 128:(fc + 1) * 128], rhs=hT[:, kc, :],
                                     start=(kc == 0), stop=(kc == 7))
            return i

        S.op("pe", f, reads=("Wi", "hT"), writes=(PS(b),))
        return b

    def proj_tm(tg, col0, ncols):
        b = ps_alloc()

        def f():
            i = None
            for kc in range(8):
                i = nc.tensor.matmul(ps[b][:, 0:ncols], lhsT=hT[:, kc, tg * 128:(tg + 1) * 128],
                                     rhs=Wi[:, kc, col0:col0 + ncols], start=(kc == 0), stop=(kc == 7))
            return i

        S.op("pe", f, reads=("Wi", "hT"), writes=(PS(b),))
        return b

    xin_slot = [0]

    def prep_hT(src_d, i, tgs=(0, 1, 2, 3)):
        t0 = i * T
        for tg in tgs:
            s_ = xin_slot[0]
            xin_slot[0] ^= 1
            r0 = t0 + tg * 128
            S.dma(("xin", s_), lambda: nc.sync.dma_start(out=xin[:, s_, :], in_=src_d[r0:r0 + 128, :]),
                  reads=rowres("xrow", r0, 128), writes=(("xin", s_),))
            S.op("act", lambda: nc.scalar.activation(out=xs[:], in_=xin[:, s_, :], func=AF.Square,
                                                     accum_out=sm[:, 8 + tg:9 + tg]),
                 reads=(("xin", s_),), writes=("xs", ("ssq", tg)))
            S.op("pool", lambda: nc.gpsimd.tensor_scalar(out=sm[:, 12 + tg:13 + tg], in0=sm[:, 8 + tg:9 + tg],
                                                        scalar1=1.0 / D, scalar2=EPS, op0=ALU.mult, op1=ALU.add),
                 reads=(("ssq", tg),), writes=(("ms", tg),))
            S.op("pool", lambda: nc.gpsimd.tensor_tensor(out=sm[:, 16 + tg:17 + tg], in0=sm[:, 12 + tg:13 + tg],
                                                        in1=sv[:, SV_M05:SV_M05 + 1], op=ALU.pow),
                 reads=(("ms", tg), "sv_c"), writes=(("rs", tg),))
            S.op("act", lambda: nc.scalar.activation(out=xs[:], in_=xin[:, s_, :], func=AF.Copy,
                                                     scale=sm[:, 16 + tg:17 + tg]),
                 reads=(("xin", s_), ("rs", tg)), writes=("xs",))
            b = ps_alloc()
            pbf = ps[b][:].bitcast(BF16)

            def f():
                i_ = None
                for kc in range(8):
                    i_ = nc.tensor.transpose(out=pbf[:, kc * 128:(kc + 1) * 128], in_=xs[:, kc * 128:(kc + 1) * 128],
                                             identity=IDB)
                return i_

            S.op("pe", f, reads=("xs", "cbf"), writes=(PS(b),))
            eng = "act"
            if eng == "act":
                S.op("act", lambda: nc.scalar.copy(out=hT[:, :, tg * 128:(tg + 1) * 128],
                                                   in_=pbf.rearrange("p (k t) -> p k t", k=8)),
                     reads=(PS(b),), writes=("hT",))
            else:
                S.op("dve", lambda: nc.vector.tensor_copy(out=hT[:, :, tg * 128:(tg + 1) * 128],
                                                          in_=pbf.rearrange("p (k t) -> p k t", k=8)),
                     reads=(PS(b),), writes=("hT",))
            ps_free(b)

    def tanh_gate(bank, dst, slot):
        S.op("act", lambda: nc.scalar.activation(out=tA[:, slot, :], in_=ps[bank][:], func=AF.Tanh, scale=0.5),
             reads=(PS(bank),), writes=(("tA", slot),))
        S.op("dve", lambda: nc.vector.scalar_tensor_tensor(out=dst, in0=tA[:, slot, :], scalar=1.0, in1=ps[bank][:],
                                                          op0=ALU.add, op1=ALU.mult),
             reads=(("tA", slot), PS(bank)), writes=())

    def hgrn_dir_elem(l, h, d_, Pz, Pq, need_q, sl):
        a_ap = lbc[:, l, d_, 0, h:h + 1]
        c_ap = lbc[:, l, d_, 1, h:h + 1]
        QK = QK2[:, sl]
        CPf = CPf2[:, sl]
        DB = DB2[:, sl]
        if Pz is not None:
            S.op("act", lambda: nc.scalar.activation(out=tA[:, d_, :], in_=ps[Pz][:], func=AF.Tanh, scale=0.5),
                 reads=(PS(Pz),), writes=(("tA", d_),))
            ps_free(Pz)
        if d_ == 0:
            S.op("pool", lambda: nc.gpsimd.tensor_scalar(out=Ff[:], in0=tA[:, 0, :], scalar1=a_ap, scalar2=c_ap,
                                                        op0=ALU.mult, op1=ALU.add),
                 reads=(("tA", 0), "lbc"), writes=("Ff",))
            S.op("dve", lambda: nc.vector.tensor_tensor_scan(out=CPf, data0=csm[:], data1=Ff[:], initial=1.0,
                                                            op0=ALU.max, op1=ALU.mult),
                 reads=("Ff", "csm"), writes=(("CPf", sl),))
            if need_q:
                S.op("dve", lambda: nc.vector.tensor_tensor(out=QK[:, 0, :], in0=ps[Pq][:], in1=CPf, op=ALU.mult),
                     reads=(PS(Pq), ("CPf", sl)), writes=(("QK", sl, 0),))
            S.op("dve", lambda: nc.vector.reciprocal(out=RC[:, 0, :], in_=CPf), reads=(("CPf", sl),),
                 writes=(("RC", 0),))
            S.op("pool", lambda: nc.gpsimd.tensor_scalar(out=tA[:, 0, :], in0=Ff[:], scalar1=-1.0, scalar2=1.0,
                                                        op0=ALU.add, op1=ALU.mult),
                 reads=("Ff",), writes=(("tA", 0),))
            S.op("pool", lambda: nc.gpsimd.tensor_tensor(out=QK[:, 1, :], in0=tA[:, 0, :], in1=RC[:, 0, :], op=ALU.mult),
                 reads=(("tA", 0), ("RC", 0)), writes=(("QK", sl, 1),))
        else:
            S.op("pool", lambda: nc.gpsimd.tensor_scalar(out=Fb[:, 1:T + 1], in0=tA[:, 1, :], scalar1=a_ap,
                                                        scalar2=c_ap, op0=ALU.mult, op1=ALU.add),
                 reads=(("tA", 1), "lbc"), writes=("Fb",))
            S.op("dve", lambda: nc.vector.tensor_tensor_scan(out=Eb[:], data0=Fb[:, 0:T], data1=csm[:], initial=1.0,
                                                            op0=ALU.mult, op1=ALU.max),
                 reads=("Fb", "csm"), writes=("Eb",))
            if need_q:
                S.op("dve", lambda: nc.vector.reciprocal(out=RC[:, 0, :], in_=Eb[:]), reads=("Eb",),
                     writes=(("RC", 0),))
                S.op("dve", lambda: nc.vector.tensor_tensor(out=QK[:, 2, :], in0=ps[Pq][:], in1=RC[:, 0, :],
                                                           op=ALU.mult),
                     reads=(PS(Pq), ("RC", 0)), writes=(("QK", sl, 2),))
            S.op("pool", lambda: nc.gpsimd.tensor_scalar(out=tA[:, 1, :], in0=Fb[:, 1:T + 1], scalar1=-1.0, scalar2=1.0,
                                                        op0=ALU.add, op1=ALU.mult),
                 reads=("Fb",), writes=(("tA", 1),))
            S.op("pool", lambda: nc.gpsimd.tensor_tensor(out=QK[:, 3, :], in0=tA[:, 1, :], in1=Eb[:], op=ALU.mult),
                 reads=(("tA", 1), "Eb"), writes=(("QK", sl, 3),))
            S.op("dve", lambda: nc.vector.tensor_tensor(
                out=DB, in0=Eb[:].rearrange("p (c t) -> p c t", t=64)[:, :, 63],
                in1=Fb[:, 1:T + 1].rearrange("p (c t) -> p c t", t=64)[:, :, 63], op=ALU.mult),
                reads=("Eb", "Fb"), writes=(("DB", sl),))

    def hgrn_elem2(l, h, Pq, sl):
        QK = QK2[:, sl]
        CPf = CPf2[:, sl]
        DB = DB2[:, sl]
        for d_, dst in ((0, Ff[:]), (1, Fb[:, 1:T + 1])):
            S.op("pool", lambda: nc.gpsimd.tensor_scalar(out=dst, in0=tA[:, d_, :], scalar1=lbc[:, l, d_, 0, h:h + 1],
                                                        scalar2=lbc[:, l, d_, 1, h:h + 1], op0=ALU.mult, op1=ALU.add),
                 reads=(("tA", d_), "lbc"), writes=("Ff" if d_ == 0 else "Fb",))
        S.op("dve", lambda: nc.vector.tensor_tensor_scan(out=CPf, data0=csm[:], data1=Ff[:], initial=1.0,
                                                        op0=ALU.max, op1=ALU.mult),
             reads=("Ff", "csm"), writes=(("CPf", sl),))
        S.op("dve", lambda: nc.vector.tensor_tensor_scan(out=Eb[:], data0=Fb[:, 0:T], data1=csm[:], initial=1.0,
                                                        op0=ALU.mult, op1=ALU.max),
             reads=("Fb", "csm"), writes=("Eb",))
        S.op("dve", lambda: nc.vector.tensor_tensor(out=QK[:, 0, :], in0=ps[Pq][:], in1=CPf, op=ALU.mult),
             reads=(PS(Pq), ("CPf", sl)), writes=(("QK", sl, 0),))
        S.op("dve", lambda: nc.vector.reciprocal(out=RC[:, 0, :], in_=CPf), reads=(("CPf", sl),), writes=(("RC", 0),))
        S.op("pool", lambda: nc.gpsimd.tensor_scalar(out=tA[:, 1, :], in0=Fb[:, 1:T + 1], scalar1=-1.0, scalar2=1.0,
                                                    op0=ALU.add, op1=ALU.mult),
             reads=("Fb",), writes=(("tA", 1),))
        S.op("pool", lambda: nc.gpsimd.tensor_tensor(out=QK[:, 3, :], in0=tA[:, 1, :], in1=Eb[:], op=ALU.mult),
             reads=(("tA", 1), "Eb"), writes=(("QK", sl, 3),))
        S.op("dve", lambda: nc.vector.tensor_tensor(
            out=DB, in0=Eb[:].rearrange("p (c t) -> p c t", t=64)[:, :, 63],
            in1=Fb[:, 1:T + 1].rearrange("p (c t) -> p c t", t=64)[:, :, 63], op=ALU.mult),
            reads=("Eb", "Fb"), writes=(("DB", sl),))
        S.op("pool", lambda: nc.gpsimd.tensor_scalar(out=tA[:, 0, :], in0=Ff[:], scalar1=-1.0, scalar2=1.0,
                                                    op0=ALU.add, op1=ALU.mult),
             reads=("Ff",), writes=(("tA", 0),))
        S.op("pool", lambda: nc.gpsimd.tensor_tensor(out=QK[:, 1, :], in0=tA[:, 0, :], in1=RC[:, 0, :], op=ALU.mult),
             reads=(("tA", 0), ("RC", 0)), writes=(("QK", sl, 1),))
        S.op("dve", lambda: nc.vector.reciprocal(out=Eb[:], in_=Eb[:]), reads=("Eb",), writes=("Eb",))
        S.op("dve", lambda: nc.vector.tensor_tensor(out=QK[:, 2, :], in0=ps[Pq][:], in1=Eb[:], op=ALU.mult),
             reads=(PS(Pq), "Eb"), writes=(("QK", sl, 2),))

    def hgrn_state_mm(h, d_, sl):
        b = ps_alloc()
        pbf = ps[b][:].bitcast(BF16)

        def f():
            i_ = None
            for pr in range(4):
                i_ = nc.tensor.transpose(out=pbf[:, pr * 128:(pr + 1) * 128],
                                         in_=QK2[:, sl, 1 + 2 * d_, pr * 128:(pr + 1) * 128], identity=IDB)
            return i_

        S.op("pe", f, reads=(("QK", sl, 1 + 2 * d_), "cbf"), writes=(PS(b),))
        S.op("act", lambda: nc.scalar.copy(out=kT[:, d_, :], in_=pbf[:, 0:T]), reads=(PS(b),), writes=(("kT", d_),))
        ps_free(b)
        banks = []
        for half in range(2):
            bb = ps_alloc()

            def g():
                i_ = None
                for cc in range(4):
                    c = cc * 2 + half
                    pr, hf = c // 2, c % 2
                    i_ = nc.tensor.matmul(ps[bb][:, cc * 128:(cc + 1) * 128],
                                          lhsT=kT[hf * 64:(hf + 1) * 64, d_, pr * 128:(pr + 1) * 128],
                                          rhs=vtok[hf * 64:(hf + 1) * 64, pr, h * 128:(h + 1) * 128],
                                          start=True, stop=True)
                return i_

            S.op("pe", g, reads=(("kT", d_), "vtok"), writes=(PS(bb),))
            banks.append(bb)
        return banks

    def Pchunk(banks, c):
        return ps[banks[c % 2]][:, (c // 2) * 128:(c // 2 + 1) * 128], PS(banks[c % 2])

    def hgrn_bwd_recur(h, banks, snapshots, sl):
        DB = DB2[:, sl]
        par = 0
        for c in range(7, -1, -1):
            pc, pres = Pchunk(banks, c)
            src, dst = par, 1 - par
            if snapshots:
                S.op("pool", lambda: nc.gpsimd.tensor_scalar(out=Sbf[:, 1, c, :], in0=Sb2[:, h, src, :],
                                                            scalar1=DB[:, c:c + 1], scalar2=1.0, op0=ALU.mult,
                                                            op1=ALU.mult),
                     reads=(("Sb", h, src), ("DB", sl)), writes=(("Sbf", 1, c),))
            S.op("dve", lambda: nc.vector.scalar_tensor_tensor(out=Sb2[:, h, dst, :], in0=Sb2[:, h, src, :],
                                                              scalar=DB[:, c:c + 1], in1=pc, op0=ALU.mult,
                                                              op1=ALU.subtract),
                 reads=(("Sb", h, src), ("DB", sl), pres), writes=(("Sb", h, dst),))
            par = dst
        assert par == 0
        for bb in banks:
            ps_free(bb)

    def hgrn_fwd_recur(h, banks, sl):
        CPf = CPf2[:, sl]
        S.op("pool", lambda: nc.gpsimd.tensor_scalar(out=Sbf[:, 0, 0, :], in0=Gf2[:, h, 0, :], scalar1=dcar[:, h:h + 1],
                                                    scalar2=1.0, op0=ALU.mult, op1=ALU.mult),
             reads=(("Gf", h, 0), ("dcar", h)), writes=(("Sbf", 0, 0),))
        par = 0
        for c in range(8):
            pc, pres = Pchunk(banks, c)
            src, dst = par, 1 - par
            dprev = dcar[:, h:h + 1] if c == 0 else CPf[:, c * 64 - 1:c * 64]
            S.op("dve", lambda: nc.vector.scalar_tensor_tensor(out=Gf2[:, h, dst, :], in0=Gf2[:, h, src, :], scalar=dprev,
                                                              in1=pc, op0=ALU.mult, op1=ALU.subtract),
                 reads=(("Gf", h, src), ("dcar", h), ("CPf", sl), pres), writes=(("Gf", h, dst),))
            par = dst
            if c < 7:
                S.op("pool", lambda: nc.gpsimd.tensor_scalar(out=Sbf[:, 0, c + 1, :], in0=Gf2[:, h, dst, :],
                                                            scalar1=CPf[:, c * 64 + 63:c * 64 + 64], scalar2=1.0,
                                                            op0=ALU.mult, op1=ALU.mult),
                     reads=(("Gf", h, dst), ("CPf", sl)), writes=(("Sbf", 0, c + 1),))
        assert par == 0
        S.op("dve", lambda: nc.vector.tensor_copy(out=dcar[:, h:h + 1], in_=CPf[:, T - 1:T]),
             reads=(("CPf", sl),), writes=(("dcar", h),))
        for bb in banks:
            ps_free(bb)

    def hgrn_recur2(h, bf_, bb_, sl):
        CPf = CPf2[:, sl]
        DB = DB2[:, sl]
        S.op("pool", lambda: nc.gpsimd.tensor_scalar(out=Sbf[:, 0, 0, :], in0=Gf2[:, h, 0, :], scalar1=dcar[:, h:h + 1],
                                                    scalar2=1.0, op0=ALU.mult, op1=ALU.mult),
             reads=(("Gf", h, 0), ("dcar", h)), writes=(("Sbf", 0, 0),))
        pf = pb = 0
        for k in range(8):
            c = k
            pc, pres = Pchunk(bf_, c)
            src, dst = pf, 1 - pf
            dprev = dcar[:, h:h + 1] if c == 0 else CPf[:, c * 64 - 1:c * 64]
            S.op("dve", lambda: nc.vector.scalar_tensor_tensor(out=Gf2[:, h, dst, :], in0=Gf2[:, h, src, :], scalar=dprev,
                                                              in1=pc, op0=ALU.mult, op1=ALU.subtract),
                 reads=(("Gf", h, src), ("dcar", h), ("CPf", sl), pres), writes=(("Gf", h, dst),))
            pf = dst
            if c < 7:
                S.op("pool", lambda: nc.gpsimd.tensor_scalar(out=Sbf[:, 0, c + 1, :], in0=Gf2[:, h, dst, :],
                                                            scalar1=CPf[:, c * 64 + 63:c * 64 + 64], scalar2=1.0,
                                                            op0=ALU.mult, op1=ALU.mult),
                     reads=(("Gf", h, dst), ("CPf", sl)), writes=(("Sbf", 0, c + 1),))
            c = 7 - k
            pc, pres = Pchunk(bb_, c)
            src, dst = pb, 1 - pb
            S.op("pool", lambda: nc.gpsimd.tensor_scalar(out=Sbf[:, 1, c, :], in0=Sb2[:, h, src, :],
                                                        scalar1=DB[:, c:c + 1], scalar2=1.0, op0=ALU.mult, op1=ALU.mult),
                 reads=(("Sb", h, src), ("DB", sl)), writes=(("Sbf", 1, c),))
            S.op("dve", lambda: nc.vector.scalar_tensor_tensor(out=Sb2[:, h, dst, :], in0=Sb2[:, h, src, :],
                                                              scalar=DB[:, c:c + 1], in1=pc, op0=ALU.mult,
                                                              op1=ALU.subtract),
                 reads=(("Sb", h, src), ("DB", sl), pres), writes=(("Sb", h, dst),))
            pb = dst
        assert pf == 0 and pb == 0
        S.op("dve", lambda: nc.vector.tensor_copy(out=dcar[:, h:h + 1], in_=CPf[:, T - 1:T]),
             reads=(("CPf", sl),), writes=(("dcar", h),))
        for bb in bf_ + bb_:
            ps_free(bb)

    def conv_mm(W):
        cb = []
        for j in range(2):
            b = ps_alloc()

            def f():
                i_ = None
                for tp in range(31):
                    i_ = nc.tensor.matmul(ps[b][:, 0:W], lhsT=Wcd[:, j, tp, :], rhs=ybuf[:, j, tp:tp + W],
                                          start=(tp == 0), stop=(tp == 30))
                return i_

            S.op("pe", f, reads=("Wcd", "ybuf"), writes=(PS(b),))
            cb.append(b)
        return cb

    def conv_tail(l, W, cb=None, hooks=None):
        if cb is None:
            cb = conv_mm(W)
        for j in range(2):
            b = cb[j]
            cbias = sv[:, SV_CB + l * 2 + j:SV_CB + l * 2 + j + 1]
            S.op("act", lambda: nc.scalar.activation(out=xres[:, j, 0:W], in_=ps[b][:, 0:W], func=AF.Identity,
                                                     bias=cbias, scale=1.0),
                 reads=(PS(b), "sv"), writes=(("xres", j),))
            S.op("act", lambda: nc.scalar.activation(out=cnb[:, j, 0:W], in_=ps[b][:, 0:W], func=AF.Identity,
                                                     bias=cbias, scale=1.0),
                 reads=(PS(b), "sv"), writes=(("osq", j),))
            S.op("act", lambda: nc.scalar.activation(out=csq[:, j, 0:W], in_=ps[b][:, 0:W], func=AF.Square,
                                                     bias=cbias, scale=1.0),
                 reads=(PS(b), "sv"), writes=(("osq", 2 + j),))
            ps_free(b)
        bm = ps_alloc()
        bq = ps_alloc()

        def fm():
            i_ = None
            for j in range(2):
                i_ = nc.tensor.matmul(ps[bm][:, 0:W], lhsT=ONES256, rhs=cnb[:, j, 0:W], start=(j == 0), stop=(j == 1))
            return i_

        def fq():
            i_ = None
            for j in range(2):
                i_ = nc.tensor.matmul(ps[bq][:, 0:W], lhsT=ONES256, rhs=csq[:, j, 0:W], start=(j == 0), stop=(j == 1))
            return i_

        S.op("pe", fm, reads=("cbf", ("osq", 0), ("osq", 1)), writes=(PS(bm),))
        S.op("pe", fq, reads=("cbf", ("osq", 2), ("osq", 3)), writes=(PS(bq),))
        mean = tA[:, 0, 0:W]
        S.op("act", lambda: nc.scalar.copy(out=mean, in_=ps[bm][:, 0:W]), reads=(PS(bm),), writes=(("tA", 0),))
        ps_free(bm)
        if hooks:
            hooks["a"]()
        S.op("dve", lambda: nc.vector.scalar_tensor_tensor(out=tA[:, 1, 0:W], in0=mean, scalar=-1.0, in1=mean,
                                                          op0=ALU.mult, op1=ALU.mult),
             reads=(("tA", 0),), writes=(("tA", 1),))
        S.op("dve", lambda: nc.vector.tensor_tensor(out=tA[:, 1, 0:W], in0=ps[bq][:, 0:W], in1=tA[:, 1, 0:W], op=ALU.add),
             reads=(PS(bq), ("tA", 1)), writes=(("tA", 1),))
        ps_free(bq)
        S.op("act", lambda: nc.scalar.activation(out=rstd[:, 0:W], in_=tA[:, 1, 0:W], func=AF.Ln,
                                                 bias=sv[:, SV_EPS:SV_EPS + 1], scale=1.0),
             reads=(("tA", 1), "sv_c"), writes=("rstd",))
        S.op("act", lambda: nc.scalar.activation(out=rstd[:, 0:W], in_=rstd[:, 0:W], func=AF.Exp, scale=-0.5),
             reads=("rstd",), writes=("rstd",))
        if hooks:
            hooks["b"]()
        exb = [(tA[:, 2, :], ("tA", 2)), (rstd[:], "rstd")]
        for j in range(2):
            S.op("dve", lambda: nc.vector.tensor_tensor(out=xres[:, j, 0:W], in0=xres[:, j, 0:W], in1=mean, op=ALU.subtract),
                 reads=(("xres", j), ("tA", 0)), writes=(("xres", j),))
            S.op("dve", lambda: nc.vector.tensor_tensor(out=xres[:, j, 0:W], in0=xres[:, j, 0:W], in1=rstd[:, 0:W],
                                                       op=ALU.mult),
                 reads=(("xres", j), "rstd"), writes=(("xres", j),))
        for j in range(2):
            g_ap = sv[:, SV_CLG + l * 2 + j:SV_CLG + l * 2 + j + 1]
            b_ap = sv[:, SV_CLB + l * 2 + j:SV_CLB + l * 2 + j + 1]
            ng_ap = sv[:, SV_NCLG + l * 2 + j:SV_NCLG + l * 2 + j + 1]
            nb_ap = sv[:, SV_NCLB + l * 2 + j:SV_NCLB + l * 2 + j + 1]
            ex, ekey = exb[j]
            S.op("act", lambda: nc.scalar.activation(out=ex[:, 0:W], in_=xres[:, j, 0:W], func=AF.Exp,
                                                     bias=nb_ap, scale=ng_ap),
                 reads=(("xres", j), "sv_d"), writes=(ekey,))
            S.op("pool", lambda: nc.gpsimd.tensor_scalar(out=xres[:, j, 0:W], in0=xres[:, j, 0:W], scalar1=g_ap,
                                                        scalar2=b_ap, op0=ALU.mult, op1=ALU.add),
                 reads=(("xres", j), "sv", ekey), writes=(("xres", j),))
            S.op("act", lambda: nc.scalar.activation(out=ex[:, 0:W], in_=ex[:, 0:W], func=AF.Ln,
                                                     bias=sv[:, SV_ONE:SV_ONE + 1], scale=1.0),
                 reads=(ekey, "sv_c"), writes=(ekey,))
            S.op("act", lambda: nc.scalar.activation(out=ex[:, 0:W], in_=ex[:, 0:W], func=AF.Exp, scale=-1.0),
                 reads=(ekey,), writes=(ekey,))
        for j in range(2):
            ex, ekey = exb[j]
            S.op("dve", lambda: nc.vector.tensor_tensor(out=cnb[:, j, 0:W], in0=xres[:, j, 0:W], in1=ex[:, 0:W],
                                                       op=ALU.mult),
                 reads=(("xres", j), ekey), writes=(("osq", j),))
        for jo in range(2):
            b = ps_alloc()

            def f():
                i_ = None
                for ji in range(2):
                    i_ = nc.tensor.matmul(ps[b][:, 0:W], lhsT=Wpw[:, ji, jo * 128:(jo + 1) * 128], rhs=cnb[:, ji, 0:W],
                                          start=(ji == 0), stop=(ji == 1))
                return i_

            S.op("pe", f, reads=("Wpw", ("osq", 0), ("osq", 1)), writes=(PS(b),))
            S.op("dve", lambda: nc.vector.tensor_tensor(out=yT[:, jo, 0:W], in0=ps[b][:, 0:W], in1=ag[:, jo, 0:W],
                                                       op=ALU.mult),
                 reads=(PS(b), "ag"), writes=(("yT", jo),))
            ps_free(b)

    xres_slot = [0]

    def out_proj(l, src_d, dst_d, tok0, ncols, col0, part="all", slot=None):
        skip = max(0, -tok0)
        nv = ncols - skip
        if slot is None:
            s_ = xres_slot[0]
            xres_slot[0] ^= 1
        else:
            s_ = slot
        r0 = tok0 + skip
        if part in ("all", "load"):
            S.dma(("xres", s_), lambda: nc.sync.dma_start(out=xres[skip:ncols, s_, :], in_=src_d[r0:r0 + nv, :]),
                  reads=rowres("xrow", r0, nv), writes=(("xres", s_),))
        if part == "load":
            return
        for dh in range(2):
            b = ps_alloc()

            def f():
                i_ = None
                for mc in range(8):
                    i_ = nc.tensor.matmul(ps[b][0:ncols, :], lhsT=yT[:, mc, col0:col0 + ncols],
                                          rhs=Wo[:, mc, dh * 512:(dh + 1) * 512], start=(mc == 0), stop=(mc == 7))
                return i_

            S.op("pe", f, reads=("Wo",) + tuple(("yT", m) for m in range(8)), writes=(PS(b),))
            S.op("dve", lambda: nc.vector.tensor_tensor(out=xres[0:ncols, s_, dh * 512:(dh + 1) * 512],
                                                       in0=ps[b][0:ncols, :], in1=xres[0:ncols, s_, dh * 512:(dh + 1) * 512],
                                                       op=ALU.add),
                 reads=(PS(b), ("xres", s_)), writes=(("xres", s_),))
            ps_free(b)
        if l == 1:
            S.op("act", lambda: nc.scalar.activation(out=xs[0:ncols, :], in_=xres[0:ncols, s_, :], func=AF.Square,
                                                     accum_out=sm[0:ncols, 20:21]),
                 reads=(("xres", s_),), writes=("xs", "fssq"))
            S.op("pool", lambda: nc.gpsimd.tensor_scalar(out=sm[0:ncols, 21:22], in0=sm[0:ncols, 20:21],
                                                        scalar1=1.0 / D, scalar2=EPS, op0=ALU.mult, op1=ALU.add),
                 reads=("fssq",), writes=("fms",))
            S.op("pool", lambda: nc.gpsimd.tensor_tensor(out=sm[0:ncols, 22:23], in0=sm[0:ncols, 21:22],
                                                        in1=sv[0:ncols, SV_M05:SV_M05 + 1], op=ALU.pow),
                 reads=("fms", "sv_c"), writes=("frs",))
            S.op("dve", lambda: nc.vector.scalar_tensor_tensor(out=xres[0:ncols, s_, :], in0=xres[0:ncols, s_, :],
                                                              scalar=sm[0:ncols, 22:23], in1=fng[0:ncols, :],
                                                              op0=ALU.mult, op1=ALU.mult),
                 reads=(("xres", s_), "frs", "fng"), writes=(("xres", s_),))
        key = ("out", s_)
        if key not in out_keys:
            out_keys.append(key)
        wr = rowres("xrow1", r0, nv) if l == 0 else ()
        S.dma(key, lambda: nc.sync.dma_start(out=dst_d[r0:r0 + nv, :], in_=xres[skip:ncols, s_, :]),
              reads=(("xres", s_),), writes=wr)

    def p1_elem(l, h, th, thkey):
        qdst = QK2[:, h % 2, 1 + 2 * (h // 2), :]
        qkey = ("QK", h % 2, 1 + 2 * (h // 2))
        S.op("pool", lambda: nc.gpsimd.tensor_scalar(out=Fb[:, 1:T + 1], in0=th, scalar1=lbc[:, l, 1, 0, h:h + 1],
                                                    scalar2=lbc[:, l, 1, 1, h:h + 1], op0=ALU.mult, op1=ALU.add),
             reads=(thkey, "lbc"), writes=("Fb",))
        S.op("dve", lambda: nc.vector.tensor_tensor_scan(out=Eb[:], data0=Fb[:, 0:T], data1=csm[:], initial=1.0,
                                                        op0=ALU.mult, op1=ALU.max),
             reads=("Fb", "csm"), writes=("Eb",))
        S.op("pool", lambda: nc.gpsimd.tensor_scalar(out=th, in0=Fb[:, 1:T + 1], scalar1=-1.0, scalar2=1.0,
                                                    op0=ALU.add, op1=ALU.mult),
             reads=("Fb",), writes=(thkey,))
        S.op("pool", lambda: nc.gpsimd.tensor_tensor(out=qdst, in0=th, in1=Eb[:], op=ALU.mult),
             reads=(thkey, "Eb"), writes=(qkey,))
        S.op("dve", lambda: nc.vector.tensor_tensor(
            out=DB4[:, h, :], in0=Eb[:].rearrange("p (c t) -> p c t", t=64)[:, :, 63],
            in1=Fb[:, 1:T + 1].rearrange("p (c t) -> p c t", t=64)[:, :, 63], op=ALU.mult),
            reads=("Eb", "Fb"), writes=(("DB4", h),))

    def p1_recur_pair(heads, banks):
        par = {h: 0 for h in heads}
        for c in range(7, -1, -1):
            for h in heads:
                pc, pres = Pchunk(banks[h], c)
                src, dst = par[h], 1 - par[h]
                S.op("dve", lambda: nc.vector.scalar_tensor_tensor(out=Sb2[:, h, dst, :], in0=Sb2[:, h, src, :],
                                                                  scalar=DB4[:, h, c:c + 1], in1=pc, op0=ALU.mult,
                                                                  op1=ALU.subtract),
                     reads=(("Sb", h, src), ("DB4", h), pres), writes=(("Sb", h, dst),))
                par[h] = dst
        for h in heads:
            assert par[h] == 0
            for bb in banks[h]:
                ps_free(bb)

    def pass1_tile(l, src_d, i):
        S.dma("sbst", lambda: nc.sync.dma_start(out=sbnd_d[i].rearrange("p (h v) -> p h v", h=4), in_=Sb2[:, :, 0, :]),
              reads=tuple(("Sb", h, 0) for h in range(4)), writes=(("sbnd", i),))
        if i == NT - 1:
            prep_hT(src_d, i)
        for tg in range(4):
            b = proj_tm(tg, C_I, 512)
            S.op("act", lambda: nc.scalar.copy(out=vtok[:, tg, :], in_=ps[b][:]), reads=(PS(b),), writes=("vtok",))
            ps_free(b)
        thb = [(tA[:, 0, :], ("tA", 0)), (tA[:, 1, :], ("tA", 1)), (tA[:, 2, :], ("tA", 2)), (rstd[:], "rstd")]
        for h in range(4):
            pz = proj_fm(C_FB // 128 + h)
            S.op("act", lambda: nc.scalar.activation(out=thb[h][0], in_=ps[pz][:], func=AF.Tanh, scale=0.5),
                 reads=(PS(pz),), writes=(thb[h][1],))
            ps_free(pz)
        nxt = (lambda tg: prep_hT(src_d, i - 1, (tg,))) if i > 0 else (lambda tg: None)
        banks = {}
        p1_elem(l, 0, *thb[0])
        p1_elem(l, 1, *thb[1])
        banks[0] = hgrn_state_mm(0, 0, 0)
        banks[1] = hgrn_state_mm(1, 0, 1)
        nxt(0)
        p1_elem(l, 2, *thb[2])
        p1_elem(l, 3, *thb[3])
        p1_recur_pair((0, 1), banks)
        nxt(1)
        banks[2] = hgrn_state_mm(2, 1, 0)
        banks[3] = hgrn_state_mm(3, 1, 1)
        nxt(2)
        p1_recur_pair((2, 3), banks)
        nxt(3)

    started = set()
    pending = []

    def e_proj(l, h):
        Pq = proj_fm(C_Q // 128 + h)
        for d_, c0 in ((0, C_FF), (1, C_FB)):
            Pz = proj_fm(c0 // 128 + h)
            S.op("act", lambda: nc.scalar.activation(out=tA[:, d_, :], in_=ps[Pz][:], func=AF.Tanh, scale=0.5),
                 reads=(PS(Pz),), writes=(("tA", d_),))
            ps_free(Pz)
        return Pq

    def e_elem(l, h, Pq):
        hgrn_elem2(l, h, Pq, h % 2)
        ps_free(Pq)

    def tile_start(l, i):
        started.add(i)
        S.dma("sbld", lambda: nc.sync.dma_start(out=Sb2[:, :, 0, :], in_=sbnd_d[i].rearrange("p (h v) -> p h v", h=4)),
              reads=(("sbnd", i),), writes=tuple(("Sb", h, 0) for h in range(4)))
        for tg in range(4):
            b = proj_tm(tg, C_I, 512)
            S.op("act", lambda: nc.scalar.copy(out=vtok[:, tg, :], in_=ps[b][:]), reads=(PS(b),), writes=("vtok",))
            ps_free(b)
        pq = e_proj(l, 0)
        e_elem(l, 0, pq)

    def pass2_tile(l, src_d, dst_d, i):
        t0 = i * T
        if i not in started:
            tile_start(l, i)
        def conv_in(j):
            Pval = proj_fm(C_AVAL // 128 + j)
            Pglu = proj_fm(C_AGLU // 128 + j)
            S.op("act", lambda: nc.scalar.activation(out=tA[:, 2, :], in_=ps[Pglu][:], func=AF.Tanh, scale=0.5),
                 reads=(PS(Pglu),), writes=(("tA", 2),))
            ps_free(Pglu)
            S.op("dve", lambda: nc.vector.scalar_tensor_tensor(out=ybuf[:, j, 31:31 + T], in0=tA[:, 2, :], scalar=1.0,
                                                              in1=ps[Pval][:], op0=ALU.add, op1=ALU.mult),
                 reads=(("tA", 2), PS(Pval)), writes=("ybuf",))
            ps_free(Pval)
            Pg = proj_fm(C_AGATE // 128 + j)
            S.op("act", lambda: nc.scalar.activation(out=tA[:, 2, :], in_=ps[Pg][:], func=AF.Tanh, scale=0.5),
                 reads=(PS(Pg),), writes=(("tA", 2),))
            S.op("dve", lambda: nc.vector.scalar_tensor_tensor(out=ag[:, j, 16:16 + T], in0=tA[:, 2, :], scalar=1.0,
                                                              in1=ps[Pg][:], op0=ALU.add, op1=ALU.mult),
                 reads=(("tA", 2), PS(Pg)), writes=("ag",))
            ps_free(Pg)
        def sgu_in(j):
            Pu = proj_fm(C_BU // 128 + j)
            S.op("act", lambda: nc.scalar.activation(out=ub[:, j, :], in_=ps[Pu][:], func=AF.Gelu),
                 reads=(PS(Pu),), writes=(("ub", j),))
            ps_free(Pu)
            Pb = proj_fm(C_BGATE // 128 + j)
            S.op("act", lambda: nc.scalar.activation(out=tA[:, 2, :], in_=ps[Pb][:], func=AF.Tanh, scale=0.5),
                 reads=(PS(Pb),), writes=(("tA", 2),))
            S.op("dve", lambda: nc.vector.scalar_tensor_tensor(out=bg[:, j, :], in0=tA[:, 2, :], scalar=1.0,
                                                              in1=ps[Pb][:], op0=ALU.add, op1=ALU.mult),
                 reads=(("tA", 2), PS(Pb)), writes=(("bg", j),))
            ps_free(Pb)
            S.op("pool", lambda: nc.gpsimd.tensor_tensor(out=ub[:, j, :], in0=ub[:, j, :], in1=bg[:, j, :], op=ALU.mult),
                 reads=(("ub", j), ("bg", j)), writes=(("ub", j),))
        def sgu_bv(tg):
            Pv = proj_tm(tg, C_BV, 256)
            gsl = 0
            S.op("act", lambda: nc.scalar.activation(out=gvb[:, gsl, :], in_=ps[Pv][:, 0:256], func=AF.Gelu),
                 reads=(PS(Pv),), writes=(("gvb", gsl),))
            ps_free(Pv)
            S.op("dve", lambda: nc.vector.bn_stats(out=st6[:, tg, :], in_=gvb[:, gsl, :]),
                 reads=(("gvb", gsl),), writes=(("st6", tg),))
            S.op("dve", lambda: nc.vector.bn_aggr(out=mv[:, tg, :], in_=st6[:, tg, :]),
                 reads=(("st6", tg),), writes=(("mv", tg),))
            S.op("pool", lambda: nc.gpsimd.tensor_scalar(out=sm[:, 24 + tg:25 + tg], in0=mv[:, tg, 1:2], scalar1=1.0,
                                                        scalar2=EPS, op0=ALU.mult, op1=ALU.add),
                 reads=(("mv", tg),), writes=(("sve", tg),))
            S.op("pool", lambda: nc.gpsimd.tensor_tensor(out=sm[:, 28 + tg:29 + tg], in0=sm[:, 24 + tg:25 + tg],
                                                        in1=sv[:, SV_M05:SV_M05 + 1], op=ALU.pow),
                 reads=(("sve", tg), "sv_c"), writes=(("srs", tg),))
            S.op("dve", lambda: nc.vector.tensor_scalar(out=gvb[:, gsl, :], in0=gvb[:, gsl, :], scalar1=mv[:, tg, 0:1],
                                                       scalar2=sm[:, 28 + tg:29 + tg], op0=ALU.subtract, op1=ALU.mult),
                 reads=(("gvb", gsl), ("mv", tg), ("srs", tg)), writes=(("gvb", gsl),))
            S.op("pool", lambda: nc.gpsimd.tensor_tensor(out=gvb[:, gsl, :], in0=gvb[:, gsl, :], in1=sgg[:, l, :],
                                                        op=ALU.mult),
                 reads=(("gvb", gsl), "sgg"), writes=(("gvb", gsl),))
            S.op("pool", lambda: nc.gpsimd.tensor_tensor(out=vsn[:, tg, :], in0=gvb[:, gsl, :], in1=sgb[:, l, :],
                                                        op=ALU.add),
                 reads=(("gvb", gsl), "sgb"), writes=(("vsn", tg),))
        def sgu_out(g2):
            b = ps_alloc()

            def f():
                i_ = None
                for tg in range(4):
                    for hh in range(2):
                        h = g2 * 2 + hh
                        i_ = nc.tensor.matmul(ps[b][hh * 64:(hh + 1) * 64, tg * 128:(tg + 1) * 128],
                                              lhsT=vsn[:, tg, h * 64:(h + 1) * 64], rhs=Wsg[:, h, :],
                                              start=True, stop=True)
                return i_

            S.op("pe", f, reads=tuple(("vsn", tg) for tg in range(4)) + ("Wsg",), writes=(PS(b),))
            S.op("dve", lambda: nc.vector.tensor_tensor(
                out=tA[:, 2, :].rearrange("p (g t) -> p g t", g=4), in0=ps[b][:].rearrange("p (g t) -> p g t", g=4),
                in1=sbias[:, l, g2, :].unsqueeze(1).to_broadcast([128, 4, 128]), op=ALU.add),
                reads=(PS(b), "sbias"), writes=(("tA", 2),))
            ps_free(b)
            S.op("dve", lambda: nc.vector.scalar_tensor_tensor(out=yT[:, 2 + g2, 16:16 + T], in0=tA[:, 2, :], scalar=0.5,
                                                              in1=ub[:, g2, :], op0=ALU.mult, op1=ALU.mult),
                 reads=(("tA", 2), ("ub", g2)), writes=(("yT", 2 + g2),))
        for h in range(4):
            Pgt = proj_fm(C_GATE // 128 + h)
            S.op("act", lambda: nc.scalar.activation(out=tA[:, 2, :], in_=ps[Pgt][:], func=AF.Tanh, scale=0.5),
                 reads=(PS(Pgt),), writes=(("tA", 2),))
            S.op("dve", lambda: nc.vector.scalar_tensor_tensor(out=gs[:, h, :], in0=tA[:, 2, :], scalar=1.0,
                                                              in1=ps[Pgt][:], op0=ALU.add, op1=ALU.mult),
                 reads=(("tA", 2), PS(Pgt)), writes=(("gs", h),))
            ps_free(Pgt)

        def m_head(h, mid=None):
            sl = h % 2
            QK = QK2[:, sl]
            bf_ = hgrn_state_mm(h, 0, sl)
            bb_ = hgrn_state_mm(h, 1, sl)
            hgrn_recur2(h, bf_, bb_, sl)
            for d_ in range(2):
                b = ps_alloc()

                def f():
                    i_ = None
                    for pr in range(4):
                        i_ = nc.tensor.matmul(ps[b][:, pr * 128:(pr + 1) * 128],
                                              lhsT=QK[:, 1 + 2 * d_, pr * 128:(pr + 1) * 128],
                                              rhs=QK[:, 2 * d_, pr * 128:(pr + 1) * 128], start=True, stop=True)
                    return i_

                S.op("pe", f, reads=(("QK", sl, 2 * d_), ("QK", sl, 1 + 2 * d_)), writes=(PS(b),))
                S.op("dve", lambda: nc.vector.tensor_tensor(
                    out=Am[:, d_, :].rearrange("p (g t) -> p g t", g=4), in0=ps[b][:].rearrange("p (g t) -> p g t", g=4),
                    in1=MASKN[d_].unsqueeze(1).to_broadcast([128, 4, 128]), op=ALU.mult),
                    reads=(PS(b), "cbf"), writes=(("Am", d_),))
                ps_free(b)
            if mid is not None:
                mid()
            b = ps_alloc()

            def fo():
                i_ = None
                for pr in range(4):
                    nc.tensor.matmul(ps[b][:, pr * 128:(pr + 1) * 128], lhsT=vtok[:, pr, h * 128:(h + 1) * 128],
                                     rhs=Am[:, 0, pr * 128:(pr + 1) * 128], start=True, stop=False)
                    nc.tensor.matmul(ps[b][:, pr * 128:(pr + 1) * 128], lhsT=vtok[:, pr, h * 128:(h + 1) * 128],
                                     rhs=Am[:, 1, pr * 128:(pr + 1) * 128], start=False, stop=False)
                    for hf in range(2):
                        c = pr * 2 + hf
                        for d_ in range(2):
                            i_ = nc.tensor.matmul(ps[b][:, c * 64:(c + 1) * 64], lhsT=Sbf[:, d_, c, :],
                                                  rhs=QK[:, 2 * d_, c * 64:(c + 1) * 64], start=False,
                                                  stop=(hf == 1 and d_ == 1))
                return i_

            S.op("pe", fo, reads=("vtok", ("Am", 0), ("Am", 1), ("QK", sl, 0), ("QK", sl, 2)) +
                 tuple(("Sbf", d_, c) for d_ in range(2) for c in range(8)), writes=(PS(b),))
            S.op("act", lambda: nc.scalar.activation(out=osq[:, h, :], in_=ps[b][:], func=AF.Square),
                 reads=(PS(b),), writes=(("osq", h),))
            S.op("act", lambda: nc.scalar.activation(out=big[:, h, :], in_=ps[b][:], func=AF.Copy,
                                                     scale=sv[:, SV_GV + l:SV_GV + l + 1]),
                 reads=(PS(b), "sv"), writes=(("big", h),))
            ps_free(b)
            S.op("pool", lambda: nc.gpsimd.tensor_tensor(out=big[:, h, :], in0=big[:, h, :], in1=gs[:, h, :], op=ALU.mult),
                 reads=(("big", h), ("gs", h)), writes=(("big", h),))

        nxt = (lambda tg: prep_hT(src_d, i + 1, (tg,))) if i + 1 < NT else (lambda tg: None)
        pq1 = e_proj(l, 1)
        conv_in(0)
        e_elem(l, 1, pq1)
        pq2 = e_proj(l, 2)
        conv_in(1)
        prev_out = pending.pop() if pending else None

        def mid0():
            sgu_in(0)
            if prev_out:
                prev_out((0, 1), False)

        def mid1():
            sgu_bv(2)
            if prev_out:
                prev_out((2, 3), True)

        m_head(0, mid=mid0)
        sgu_in(1)
        e_elem(l, 2, pq2)
        pq3 = e_proj(l, 3)
        sgu_bv(0)
        sgu_bv(1)
        m_head(1, mid=mid1)
        sgu_bv(3)
        e_elem(l, 3, pq3)
        nxt(0)
        sgu_out(0)
        m_head(2, mid=lambda: nxt(1))
        sgu_out(1)
        nxt(2)
        m_head(3, mid=lambda: nxt(3))
        rbuf = [(RC[:, 0, :], ("RC", 0)), (Ff[:], "Ff"), (Eb[:], "Eb"), (Fb[:, 1:T + 1], "Fb")]
        mb = []
        for h in range(4):
            b = ps_alloc()
            S.op("pe", lambda: nc.tensor.matmul(ps[b][:], lhsT=ONES128, rhs=osq[:, h, :], start=True, stop=True),
                 reads=("cbf", ("osq", h)), writes=(PS(b),))
            mb.append(b)

        cbanks = conv_mm(T)
        if True:
            for h in range(4):
                rb, rk = rbuf[h]
                S.op("act", lambda: nc.scalar.activation(out=rb, in_=ps[mb[h]][:], func=AF.Ln,
                                                         bias=sv[:, SV_EPS:SV_EPS + 1], scale=1.0),
                     reads=(PS(mb[h]), "sv_c"), writes=(rk,))
                ps_free(mb[h])
            for h in range(4):
                rb, rk = rbuf[h]
                S.op("act", lambda: nc.scalar.activation(out=rb, in_=rb, func=AF.Exp, scale=-0.5),
                     reads=(rk,), writes=(rk,))

        def hook_b():
            for h in range(4):
                rb, rk = rbuf[h]
                if h % 2 == 0:
                    S.op("dve", lambda: nc.vector.tensor_tensor(out=yT[:, 4 + h, 16:16 + T], in0=big[:, h, :], in1=rb,
                                                               op=ALU.mult),
                         reads=(("big", h), rk), writes=(("yT", 4 + h),))
                else:
                    S.op("pool", lambda: nc.gpsimd.tensor_tensor(out=yT[:, 4 + h, 16:16 + T], in0=big[:, h, :], in1=rb,
                                                                op=ALU.mult),
                         reads=(("big", h), rk), writes=(("yT", 4 + h),))

        conv_tail(l, T, cbanks, hooks={"a": (lambda: None), "b": hook_b})
        if i + 1 < NT:
            tile_start(l, i + 1)
        def do_out(wgs, carry):
            assert len(wgs) <= 2
            for wg in wgs:
                out_proj(l, src_d, dst_d, t0 - 16 + wg * 128, 128, wg * 128, part="load", slot=wg % 2)
            for wg in wgs:
                out_proj(l, src_d, dst_d, t0 - 16 + wg * 128, 128, wg * 128, part="rest", slot=wg % 2)
            if carry:
                S.op("pool", lambda: nc.gpsimd.tensor_copy(out=yT[:, 2:8, 0:16], in_=yT[:, 2:8, T:T + 16]),
                     reads=tuple(("yT", m) for m in range(2, 8)), writes=tuple(("yT", m) for m in range(2, 8)))

        S.op("pool", lambda: nc.gpsimd.tensor_copy(out=ybuf[:, :, 0:31], in_=ybuf[:, :, T:T + 31]),
             reads=("ybuf",), writes=("ybuf",))
        S.op("pool", lambda: nc.gpsimd.tensor_copy(out=ag[:, :, 0:16], in_=ag[:, :, T:T + 16]),
             reads=("ag",), writes=("ag",))
        if i + 1 < NT:
            pending.append(do_out)
        else:
            do_out((0, 1), False)
            do_out((2, 3), True)

    def epilogue(l, src_d, dst_d):
        S.op("pool", lambda: nc.gpsimd.memset(ybuf[:, :, 31:64], 0.0), reads=(), writes=("ybuf",))
        conv_tail(l, 16)
        out_proj(l, src_d, dst_d, L - 16, 16, 0)

    def layer(l, src_d, dst_d):
        load_weights(l)
        S.op("dve", lambda: nc.vector.memset(Sb2[:], 0.0), writes=tuple(("Sb", h, p_) for h in range(4) for p_ in range(2)))
        for i in range(NT - 1, -1, -1):
            pass1_tile(l, src_d, i)
        S.op("dve", lambda: nc.vector.memset(Gf2[:], 0.0), writes=tuple(("Gf", h, p_) for h in range(4) for p_ in range(2)))
        S.op("dve", lambda: nc.vector.memset(dcar[:], 1.0), writes=tuple(("dcar", h) for h in range(4)))
        S.op("pool", lambda: nc.gpsimd.memset(ybuf[:], 0.0), writes=("ybuf",))
        S.op("pool", lambda: nc.gpsimd.memset(ag[:], 0.0), writes=("ag",))
        S.op("pool", lambda: nc.gpsimd.memset(yT[:], 0.0), writes=tuple(("yT", m) for m in range(8)))
        prep_hT(src_d, 0)
        started.clear()
        for i in range(NT):
            pass2_tile(l, src_d, dst_d, i)
        epilogue(l, src_d, dst_d)

    class _Stop(Exception):
        pass

    def layer_staged(l, src_d, dst_d):
        if stage < 1:
            raise _Stop()
        load_weights(l)
        if stage < 2:
            raise _Stop()
        S.op("dve", lambda: nc.vector.memset(Sb2[:], 0.0), writes=tuple(("Sb", h, p_) for h in range(4) for p_ in range(2)))
        for i in range(NT - 1, -1, -1):
            pass1_tile(l, src_d, i)
            if stage < 3:
                raise _Stop()
        if stage < 4:
            raise _Stop()
        S.op("dve", lambda: nc.vector.memset(Gf2[:], 0.0), writes=tuple(("Gf", h, p_) for h in range(4) for p_ in range(2)))
        S.op("dve", lambda: nc.vector.memset(dcar[:], 1.0), writes=tuple(("dcar", h) for h in range(4)))
        S.op("pool", lambda: nc.gpsimd.memset(ybuf[:], 0.0), writes=("ybuf",))
        S.op("pool", lambda: nc.gpsimd.memset(ag[:], 0.0), writes=("ag",))
        S.op("pool", lambda: nc.gpsimd.memset(yT[:], 0.0), writes=tuple(("yT", m) for m in range(8)))
        prep_hT(src_d, 0)
        started.clear()
        for i in range(NT):
            pass2_tile(l, src_d, dst_d, i)
            if stage < 5:
                raise _Stop()
        if stage < 6:
            raise _Stop()
        epilogue(l, src_d, dst_d)
        if stage < 7:
            raise _Stop()

    if stage < 99:
        setup()
        try:
            layer_staged(0, x_d, y_d)
        except (_Stop, StopBuild):
            pass
        for e in ("pe", "act", "dve", "pool"):
            if S.cnt[e] > 0:
                nc.sync.wait_ge(S.sems[e], S.cnt[e])
        S.finish([k for k in S.cnt if k not in ("pe", "act", "dve", "pool")])
        es.close()
        build.stats = (S.nops, S.nwaits)
        return nc

    setup()
    layer(0, x_d, x1_d)
    for key_ in list(S.last_w.keys()):
        if isinstance(key_, tuple) and key_[0] == "xrow1":
            S.last_w[("xrow",) + key_[1:]] = S.last_w[key_]
    layer(1, x1_d, y_d)
    S.finish(out_keys)
    es.close()
    build.stats = (S.nops, S.nwaits)
    return nc


def make_consts():
    c = np.zeros((128, 640), np.float32)
    c[:, 0:128] = np.eye(128)
    s = np.arange(128)[:, None]
    t = np.arange(128)[None, :]
    same = (s // 64) == (t // 64)
    c[:, 128:256] = -(same & (s <= t)).astype(np.float32)
    c[:, 256:384] = -(same & (s >= t)).astype(np.float32)
    c[:, 384:512] = 1.0 / 128
    c[:, 512:640] = 1.0 / 256
    csm = np.full((128, 512), 1e-35, np.float32)
    csm[:, ::64] = 1.0
    return c.astype(ml_dtypes.bfloat16), csm


_WNAMES = ["norm_g", "w_in", "conv_w", "conv_b", "conv_ln_g", "conv_ln_b", "conv_pw", "sgu_ln_g", "sgu_ln_b",
           "sgu_w", "sgu_b", "hgrn_lb_fwd", "hgrn_lb_bwd", "hgrn_norm_g", "w_out", "final_norm_g"]


def kernel(x_prompt, x_sample, **w):
    L = SEQ_P
    nc = build(L)
    cbf, csm = make_consts()
    base = {k: np.ascontiguousarray(np.asarray(w[k], dtype=np.float32)) for k in _WNAMES}
    base["cbf"] = cbf
    base["csm"] = csm
    xp = np.asarray(x_prompt, dtype=np.float32)
    xsm = np.asarray(x_sample, dtype=np.float32)
    in_maps = []
    for c in range(8):
        m = dict(base)
        if c < 4:
            m["x"] = np.ascontiguousarray(xp[c])
        else:
            pad = np.zeros((L, D), np.float32)
            pad[:SEQ_S] = xsm[c - 4]
            m["x"] = pad
        in_maps.append(m)
    res = run_bass_kernel_spmd(nc, in_maps, core_ids=list(range(8)))
    yp = np.stack([np.asarray(res.results[c]["y"], dtype=np.float32) for c in range(4)], axis=0)
    ys = np.stack([np.asarray(res.results[c]["y"], dtype=np.float32)[:SEQ_S] for c in range(4, 8)], axis=0)
    return (yp, ys)
```

```python
import numpy as np
import ml_dtypes
from contextlib import ExitStack
import concourse.bass as bass
import concourse.mybir as mybir
from concourse.bass_utils import run_bass_kernel_spmd

F32 = mybir.dt.float32
BF16 = mybir.dt.bfloat16
AF = mybir.ActivationFunctionType
ALU = mybir.AluOpType

D = 1024
DIN = 4096
T = 512
EPS = 1e-6
SEQ_P = 8192
SEQ_S = 4096

C_AVAL, C_AGLU, C_AGATE, C_BU, C_BV, C_BGATE, C_Q, C_I, C_FF, C_FB, C_GATE = (
    0, 256, 512, 768, 1024, 1280, 1536, 2048, 2560, 3072, 3584)


class StopBuild(Exception):
    pass


class Sched:
    def __init__(self, nc, es):
        self.nc = nc
        self.es = es
        self.engs = {"pe": nc.tensor, "act": nc.scalar, "dve": nc.vector, "pool": nc.gpsimd, "sp": nc.sync}
        self.sems = {}
        self.cnt = {}
        for e in ("pe", "act", "dve", "pool"):
            self.sems[e] = es.enter_context(nc.semaphore("s_" + e))
            self.cnt[e] = 0
        self.known = {e: {} for e in self.engs}
        self.last_w = {}
        self.readers = {}
        self.nwaits = 0
        self.nops = 0
        self.limit = 10 ** 9
        self.log = []

    def _deps(self, reads, writes):
        deps = {}

        def add(tok):
            if tok is None:
                return
            k, v = tok
            if deps.get(k, 0) < v:
                deps[k] = v

        for r in reads:
            add(self.last_w.get(r))
        for w in writes:
            add(self.last_w.get(w))
            for k, v in self.readers.get(w, {}).items():
                add((k, v))
        return deps

    def _emit_waits(self, eng, deps):
        kn = self.known[eng]
        for k, v in deps.items():
            if eng == "pe" and k == "pe":
                continue
            if kn.get(k, 0) >= v:
                continue
            self.engs[eng].wait_ge(self.sems[k], v)
            kn[k] = v
            self.nwaits += 1

    def _record(self, tok, reads, writes):
        for w in writes:
            self.last_w[w] = tok
            self.readers[w] = {}
        k, v = tok
        for r in reads:
            d = self.readers.setdefault(r, {})
            if d.get(k, 0) < v:
                d[k] = v

    def op(self, eng, fn, reads=(), writes=()):
        if self.nops >= self.limit:
            raise StopBuild()
        self.log.append((self.nops, eng, reads, writes))
        deps = self._deps(reads, writes)
        self._emit_waits(eng, deps)
        inst = fn()
        self.cnt[eng] += 1
        inst.then_inc(self.sems[eng], 1)
        self._record((eng, self.cnt[eng]), reads, writes)
        self.nops += 1

    def dma(self, key, fn, reads=(), writes=(), queue="sp"):
        if self.nops >= self.limit:
            raise StopBuild()
        self.log.append((self.nops, "dma", reads, writes))
        if key not in self.sems:
            self.sems[key] = self.es.enter_context(self.nc.semaphore("d_" + str(key).replace(" ", "")))
            self.cnt[key] = 0
        deps = self._deps(reads, writes)
        self._emit_waits(queue, deps)
        inst = fn()
        self.cnt[key] += 16
        inst.then_inc(self.sems[key], 16)
        self._record((key, self.cnt[key]), reads, writes)
        self.nops += 1

    def finish(self, keys):
        for k in keys:
            if k in self.cnt and self.cnt[k] > 0:
                self.engs["sp"].wait_ge(self.sems[k], self.cnt[k])


def build(L, debug=False, stage=99, limit=None):
    NT = L // T
    nc = bass.Bass("TRN2", target_bir_lowering=False, dynamic_dma_scratch_size=1024)
    es = ExitStack()
    es.enter_context(nc.allow_non_contiguous_dma(reason="small param vectors"))
    es.enter_context(nc.allow_low_precision(reason="bf16 matmul operands, fp32 accumulation"))

    def din(name, shape, dt=F32):
        return nc.dram_tensor(name, list(shape), dt, kind="ExternalInput").ap()

    x_d = din("x", [L, D])
    norm_g_d = din("norm_g", [2, D])
    w_in_d = din("w_in", [2, D, DIN])
    conv_w_d = din("conv_w", [2, 31, 256])
    conv_b_d = din("conv_b", [2, 256])
    cln_g_d = din("conv_ln_g", [2, 256])
    cln_b_d = din("conv_ln_b", [2, 256])
    conv_pw_d = din("conv_pw", [2, 256, 256])
    sln_g_d = din("sgu_ln_g", [2, 256])
    sln_b_d = din("sgu_ln_b", [2, 256])
    sgu_w_d = din("sgu_w", [2, 4, 128, 128])
    sgu_b_d = din("sgu_b", [2, 4, 128])
    lbf_d = din("hgrn_lb_fwd", [2, 512])
    lbb_d = din("hgrn_lb_bwd", [2, 512])
    hng_d = din("hgrn_norm_g", [2, 128])
    w_out_d = din("w_out", [2, D, D])
    fng_d = din("final_norm_g", [D])
    cbf_d = din("cbf", [128, 640], BF16)
    csm_d = din("csm", [128, 512])
    y_d = nc.dram_tensor("y", [L, D], F32, kind="ExternalOutput").ap()
    x1_d = nc.dram_tensor("x1s", [L, D], F32, kind="Internal").ap()
    sbnd_d = nc.dram_tensor("sbnd", [NT, 128, 512], F32, kind="Internal").ap()
    dbg_outs = {}

    def sb(name, shape, dt):
        return es.enter_context(nc.sbuf_tensor("sb_" + name, list(shape), dt))

    Wi = sb("Wi", [128, 8, DIN], BF16)
    Wo = sb("Wo", [128, 8, D], BF16)
    Wpw = sb("Wpw", [128, 2, 256], BF16)
    Wcd = sb("Wcd", [128, 2, 31, 128], BF16)
    Wsg = sb("Wsg", [128, 4, 128], BF16)
    cbf = sb("cbf", [128, 640], BF16)
    IDB = cbf[:, 0:128]
    MASKN = [cbf[:, 128:256], cbf[:, 256:384]]
    ONES128 = cbf[:, 384:512]
    ONES256 = cbf[:, 512:640]
    csm = sb("csm", [128, 512], F32)
    identF = sb("identF", [128, 128], F32)
    fng = sb("fng", [128, D], F32)
    sgg = sb("sgg", [128, 2, 256], F32)
    sgb = sb("sgb", [128, 2, 256], F32)
    sbias = sb("sbias", [128, 2, 2, 128], F32)
    sv = sb("sv", [128, 64], F32)
    lbc = sb("lbc", [128, 2, 2, 2, 4], F32)
    cw = sb("cw", [128, 2, 31], F32)
    SV_NG, SV_CB, SV_CLG, SV_CLB, SV_NCLG, SV_NCLB, SV_GV, SV_LBF, SV_LBB, SV_EPS, SV_M05, SV_ONE = (
        0, 16, 20, 24, 28, 32, 36, 38, 46, 54, 55, 56)
    xin = sb("xin", [128, 2, D], F32)
    xs = sb("xs", [128, D], BF16)
    hT = sb("hT", [128, 8, T], BF16)
    ybuf = sb("ybuf", [128, 2, 544], BF16)
    ag = sb("ag", [128, 2, 528], BF16)
    tA = sb("tA", [128, 3, T], F32)
    ub = sb("ub", [128, 2, T], BF16)
    bg = sb("bg", [128, 2, T], BF16)
    gvb = sb("gvb", [128, 1, 256], F32)
    vsn = sb("vsn", [128, 4, 256], BF16)
    vtok = sb("vtok", [128, 4, T], BF16)
    gs = sb("gs", [128, 4, T], BF16)
    Ff = sb("Ff", [128, T], F32)
    Fb = sb("Fb", [128, T + 1], F32)
    CPf2 = sb("CPf", [128, 2, T], F32)
    Eb = sb("Eb", [128, T], F32)
    RC = sb("RC", [128, 1, T], F32)
    QK2 = sb("QK", [128, 2, 4, T], BF16)
    kT = sb("kT", [128, 2, T], BF16)
    Am = sb("Am", [128, 2, T], BF16)
    Sbf = sb("Sbf", [128, 2, 8, 128], BF16)
    Gf2 = sb("Gf", [128, 4, 2, 128], F32)
    Sb2 = sb("Sb", [128, 4, 2, 128], F32)
    dcar = sb("dcar", [128, 4], F32)
    DB2 = sb("DB", [128, 2, 8], F32)
    DB4 = sb("DB4", [128, 4, 8], F32)
    big = sb("big", [128, 4, T], F32)
    osq = sb("osq", [128, 4, T], BF16)
    rstd = sb("rstd", [128, T], F32)
    yT = sb("yT", [128, 8, 528], BF16)
    cn = big
    cnb = osq
    csq = osq[:, 2:4]
    xres = sb("xres", [128, 2, D], F32)
    st6 = sb("st6", [128, 4, 6], F32)
    mv = sb("mv", [128, 4, 2], F32)
    sm = sb("sm", [128, 32], F32)

    ps = [es.enter_context(nc.psum_tensor("ps%d" % i, [128, 512], F32)) for i in range(8)]

    S = Sched(nc, es)
    if limit is not None:
        S.limit = limit
    build.S = S
    free_banks = list(range(8))

    def ps_alloc():
        assert free_banks, "out of PSUM banks"
        return free_banks.pop(0)

    def ps_free(b):
        free_banks.append(b)

    def PS(b):
        return ("ps", b)

    out_keys = []

    def rowres(prefix, r0, n):
        res = []
        for g in range(r0 // 128, (r0 + n - 1) // 128 + 1):
            lo0, hi0, end = g * 128, g * 128 + 112, (g + 1) * 128
            if r0 < hi0 and r0 + n > lo0:
                res.append((prefix, g, "lo"))
            if r0 < end and r0 + n > hi0:
                res.append((prefix, g, "hi"))
        return tuple(res)

    def setup():
        loads = []

        def ld(out, in_, res):
            k = ("setup", len(loads))
            S.dma(k, lambda: nc.sync.dma_start(out=out, in_=in_), reads=(), writes=(("setup_r", len(loads)),))
            loads.append((k, res))

        ld(cbf[:], cbf_d[:, :], "cbf")
        ld(csm[:], csm_d[:, :], "csm")
        ld(fng[:], fng_d.partition_broadcast(128), "fng")
        for l in range(2):
            ld(sgg[:, l, :], sln_g_d[l, :].partition_broadcast(128), "sgg")
            ld(sgb[:, l, :], sln_b_d[l, :].partition_broadcast(128), "sgb")
            for h in range(4):
                ld(sbias[(h % 2) * 64:(h % 2) * 64 + 64, l, h // 2, :],
                   sgu_b_d[l, h, :].partition_broadcast(64), "sbias")
        ld(sv[:, SV_NG:SV_NG + 16].rearrange("p (l k) -> p l k", l=2),
           norm_g_d.rearrange("l (k p) -> p l k", p=128), "sv")
        for col, src in ((SV_CB, conv_b_d), (SV_CLG, cln_g_d), (SV_CLB, cln_b_d)):
            ld(sv[:, col:col + 4].rearrange("p (l j) -> p l j", l=2),
               src.rearrange("l (j p) -> p l j", p=128), "sv")
        ld(sv[:, SV_GV:SV_GV + 2], hng_d.rearrange("l p -> p l"), "sv")
        ld(sv[:, SV_LBF:SV_LBF + 8].rearrange("p (l h) -> p l h", l=2),
           lbf_d.rearrange("l (h p) -> p l h", p=128), "sv")
        ld(sv[:, SV_LBB:SV_LBB + 8].rearrange("p (l h) -> p l h", l=2),
           lbb_d.rearrange("l (h p) -> p l h", p=128), "sv")
        for eng in ("pe", "act", "dve", "pool", "sp"):
            for k, _ in loads:
                S.engs[eng].wait_ge(S.sems[k], 16)
                S.known[eng][k] = 16
        S.op("dve", lambda: nc.vector.memset(sv[:, SV_EPS:SV_EPS + 1], EPS), writes=("sv_c",))
        S.op("dve", lambda: nc.vector.memset(sv[:, SV_M05:SV_M05 + 1], -0.5), writes=("sv_c",))
        S.op("dve", lambda: nc.vector.memset(sv[:, SV_ONE:SV_ONE + 1], 1.0), writes=("sv_c",))
        S.op("pool", lambda: nc.gpsimd.memset(Fb[:, 0:1], 1.0), writes=("Fb",))
        for s_ in range(2):
            S.op("pool", lambda: nc.gpsimd.memset(xres[:, s_, :], 0.0), writes=(("xres", s_),))
        S.op("dve", lambda: nc.vector.tensor_copy(out=identF[:], in_=IDB), reads=("cbf",), writes=("identF",))
        S.op("dve", lambda: nc.vector.tensor_scalar(out=sv[:, SV_NCLG:SV_NCLG + 8], in0=sv[:, SV_CLG:SV_CLG + 8],
                                                    scalar1=-1.0, scalar2=None, op0=ALU.mult),
             reads=("sv",), writes=("sv_d",))
        S.op("dve", lambda: nc.vector.tensor_scalar(out=sv[:, SV_GV:SV_GV + 2], in0=sv[:, SV_GV:SV_GV + 2],
                                                    scalar1=0.5, scalar2=None, op0=ALU.mult),
             reads=("sv",), writes=("sv",))
        S.op("dve", lambda: nc.vector.memset(lbc[:, 0], 0.5), writes=("lbc",))
        for d_, col in ((0, SV_LBF), (1, SV_LBB)):
            S.op("dve", lambda: nc.vector.tensor_tensor(out=sm[:, 0:4], in0=sv[:, col + 4:col + 8],
                                                        in1=sv[:, col:col + 4], op=ALU.subtract),
                 reads=("sv",), writes=("sm",))
            S.op("act", lambda: nc.scalar.activation(out=sm[:, 4:8], in_=sm[:, 0:4], func=AF.Tanh, scale=0.5),
                 reads=("sm",), writes=("sm2",))
            S.op("dve", lambda: nc.vector.tensor_scalar(out=lbc[:, 1, d_, 0, :], in0=sm[:, 4:8], scalar1=-0.25,
                                                        scalar2=0.25, op0=ALU.mult, op1=ALU.add),
                 reads=("sm2",), writes=("lbc",))
            S.op("dve", lambda: nc.vector.tensor_scalar(out=lbc[:, 1, d_, 1, :], in0=sm[:, 4:8], scalar1=0.25,
                                                        scalar2=0.75, op0=ALU.mult, op1=ALU.add),
                 reads=("sm2",), writes=("lbc",))

    def load_weights(l):
        stg = [(xin[:, 0, :], ("xin", 0)), (xin[:, 1, :], ("xin", 1)), (xres[:, 0, :], ("xres", 0)),
               (xres[:, 1, :], ("xres", 1))]
        slot = [0]

        def nslot4():
            s_ = slot[0]
            slot[0] = (slot[0] + 1) % 4
            return s_

        def nslot():
            s_ = slot[0] % 2
            slot[0] = (slot[0] + 1) % 4
            return s_

        first = {"act": True, "dve": True, "pool": True}
        n = 0
        for kc in range(8):
            for q in range(4):
                buf, bkey = stg[nslot4()]
                S.dma(bkey, lambda: nc.sync.dma_start(
                    out=buf, in_=w_in_d[l, kc * 128:(kc + 1) * 128, q * 1024:(q + 1) * 1024]), writes=(bkey,))
                eng = "act" if n % 2 == 0 else "dve"
                wr = ("Wi", ("Wi", n)) if first[eng] else (("Wi", n),)
                first[eng] = False
                if eng == "act":
                    S.op("act", lambda: nc.scalar.activation(
                        out=Wi[:, kc, q * 1024:(q + 1) * 1024], in_=buf, func=AF.Copy,
                        scale=sv[:, SV_NG + l * 8 + kc:SV_NG + l * 8 + kc + 1]),
                        reads=(bkey, "sv"), writes=wr)
                else:
                    S.op("dve", lambda: nc.vector.tensor_scalar(
                        out=Wi[:, kc, q * 1024:(q + 1) * 1024], in0=buf,
                        scalar1=sv[:, SV_NG + l * 8 + kc:SV_NG + l * 8 + kc + 1], scalar2=None, op0=ALU.mult),
                        reads=(bkey, "sv"), writes=wr)
                n += 1
        for mc in range(8):
            buf, bkey = stg[nslot4()]
            S.dma(bkey, lambda: nc.sync.dma_start(out=buf, in_=w_out_d[l, mc * 128:(mc + 1) * 128, :]), writes=(bkey,))
            wr = ("Wo", ("Wo", mc)) if first["pool"] else (("Wo", mc),)
            first["pool"] = False
            S.op("pool", lambda: nc.gpsimd.tensor_copy(out=Wo[:, mc, :], in_=buf), reads=(bkey,), writes=wr)
        S._emit_waits("pe", {"act": S.cnt["act"], "dve": S.cnt["dve"], "pool": S.cnt["pool"]})
        slot[0] = 0
        s_ = nslot()
        S.dma(("xin", s_), lambda: nc.sync.dma_start(
            out=xin[:, s_, 0:512].rearrange("p (j c) -> p j c", j=2),
            in_=conv_pw_d[l].rearrange("(j p) c -> p j c", p=128)), writes=(("xin", s_),))
        S.op("dve", lambda: nc.vector.tensor_scalar(
            out=Wpw[:].rearrange("p j c -> p (j c)"), in0=xin[:, s_, 0:512], scalar1=0.5, scalar2=None,
            op0=ALU.mult), reads=(("xin", s_),), writes=("Wpw",))
        s_ = nslot()
        S.dma(("xin", s_), lambda: nc.sync.dma_start(out=xin[0:31, s_, 0:256], in_=conv_w_d[l]), writes=(("xin", s_),))
        b = ps_alloc()
        for j in range(2):
            S.op("pe", lambda: nc.tensor.transpose(out=ps[b][:, j * 32:j * 32 + 31],
                                                   in_=xin[0:31, s_, j * 128:(j + 1) * 128],
                                                   identity=identF[0:31, 0:31]),
                 reads=(("xin", s_), "identF"), writes=(PS(b),))
        S.op("dve", lambda: nc.vector.tensor_copy(
            out=cw[:], in_=ps[b][:, 0:64].rearrange("p (j t) -> p j t", j=2)[:, :, 0:31]),
            reads=(PS(b),), writes=("cw",))
        ps_free(b)
        for j in range(2):
            for tp in range(31):
                S.op("pool", lambda: nc.gpsimd.tensor_scalar(
                    out=Wcd[:, j, tp, :], in0=IDB, scalar1=cw[:, j, tp:tp + 1], scalar2=0.5,
                    op0=ALU.mult, op1=ALU.mult), reads=("cw", "cbf"), writes=("Wcd",))
        s_ = nslot()
        S.dma(("xin", s_), lambda: nc.sync.dma_start(
            out=xin[:, s_, 0:512].rearrange("p (h s) -> p h s", h=4),
            in_=sgu_w_d[l].rearrange("h t s -> t h s")), writes=(("xin", s_),))
        b = ps_alloc()
        for h in range(4):
            S.op("pe", lambda: nc.tensor.transpose(out=ps[b][:, h * 128:(h + 1) * 128],
                                                   in_=xin[:, s_, h * 128:(h + 1) * 128], identity=identF[:]),
                 reads=(("xin", s_), "identF"), writes=(PS(b),))
        S.op("dve", lambda: nc.vector.tensor_copy(out=Wsg[:].rearrange("p h t -> p (h t)"), in_=ps[b][:]),
             reads=(PS(b),), writes=("Wsg",))
        ps_free(b)

    def proj_fm(fc):
        b = ps_alloc()

        def f():
            i = None
            for kc in range(8):
                i = nc.tensor.matmul(ps[b][:], lhsT=Wi[:, kc, fc * 128:(fc + 1) * 128], rhs=hT[:, kc, :],
                                     start=(kc == 0), stop=(kc == 7))
            return i

        S.op("pe", f, reads=("Wi", "hT"), writes=(PS(b),))
        return b

    def proj_tm(tg, col0, ncols):
        b = ps_alloc()

        def f():
            i = None
            for kc in range(8):
                i = nc.tensor.matmul(ps[b][:, 0:ncols], lhsT=hT[:, kc, tg * 128:(tg + 1) * 128],
                                     rhs=Wi[:, kc, col0:col0 + ncols], start=(kc == 0), stop=(kc == 7))
            return i

        S.op("pe", f, reads=("Wi", "hT"), writes=(PS(b),))
        return b

    xin_slot = [0]

    def prep_hT(src_d, i, tgs=(0, 1, 2, 3), part="all"):
        t0 = i * T
        for tg in tgs:
            s_ = tg % 2
            r0 = t0 + tg * 128
            if part in ("all", "load"):
                S.dma(("xin", s_), lambda: nc.sync.dma_start(out=xin[:, s_, :], in_=src_d[r0:r0 + 128, :]),
                      reads=rowres("xrow", r0, 128), writes=(("xin", s_),))
            if part == "load":
                continue
            S.op("act", lambda: nc.scalar.activation(out=xs[:], in_=xin[:, s_, :], func=AF.Square,
                                                     accum_out=sm[:, 8 + tg:9 + tg]),
                 reads=(("xin", s_),), writes=("xs", ("ssq", tg)))
            S.op("pool", lambda: nc.gpsimd.tensor_scalar(out=sm[:, 12 + tg:13 + tg], in0=sm[:, 8 + tg:9 + tg],
                                                        scalar1=1.0 / D, scalar2=EPS, op0=ALU.mult, op1=ALU.add),
                 reads=(("ssq", tg),), writes=(("ms", tg),))
            S.op("pool", lambda: nc.gpsimd.tensor_tensor(out=sm[:, 16 + tg:17 + tg], in0=sm[:, 12 + tg:13 + tg],
                                                        in1=sv[:, SV_M05:SV_M05 + 1], op=ALU.pow),
                 reads=(("ms", tg), "sv_c"), writes=(("rs", tg),))
            S.op("act", lambda: nc.scalar.activation(out=xs[:], in_=xin[:, s_, :], func=AF.Copy,
                                                     scale=sm[:, 16 + tg:17 + tg]),
                 reads=(("xin", s_), ("rs", tg)), writes=("xs",))
            b = ps_alloc()
            pbf = ps[b][:].bitcast(BF16)

            def f():
                i_ = None
                for kc in range(8):
                    i_ = nc.tensor.transpose(out=pbf[:, kc * 128:(kc + 1) * 128], in_=xs[:, kc * 128:(kc + 1) * 128],
                                             identity=IDB)
                return i_

            S.op("pe", f, reads=("xs", "cbf"), writes=(PS(b),))
            eng = "act"
            if eng == "act":
                S.op("act", lambda: nc.scalar.copy(out=hT[:, :, tg * 128:(tg + 1) * 128],
                                                   in_=pbf.rearrange("p (k t) -> p k t", k=8)),
                     reads=(PS(b),), writes=("hT",))
            else:
                S.op("dve", lambda: nc.vector.tensor_copy(out=hT[:, :, tg * 128:(tg + 1) * 128],
                                                          in_=pbf.rearrange("p (k t) -> p k t", k=8)),
                     reads=(PS(b),), writes=("hT",))
            ps_free(b)

    def tanh_gate(bank, dst, slot):
        S.op("act", lambda: nc.scalar.activation(out=tA[:, slot, :], in_=ps[bank][:], func=AF.Tanh, scale=0.5),
             reads=(PS(bank),), writes=(("tA", slot),))
        S.op("dve", lambda: nc.vector.scalar_tensor_tensor(out=dst, in0=tA[:, slot, :], scalar=1.0, in1=ps[bank][:],
                                                          op0=ALU.add, op1=ALU.mult),
             reads=(("tA", slot), PS(bank)), writes=())

    def hgrn_dir_elem(l, h, d_, Pz, Pq, need_q, sl):
        a_ap = lbc[:, l, d_, 0, h:h + 1]
        c_ap = lbc[:, l, d_, 1, h:h + 1]
        QK = QK2[:, sl]
        CPf = CPf2[:, sl]
        DB = DB2[:, sl]
        if Pz is not None:
            S.op("act", lambda: nc.scalar.activation(out=tA[:, d_, :], in_=ps[Pz][:], func=AF.Tanh, scale=0.5),
                 reads=(PS(Pz),), writes=(("tA", d_),))
            ps_free(Pz)
        if d_ == 0:
            S.op("pool", lambda: nc.gpsimd.tensor_scalar(out=Ff[:], in0=tA[:, 0, :], scalar1=a_ap, scalar2=c_ap,
                                                        op0=ALU.mult, op1=ALU.add),
                 reads=(("tA", 0), "lbc"), writes=("Ff",))
            S.op("dve", lambda: nc.vector.tensor_tensor_scan(out=CPf, data0=csm[:], data1=Ff[:], initial=1.0,
                                                            op0=ALU.max, op1=ALU.mult),
                 reads=("Ff", "csm"), writes=(("CPf", sl),))
            if need_q:
                S.op("dve", lambda: nc.vector.tensor_tensor(out=QK[:, 0, :], in0=ps[Pq][:], in1=CPf, op=ALU.mult),
                     reads=(PS(Pq), ("CPf", sl)), writes=(("QK", sl, 0),))
            S.op("dve", lambda: nc.vector.reciprocal(out=RC[:, 0, :], in_=CPf), reads=(("CPf", sl),),
                 writes=(("RC", 0),))
            S.op("pool", lambda: nc.gpsimd.tensor_scalar(out=tA[:, 0, :], in0=Ff[:], scalar1=-1.0, scalar2=1.0,
                                                        op0=ALU.add, op1=ALU.mult),
                 reads=("Ff",), writes=(("tA", 0),))
            S.op("pool", lambda: nc.gpsimd.tensor_tensor(out=QK[:, 1, :], in0=tA[:, 0, :], in1=RC[:, 0, :], op=ALU.mult),
                 reads=(("tA", 0), ("RC", 0)), writes=(("QK", sl, 1),))
        else:
            S.op("pool", lambda: nc.gpsimd.tensor_scalar(out=Fb[:, 1:T + 1], in0=tA[:, 1, :], scalar1=a_ap,
                                                        scalar2=c_ap, op0=ALU.mult, op1=ALU.add),
                 reads=(("tA", 1), "lbc"), writes=("Fb",))
            S.op("dve", lambda: nc.vector.tensor_tensor_scan(out=Eb[:], data0=Fb[:, 0:T], data1=csm[:], initial=1.0,
                                                            op0=ALU.mult, op1=ALU.max),
                 reads=("Fb", "csm"), writes=("Eb",))
            if need_q:
                S.op("dve", lambda: nc.vector.reciprocal(out=RC[:, 0, :], in_=Eb[:]), reads=("Eb",),
                     writes=(("RC", 0),))
                S.op("dve", lambda: nc.vector.tensor_tensor(out=QK[:, 2, :], in0=ps[Pq][:], in1=RC[:, 0, :],
                                                           op=ALU.mult),
                     reads=(PS(Pq), ("RC", 0)), writes=(("QK", sl, 2),))
            S.op("pool", lambda: nc.gpsimd.tensor_scalar(out=tA[:, 1, :], in0=Fb[:, 1:T + 1], scalar1=-1.0, scalar2=1.0,
                                                        op0=ALU.add, op1=ALU.mult),
                 reads=("Fb",), writes=(("tA", 1),))
            S.op("pool", lambda: nc.gpsimd.tensor_tensor(out=QK[:, 3, :], in0=tA[:, 1, :], in1=Eb[:], op=ALU.mult),
                 reads=(("tA", 1), "Eb"), writes=(("QK", sl, 3),))
            S.op("dve", lambda: nc.vector.tensor_tensor(
                out=DB, in0=Eb[:].rearrange("p (c t) -> p c t", t=64)[:, :, 63],
                in1=Fb[:, 1:T + 1].rearrange("p (c t) -> p c t", t=64)[:, :, 63], op=ALU.mult),
                reads=("Eb", "Fb"), writes=(("DB", sl),))

    def hgrn_elem2(l, h, Pq, sl):
        QK = QK2[:, sl]
        CPf = CPf2[:, sl]
        DB = DB2[:, sl]
        for d_, dst in ((0, Ff[:]), (1, Fb[:, 1:T + 1])):
            S.op("pool", lambda: nc.gpsimd.tensor_scalar(out=dst, in0=tA[:, d_, :], scalar1=lbc[:, l, d_, 0, h:h + 1],
                                                        scalar2=lbc[:, l, d_, 1, h:h + 1], op0=ALU.mult, op1=ALU.add),
                 reads=(("tA", d_), "lbc"), writes=("Ff" if d_ == 0 else "Fb",))
        S.op("dve", lambda: nc.vector.tensor_tensor_scan(out=CPf, data0=csm[:], data1=Ff[:], initial=1.0,
                                                        op0=ALU.max, op1=ALU.mult),
             reads=("Ff", "csm"), writes=(("CPf", sl),))
        S.op("dve", lambda: nc.vector.tensor_tensor_scan(out=Eb[:], data0=Fb[:, 0:T], data1=csm[:], initial=1.0,
                                                        op0=ALU.mult, op1=ALU.max),
             reads=("Fb", "csm"), writes=("Eb",))
        S.op("dve", lambda: nc.vector.tensor_tensor(out=QK[:, 0, :], in0=ps[Pq][:], in1=CPf, op=ALU.mult),
             reads=(PS(Pq), ("CPf", sl)), writes=(("QK", sl, 0),))
        S.op("dve", lambda: nc.vector.reciprocal(out=RC[:, 0, :], in_=CPf), reads=(("CPf", sl),), writes=(("RC", 0),))
        S.op("pool", lambda: nc.gpsimd.tensor_scalar(out=tA[:, 1, :], in0=Fb[:, 1:T + 1], scalar1=-1.0, scalar2=1.0,
                                                    op0=ALU.add, op1=ALU.mult),
             reads=("Fb",), writes=(("tA", 1),))
        S.op("pool", lambda: nc.gpsimd.tensor_tensor(out=QK[:, 3, :], in0=tA[:, 1, :], in1=Eb[:], op=ALU.mult),
             reads=(("tA", 1), "Eb"), writes=(("QK", sl, 3),))
        S.op("dve", lambda: nc.vector.tensor_tensor(
            out=DB, in0=Eb[:].rearrange("p (c t) -> p c t", t=64)[:, :, 63],
            in1=Fb[:, 1:T + 1].rearrange("p (c t) -> p c t", t=64)[:, :, 63], op=ALU.mult),
            reads=("Eb", "Fb"), writes=(("DB", sl),))
        S.op("pool", lambda: nc.gpsimd.tensor_scalar(out=tA[:, 0, :], in0=Ff[:], scalar1=-1.0, scalar2=1.0,
                                                    op0=ALU.add, op1=ALU.mult),
             reads=("Ff",), writes=(("tA", 0),))
        S.op("pool", lambda: nc.gpsimd.tensor_tensor(out=QK[:, 1, :], in0=tA[:, 0, :], in1=RC[:, 0, :], op=ALU.mult),
             reads=(("tA", 0), ("RC", 0)), writes=(("QK", sl, 1),))
        S.op("dve", lambda: nc.vector.reciprocal(out=Eb[:], in_=Eb[:]), reads=("Eb",), writes=("Eb",))
        S.op("dve", lambda: nc.vector.tensor_tensor(out=QK[:, 2, :], in0=ps[Pq][:], in1=Eb[:], op=ALU.mult),
             reads=(PS(Pq), "Eb"), writes=(("QK", sl, 2),))

    def hgrn_state_mm(h, d_, sl):
        b = ps_alloc()
        pbf = ps[b][:].bitcast(BF16)

        def f():
            i_ = None
            for pr in range(4):
                i_ = nc.tensor.transpose(out=pbf[:, pr * 128:(pr + 1) * 128],
                                         in_=QK2[:, sl, 1 + 2 * d_, pr * 128:(pr + 1) * 128], identity=IDB)
            return i_

        S.op("pe", f, reads=(("QK", sl, 1 + 2 * d_), "cbf"), writes=(PS(b),))
        S.op("act", lambda: nc.scalar.copy(out=kT[:, d_, :], in_=pbf[:, 0:T]), reads=(PS(b),), writes=(("kT", d_),))
        ps_free(b)
        banks = []
        for half in range(2):
            bb = ps_alloc()

            def g():
                i_ = None
                for cc in range(4):
                    c = cc * 2 + half
                    pr, hf = c // 2, c % 2
                    i_ = nc.tensor.matmul(ps[bb][:, cc * 128:(cc + 1) * 128],
                                          lhsT=kT[hf * 64:(hf + 1) * 64, d_, pr * 128:(pr + 1) * 128],
                                          rhs=vtok[hf * 64:(hf + 1) * 64, pr, h * 128:(h + 1) * 128],
                                          start=True, stop=True)
                return i_

            S.op("pe", g, reads=(("kT", d_), "vtok"), writes=(PS(bb),))
            banks.append(bb)
        return banks

    def Pchunk(banks, c):
        return ps[banks[c % 2]][:, (c // 2) * 128:(c // 2 + 1) * 128], PS(banks[c % 2])

    def hgrn_bwd_recur(h, banks, snapshots, sl):
        DB = DB2[:, sl]
        par = 0
        for c in range(7, -1, -1):
            pc, pres = Pchunk(banks, c)
            src, dst = par, 1 - par
            if snapshots:
                S.op("pool", lambda: nc.gpsimd.tensor_scalar(out=Sbf[:, 1, c, :], in0=Sb2[:, h, src, :],
                                                            scalar1=DB[:, c:c + 1], scalar2=1.0, op0=ALU.mult,
                                                            op1=ALU.mult),
                     reads=(("Sb", h, src), ("DB", sl)), writes=(("Sbf", 1, c),))
            S.op("dve", lambda: nc.vector.scalar_tensor_tensor(out=Sb2[:, h, dst, :], in0=Sb2[:, h, src, :],
                                                              scalar=DB[:, c:c + 1], in1=pc, op0=ALU.mult,
                                                              op1=ALU.subtract),
                 reads=(("Sb", h, src), ("DB", sl), pres), writes=(("Sb", h, dst),))
            par = dst
        assert par == 0
        for bb in banks:
            ps_free(bb)

    def hgrn_fwd_recur(h, banks, sl):
        CPf = CPf2[:, sl]
        S.op("pool", lambda: nc.gpsimd.tensor_scalar(out=Sbf[:, 0, 0, :], in0=Gf2[:, h, 0, :], scalar1=dcar[:, h:h + 1],
                                                    scalar2=1.0, op0=ALU.mult, op1=ALU.mult),
             reads=(("Gf", h, 0), ("dcar", h)), writes=(("Sbf", 0, 0),))
        par = 0
        for c in range(8):
            pc, pres = Pchunk(banks, c)
            src, dst = par, 1 - par
            dprev = dcar[:, h:h + 1] if c == 0 else CPf[:, c * 64 - 1:c * 64]
            S.op("dve", lambda: nc.vector.scalar_tensor_tensor(out=Gf2[:, h, dst, :], in0=Gf2[:, h, src, :], scalar=dprev,
                                                              in1=pc, op0=ALU.mult, op1=ALU.subtract),
                 reads=(("Gf", h, src), ("dcar", h), ("CPf", sl), pres), writes=(("Gf", h, dst),))
            par = dst
            if c < 7:
                S.op("pool", lambda: nc.gpsimd.tensor_scalar(out=Sbf[:, 0, c + 1, :], in0=Gf2[:, h, dst, :],
                                                            scalar1=CPf[:, c * 64 + 63:c * 64 + 64], scalar2=1.0,
                                                            op0=ALU.mult, op1=ALU.mult),
                     reads=(("Gf", h, dst), ("CPf", sl)), writes=(("Sbf", 0, c + 1),))
        assert par == 0
        S.op("dve", lambda: nc.vector.tensor_copy(out=dcar[:, h:h + 1], in_=CPf[:, T - 1:T]),
             reads=(("CPf", sl),), writes=(("dcar", h),))
        for bb in banks:
            ps_free(bb)

    def hgrn_recur2(h, bf_, bb_, sl):
        CPf = CPf2[:, sl]
        DB = DB2[:, sl]
        S.op("pool", lambda: nc.gpsimd.tensor_scalar(out=Sbf[:, 0, 0, :], in0=Gf2[:, h, 0, :], scalar1=dcar[:, h:h + 1],
                                                    scalar2=1.0, op0=ALU.mult, op1=ALU.mult),
             reads=(("Gf", h, 0), ("dcar", h)), writes=(("Sbf", 0, 0),))
        pf = pb = 0
        for k in range(8):
            c = k
            pc, pres = Pchunk(bf_, c)
            src, dst = pf, 1 - pf
            dprev = dcar[:, h:h + 1] if c == 0 else CPf[:, c * 64 - 1:c * 64]
            S.op("dve", lambda: nc.vector.scalar_tensor_tensor(out=Gf2[:, h, dst, :], in0=Gf2[:, h, src, :], scalar=dprev,
                                                              in1=pc, op0=ALU.mult, op1=ALU.subtract),
                 reads=(("Gf", h, src), ("dcar", h), ("CPf", sl), pres), writes=(("Gf", h, dst),))
            pf = dst
            if c < 7:
                S.op("pool", lambda: nc.gpsimd.tensor_scalar(out=Sbf[:, 0, c + 1, :], in0=Gf2[:, h, dst, :],
                                                            scalar1=CPf[:, c * 64 + 63:c * 64 + 64], scalar2=1.0,
                                                            op0=ALU.mult, op1=ALU.mult),
                     reads=(("Gf", h, dst), ("CPf", sl)), writes=(("Sbf", 0, c + 1),))
            c = 7 - k
            pc, pres = Pchunk(bb_, c)
            src, dst = pb, 1 - pb
            S.op("pool", lambda: nc.gpsimd.tensor_scalar(out=Sbf[:, 1, c, :], in0=Sb2[:, h, src, :],
                                                        scalar1=DB[:, c:c + 1], scalar2=1.0, op0=ALU.mult, op1=ALU.mult),
                 reads=(("Sb", h, src), ("DB", sl)), writes=(("Sbf", 1, c),))
            S.op("dve", lambda: nc.vector.scalar_tensor_tensor(out=Sb2[:, h, dst, :], in0=Sb2[:, h, src, :],
                                                              scalar=DB[:, c:c + 1], in1=pc, op0=ALU.mult,
                                                              op1=ALU.subtract),
                 reads=(("Sb", h, src), ("DB", sl), pres), writes=(("Sb", h, dst),))
            pb = dst
        assert pf == 0 and pb == 0
        S.op("dve", lambda: nc.vector.tensor_copy(out=dcar[:, h:h + 1], in_=CPf[:, T - 1:T]),
             reads=(("CPf", sl),), writes=(("dcar", h),))
        for bb in bf_ + bb_:
            ps_free(bb)

    def conv_mm(W):
        cb = []
        for j in range(2):
            b = ps_alloc()

            def f():
                i_ = None
                for tp in range(31):
                    i_ = nc.tensor.matmul(ps[b][:, 0:W], lhsT=Wcd[:, j, tp, :], rhs=ybuf[:, j, tp:tp + W],
                                          start=(tp == 0), stop=(tp == 30))
                return i_

            S.op("pe", f, reads=("Wcd", "ybuf"), writes=(PS(b),))
            cb.append(b)
        return cb

    def conv_tail(l, W, cb=None, hooks=None):
        if cb is None:
            cb = conv_mm(W)
        for j in range(2):
            b = cb[j]
            cbias = sv[:, SV_CB + l * 2 + j:SV_CB + l * 2 + j + 1]
            S.op("act", lambda: nc.scalar.activation(out=xres[:, j, 0:W], in_=ps[b][:, 0:W], func=AF.Identity,
                                                     bias=cbias, scale=1.0),
                 reads=(PS(b), "sv"), writes=(("xres", j),))
            S.op("act", lambda: nc.scalar.activation(out=cnb[:, j, 0:W], in_=ps[b][:, 0:W], func=AF.Identity,
                                                     bias=cbias, scale=1.0),
                 reads=(PS(b), "sv"), writes=(("osq", j),))
            S.op("act", lambda: nc.scalar.activation(out=csq[:, j, 0:W], in_=ps[b][:, 0:W], func=AF.Square,
                                                     bias=cbias, scale=1.0),
                 reads=(PS(b), "sv"), writes=(("osq", 2 + j),))
            ps_free(b)
        bm = ps_alloc()
        bq = ps_alloc()

        def fm():
            i_ = None
            for j in range(2):
                i_ = nc.tensor.matmul(ps[bm][:, 0:W], lhsT=ONES256, rhs=cnb[:, j, 0:W], start=(j == 0), stop=(j == 1))
            return i_

        def fq():
            i_ = None
            for j in range(2):
                i_ = nc.tensor.matmul(ps[bq][:, 0:W], lhsT=ONES256, rhs=csq[:, j, 0:W], start=(j == 0), stop=(j == 1))
            return i_

        S.op("pe", fm, reads=("cbf", ("osq", 0), ("osq", 1)), writes=(PS(bm),))
        S.op("pe", fq, reads=("cbf", ("osq", 2), ("osq", 3)), writes=(PS(bq),))
        mean = tA[:, 0, 0:W]
        S.op("act", lambda: nc.scalar.copy(out=mean, in_=ps[bm][:, 0:W]), reads=(PS(bm),), writes=(("tA", 0),))
        ps_free(bm)
        if hooks:
            hooks["a"]()
        S.op("dve", lambda: nc.vector.scalar_tensor_tensor(out=tA[:, 1, 0:W], in0=mean, scalar=-1.0, in1=mean,
                                                          op0=ALU.mult, op1=ALU.mult),
             reads=(("tA", 0),), writes=(("tA", 1),))
        S.op("dve", lambda: nc.vector.tensor_tensor(out=tA[:, 1, 0:W], in0=ps[bq][:, 0:W], in1=tA[:, 1, 0:W], op=ALU.add),
             reads=(PS(bq), ("tA", 1)), writes=(("tA", 1),))
        ps_free(bq)
        S.op("act", lambda: nc.scalar.activation(out=rstd[:, 0:W], in_=tA[:, 1, 0:W], func=AF.Ln,
                                                 bias=sv[:, SV_EPS:SV_EPS + 1], scale=1.0),
             reads=(("tA", 1), "sv_c"), writes=("rstd",))
        S.op("act", lambda: nc.scalar.activation(out=rstd[:, 0:W], in_=rstd[:, 0:W], func=AF.Exp, scale=-0.5),
             reads=("rstd",), writes=("rstd",))
        if hooks:
            hooks["b"]()
        exb = [(tA[:, 2, :], ("tA", 2)), (rstd[:], "rstd")]
        for j in range(2):
            S.op("dve", lambda: nc.vector.tensor_tensor(out=xres[:, j, 0:W], in0=xres[:, j, 0:W], in1=mean, op=ALU.subtract),
                 reads=(("xres", j), ("tA", 0)), writes=(("xres", j),))
            S.op("dve", lambda: nc.vector.tensor_tensor(out=xres[:, j, 0:W], in0=xres[:, j, 0:W], in1=rstd[:, 0:W],
                                                       op=ALU.mult),
                 reads=(("xres", j), "rstd"), writes=(("xres", j),))
        for j in range(2):
            g_ap = sv[:, SV_CLG + l * 2 + j:SV_CLG + l * 2 + j + 1]
            b_ap = sv[:, SV_CLB + l * 2 + j:SV_CLB + l * 2 + j + 1]
            ng_ap = sv[:, SV_NCLG + l * 2 + j:SV_NCLG + l * 2 + j + 1]
            nb_ap = sv[:, SV_NCLB + l * 2 + j:SV_NCLB + l * 2 + j + 1]
            ex, ekey = exb[j]
            S.op("act", lambda: nc.scalar.activation(out=ex[:, 0:W], in_=xres[:, j, 0:W], func=AF.Exp,
                                                     bias=nb_ap, scale=ng_ap),
                 reads=(("xres", j), "sv_d"), writes=(ekey,))
            S.op("pool", lambda: nc.gpsimd.tensor_scalar(out=xres[:, j, 0:W], in0=xres[:, j, 0:W], scalar1=g_ap,
                                                        scalar2=b_ap, op0=ALU.mult, op1=ALU.add),
                 reads=(("xres", j), "sv", ekey), writes=(("xres", j),))
            S.op("act", lambda: nc.scalar.activation(out=ex[:, 0:W], in_=ex[:, 0:W], func=AF.Ln,
                                                     bias=sv[:, SV_ONE:SV_ONE + 1], scale=1.0),
                 reads=(ekey, "sv_c"), writes=(ekey,))
            S.op("act", lambda: nc.scalar.activation(out=ex[:, 0:W], in_=ex[:, 0:W], func=AF.Exp, scale=-1.0),
                 reads=(ekey,), writes=(ekey,))
        for j in range(2):
            ex, ekey = exb[j]
            S.op("dve", lambda: nc.vector.tensor_tensor(out=cnb[:, j, 0:W], in0=xres[:, j, 0:W], in1=ex[:, 0:W],
                                                       op=ALU.mult),
                 reads=(("xres", j), ekey), writes=(("osq", j),))
        for jo in range(2):
            b = ps_alloc()

            def f():
                i_ = None
                for ji in range(2):
                    i_ = nc.tensor.matmul(ps[b][:, 0:W], lhsT=Wpw[:, ji, jo * 128:(jo + 1) * 128], rhs=cnb[:, ji, 0:W],
                                          start=(ji == 0), stop=(ji == 1))
                return i_

            S.op("pe", f, reads=("Wpw", ("osq", 0), ("osq", 1)), writes=(PS(b),))
            S.op("dve", lambda: nc.vector.tensor_tensor(out=yT[:, jo, 0:W], in0=ps[b][:, 0:W], in1=ag[:, jo, 0:W],
                                                       op=ALU.mult),
                 reads=(PS(b), "ag"), writes=(("yT", jo),))
            ps_free(b)

    xres_slot = [0]

    def out_proj(l, src_d, dst_d, tok0, ncols, col0, part="all", slot=None):
        skip = max(0, -tok0)
        nv = ncols - skip
        if slot is None:
            s_ = xres_slot[0]
            xres_slot[0] ^= 1
        else:
            s_ = slot
        r0 = tok0 + skip
        if part in ("all", "load"):
            S.dma(("xres", s_), lambda: nc.sync.dma_start(out=xres[skip:ncols, s_, :], in_=src_d[r0:r0 + nv, :]),
                  reads=rowres("xrow", r0, nv), writes=(("xres", s_),))
        if part == "load":
            return
        for dh in range(2):
            b = ps_alloc()

            def f():
                i_ = None
                for mc in range(8):
                    i_ = nc.tensor.matmul(ps[b][0:ncols, :], lhsT=yT[:, mc, col0:col0 + ncols],
                                          rhs=Wo[:, mc, dh * 512:(dh + 1) * 512], start=(mc == 0), stop=(mc == 7))
                return i_

            S.op("pe", f, reads=("Wo",) + tuple(("yT", m) for m in range(8)), writes=(PS(b),))
            S.op("dve", lambda: nc.vector.tensor_tensor(out=xres[0:ncols, s_, dh * 512:(dh + 1) * 512],
                                                       in0=ps[b][0:ncols, :], in1=xres[0:ncols, s_, dh * 512:(dh + 1) * 512],
                                                       op=ALU.add),
                 reads=(PS(b), ("xres", s_)), writes=(("xres", s_),))
            ps_free(b)
        if l == 1:
            S.op("act", lambda: nc.scalar.activation(out=xs[0:ncols, :], in_=xres[0:ncols, s_, :], func=AF.Square,
                                                     accum_out=sm[0:ncols, 20:21]),
                 reads=(("xres", s_),), writes=("xs", "fssq"))
            S.op("pool", lambda: nc.gpsimd.tensor_scalar(out=sm[0:ncols, 21:22], in0=sm[0:ncols, 20:21],
                                                        scalar1=1.0 / D, scalar2=EPS, op0=ALU.mult, op1=ALU.add),
                 reads=("fssq",), writes=("fms",))
            S.op("pool", lambda: nc.gpsimd.tensor_tensor(out=sm[0:ncols, 22:23], in0=sm[0:ncols, 21:22],
                                                        in1=sv[0:ncols, SV_M05:SV_M05 + 1], op=ALU.pow),
                 reads=("fms", "sv_c"), writes=("frs",))
            S.op("dve", lambda: nc.vector.scalar_tensor_tensor(out=xres[0:ncols, s_, :], in0=xres[0:ncols, s_, :],
                                                              scalar=sm[0:ncols, 22:23], in1=fng[0:ncols, :],
                                                              op0=ALU.mult, op1=ALU.mult),
                 reads=(("xres", s_), "frs", "fng"), writes=(("xres", s_),))
        key = ("out", s_)
        if key not in out_keys:
            out_keys.append(key)
        wr = rowres("xrow1", r0, nv) if l == 0 else ()
        S.dma(key, lambda: nc.sync.dma_start(out=dst_d[r0:r0 + nv, :], in_=xres[skip:ncols, s_, :]),
              reads=(("xres", s_),), writes=wr)

    def p1_elem(l, h, th, thkey):
        qdst = QK2[:, h % 2, 1 + 2 * (h // 2), :]
        qkey = ("QK", h % 2, 1 + 2 * (h // 2))
        S.op("pool", lambda: nc.gpsimd.tensor_scalar(out=Fb[:, 1:T + 1], in0=th, scalar1=lbc[:, l, 1, 0, h:h + 1],
                                                    scalar2=lbc[:, l, 1, 1, h:h + 1], op0=ALU.mult, op1=ALU.add),
             reads=(thkey, "lbc"), writes=("Fb",))
        S.op("dve", lambda: nc.vector.tensor_tensor_scan(out=Eb[:], data0=Fb[:, 0:T], data1=csm[:], initial=1.0,
                                                        op0=ALU.mult, op1=ALU.max),
             reads=("Fb", "csm"), writes=("Eb",))
        S.op("pool", lambda: nc.gpsimd.tensor_scalar(out=th, in0=Fb[:, 1:T + 1], scalar1=-1.0, scalar2=1.0,
                                                    op0=ALU.add, op1=ALU.mult),
             reads=("Fb",), writes=(thkey,))
        S.op("pool", lambda: nc.gpsimd.tensor_tensor(out=qdst, in0=th, in1=Eb[:], op=ALU.mult),
             reads=(thkey, "Eb"), writes=(qkey,))
        S.op("dve", lambda: nc.vector.tensor_tensor(
            out=DB4[:, h, :], in0=Eb[:].rearrange("p (c t) -> p c t", t=64)[:, :, 63],
            in1=Fb[:, 1:T + 1].rearrange("p (c t) -> p c t", t=64)[:, :, 63], op=ALU.mult),
            reads=("Eb", "Fb"), writes=(("DB4", h),))

    def p1_recur_pair(heads, banks):
        par = {h: 0 for h in heads}
        for c in range(7, -1, -1):
            for h in heads:
                pc, pres = Pchunk(banks[h], c)
                src, dst = par[h], 1 - par[h]
                S.op("dve", lambda: nc.vector.scalar_tensor_tensor(out=Sb2[:, h, dst, :], in0=Sb2[:, h, src, :],
                                                                  scalar=DB4[:, h, c:c + 1], in1=pc, op0=ALU.mult,
                                                                  op1=ALU.subtract),
                     reads=(("Sb", h, src), ("DB4", h), pres), writes=(("Sb", h, dst),))
                par[h] = dst
        for h in heads:
            assert par[h] == 0
            for bb in banks[h]:
                ps_free(bb)

    def pass1_tile(l, src_d, i):
        S.dma("sbst", lambda: nc.sync.dma_start(out=sbnd_d[i].rearrange("p (h v) -> p h v", h=4), in_=Sb2[:, :, 0, :]),
              reads=tuple(("Sb", h, 0) for h in range(4)), writes=(("sbnd", i),))
        if i == NT - 1:
            prep_hT(src_d, i)
        for tg in range(4):
            b = proj_tm(tg, C_I, 512)
            S.op("act", lambda: nc.scalar.copy(out=vtok[:, tg, :], in_=ps[b][:]), reads=(PS(b),), writes=("vtok",))
            ps_free(b)
        thb = [(tA[:, 0, :], ("tA", 0)), (tA[:, 1, :], ("tA", 1)), (tA[:, 2, :], ("tA", 2)), (rstd[:], "rstd")]
        for h in range(4):
            pz = proj_fm(C_FB // 128 + h)
            S.op("act", lambda: nc.scalar.activation(out=thb[h][0], in_=ps[pz][:], func=AF.Tanh, scale=0.5),
                 reads=(PS(pz),), writes=(thb[h][1],))
            ps_free(pz)
        nxt = (lambda tg: prep_hT(src_d, i - 1, (tg,), part="rest")) if i > 0 else (lambda tg: None)
        nxl = (lambda tg: prep_hT(src_d, i - 1, (tg,), part="load")) if i > 0 else (lambda tg: None)
        banks = {}
        p1_elem(l, 0, *thb[0])
        p1_elem(l, 1, *thb[1])
        banks[0] = hgrn_state_mm(0, 0, 0)
        banks[1] = hgrn_state_mm(1, 0, 1)
        nxl(0)
        nxl(1)
        nxt(0)
        nxl(2)
        p1_elem(l, 2, *thb[2])
        p1_elem(l, 3, *thb[3])
        p1_recur_pair((0, 1), banks)
        nxt(1)
        nxl(3)
        banks[2] = hgrn_state_mm(2, 1, 0)
        banks[3] = hgrn_state_mm(3, 1, 1)
        nxt(2)
        p1_recur_pair((2, 3), banks)
        nxt(3)

    started = set()
    pending = []

    def e_proj(l, h):
        Pq = proj_fm(C_Q // 128 + h)
        for d_, c0 in ((0, C_FF), (1, C_FB)):
            Pz = proj_fm(c0 // 128 + h)
            S.op("act", lambda: nc.scalar.activation(out=tA[:, d_, :], in_=ps[Pz][:], func=AF.Tanh, scale=0.5),
                 reads=(PS(Pz),), writes=(("tA", d_),))
            ps_free(Pz)
        return Pq

    def e_elem(l, h, Pq):
        hgrn_elem2(l, h, Pq, h % 2)
        ps_free(Pq)

    def tile_start(l, i):
        started.add(i)
        S.dma("sbld", lambda: nc.sync.dma_start(out=Sb2[:, :, 0, :], in_=sbnd_d[i].rearrange("p (h v) -> p h v", h=4)),
              reads=(("sbnd", i),), writes=tuple(("Sb", h, 0) for h in range(4)))
        for tg in range(4):
            b = proj_tm(tg, C_I, 512)
            S.op("act", lambda: nc.scalar.copy(out=vtok[:, tg, :], in_=ps[b][:]), reads=(PS(b),), writes=("vtok",))
            ps_free(b)
        pq = e_proj(l, 0)
        e_elem(l, 0, pq)

    def pass2_tile(l, src_d, dst_d, i):
        t0 = i * T
        if i not in started:
            tile_start(l, i)
        def conv_in(j):
            Pval = proj_fm(C_AVAL // 128 + j)
            Pglu = proj_fm(C_AGLU // 128 + j)
            S.op("act", lambda: nc.scalar.activation(out=tA[:, 2, :], in_=ps[Pglu][:], func=AF.Tanh, scale=0.5),
                 reads=(PS(Pglu),), writes=(("tA", 2),))
            ps_free(Pglu)
            S.op("dve", lambda: nc.vector.scalar_tensor_tensor(out=ybuf[:, j, 31:31 + T], in0=tA[:, 2, :], scalar=1.0,
                                                              in1=ps[Pval][:], op0=ALU.add, op1=ALU.mult),
                 reads=(("tA", 2), PS(Pval)), writes=("ybuf",))
            ps_free(Pval)
            Pg = proj_fm(C_AGATE // 128 + j)
            S.op("act", lambda: nc.scalar.activation(out=tA[:, 2, :], in_=ps[Pg][:], func=AF.Tanh, scale=0.5),
                 reads=(PS(Pg),), writes=(("tA", 2),))
            S.op("dve", lambda: nc.vector.scalar_tensor_tensor(out=ag[:, j, 16:16 + T], in0=tA[:, 2, :], scalar=1.0,
                                                              in1=ps[Pg][:], op0=ALU.add, op1=ALU.mult),
                 reads=(("tA", 2), PS(Pg)), writes=("ag",))
            ps_free(Pg)
        def sgu_in(j):
            Pu = proj_fm(C_BU // 128 + j)
            S.op("act", lambda: nc.scalar.activation(out=ub[:, j, :], in_=ps[Pu][:], func=AF.Gelu),
                 reads=(PS(Pu),), writes=(("ub", j),))
            ps_free(Pu)
            Pb = proj_fm(C_BGATE // 128 + j)
            S.op("act", lambda: nc.scalar.activation(out=tA[:, 2, :], in_=ps[Pb][:], func=AF.Tanh, scale=0.5),
                 reads=(PS(Pb),), writes=(("tA", 2),))
            S.op("dve", lambda: nc.vector.scalar_tensor_tensor(out=bg[:, j, :], in0=tA[:, 2, :], scalar=1.0,
                                                              in1=ps[Pb][:], op0=ALU.add, op1=ALU.mult),
                 reads=(("tA", 2), PS(Pb)), writes=(("bg", j),))
            ps_free(Pb)
            S.op("pool", lambda: nc.gpsimd.tensor_tensor(out=ub[:, j, :], in0=ub[:, j, :], in1=bg[:, j, :], op=ALU.mult),
                 reads=(("ub", j), ("bg", j)), writes=(("ub", j),))
        def sgu_bv(tg):
            Pv = proj_tm(tg, C_BV, 256)
            gsl = 0
            S.op("act", lambda: nc.scalar.activation(out=gvb[:, gsl, :], in_=ps[Pv][:, 0:256], func=AF.Gelu),
                 reads=(PS(Pv),), writes=(("gvb", gsl),))
            ps_free(Pv)
            S.op("dve", lambda: nc.vector.bn_stats(out=st6[:, tg, :], in_=gvb[:, gsl, :]),
                 reads=(("gvb", gsl),), writes=(("st6", tg),))
            S.op("dve", lambda: nc.vector.bn_aggr(out=mv[:, tg, :], in_=st6[:, tg, :]),
                 reads=(("st6", tg),), writes=(("mv", tg),))
            S.op("pool", lambda: nc.gpsimd.tensor_scalar(out=sm[:, 24 + tg:25 + tg], in0=mv[:, tg, 1:2], scalar1=1.0,
                                                        scalar2=EPS, op0=ALU.mult, op1=ALU.add),
                 reads=(("mv", tg),), writes=(("sve", tg),))
            S.op("pool", lambda: nc.gpsimd.tensor_tensor(out=sm[:, 28 + tg:29 + tg], in0=sm[:, 24 + tg:25 + tg],
                                                        in1=sv[:, SV_M05:SV_M05 + 1], op=ALU.pow),
                 reads=(("sve", tg), "sv_c"), writes=(("srs", tg),))
            S.op("dve", lambda: nc.vector.tensor_scalar(out=gvb[:, gsl, :], in0=gvb[:, gsl, :], scalar1=mv[:, tg, 0:1],
                                                       scalar2=sm[:, 28 + tg:29 + tg], op0=ALU.subtract, op1=ALU.mult),
                 reads=(("gvb", gsl), ("mv", tg), ("srs", tg)), writes=(("gvb", gsl),))
            S.op("pool", lambda: nc.gpsimd.tensor_tensor(out=gvb[:, gsl, :], in0=gvb[:, gsl, :], in1=sgg[:, l, :],
                                                        op=ALU.mult),
                 reads=(("gvb", gsl), "sgg"), writes=(("gvb", gsl),))
            S.op("pool", lambda: nc.gpsimd.tensor_tensor(out=vsn[:, tg, :], in0=gvb[:, gsl, :], in1=sgb[:, l, :],
                                                        op=ALU.add),
                 reads=(("gvb", gsl), "sgb"), writes=(("vsn", tg),))
        def sgu_out(g2):
            b = ps_alloc()

            def f():
                i_ = None
                for tg in range(4):
                    for hh in range(2):
                        h = g2 * 2 + hh
                        i_ = nc.tensor.matmul(ps[b][hh * 64:(hh + 1) * 64, tg * 128:(tg + 1) * 128],
                                              lhsT=vsn[:, tg, h * 64:(h + 1) * 64], rhs=Wsg[:, h, :],
                                              start=True, stop=True)
                return i_

            S.op("pe", f, reads=tuple(("vsn", tg) for tg in range(4)) + ("Wsg",), writes=(PS(b),))
            S.op("dve", lambda: nc.vector.tensor_tensor(
                out=tA[:, 2, :].rearrange("p (g t) -> p g t", g=4), in0=ps[b][:].rearrange("p (g t) -> p g t", g=4),
                in1=sbias[:, l, g2, :].unsqueeze(1).to_broadcast([128, 4, 128]), op=ALU.add),
                reads=(PS(b), "sbias"), writes=(("tA", 2),))
            ps_free(b)
            S.op("dve", lambda: nc.vector.scalar_tensor_tensor(out=yT[:, 2 + g2, 16:16 + T], in0=tA[:, 2, :], scalar=0.5,
                                                              in1=ub[:, g2, :], op0=ALU.mult, op1=ALU.mult),
                 reads=(("tA", 2), ("ub", g2)), writes=(("yT", 2 + g2),))
        for h in range(4):
            Pgt = proj_fm(C_GATE // 128 + h)
            S.op("act", lambda: nc.scalar.activation(out=tA[:, 2, :], in_=ps[Pgt][:], func=AF.Tanh, scale=0.5),
                 reads=(PS(Pgt),), writes=(("tA", 2),))
            S.op("dve", lambda: nc.vector.scalar_tensor_tensor(out=gs[:, h, :], in0=tA[:, 2, :], scalar=1.0,
                                                              in1=ps[Pgt][:], op0=ALU.add, op1=ALU.mult),
                 reads=(("tA", 2), PS(Pgt)), writes=(("gs", h),))
            ps_free(Pgt)

        def m_head(h, mid=None):
            sl = h % 2
            QK = QK2[:, sl]
            bf_ = hgrn_state_mm(h, 0, sl)
            bb_ = hgrn_state_mm(h, 1, sl)
            hgrn_recur2(h, bf_, bb_, sl)
            for d_ in range(2):
                b = ps_alloc()

                def f():
                    i_ = None
                    for pr in range(4):
                        i_ = nc.tensor.matmul(ps[b][:, pr * 128:(pr + 1) * 128],
                                              lhsT=QK[:, 1 + 2 * d_, pr * 128:(pr + 1) * 128],
                                              rhs=QK[:, 2 * d_, pr * 128:(pr + 1) * 128], start=True, stop=True)
                    return i_

                S.op("pe", f, reads=(("QK", sl, 2 * d_), ("QK", sl, 1 + 2 * d_)), writes=(PS(b),))
                S.op("dve", lambda: nc.vector.tensor_tensor(
                    out=Am[:, d_, :].rearrange("p (g t) -> p g t", g=4), in0=ps[b][:].rearrange("p (g t) -> p g t", g=4),
                    in1=MASKN[d_].unsqueeze(1).to_broadcast([128, 4, 128]), op=ALU.mult),
                    reads=(PS(b), "cbf"), writes=(("Am", d_),))
                ps_free(b)
            if mid is not None:
                mid()
            b = ps_alloc()

            def fo():
                i_ = None
                for pr in range(4):
                    nc.tensor.matmul(ps[b][:, pr * 128:(pr + 1) * 128], lhsT=vtok[:, pr, h * 128:(h + 1) * 128],
                                     rhs=Am[:, 0, pr * 128:(pr + 1) * 128], start=True, stop=False)
                    nc.tensor.matmul(ps[b][:, pr * 128:(pr + 1) * 128], lhsT=vtok[:, pr, h * 128:(h + 1) * 128],
                                     rhs=Am[:, 1, pr * 128:(pr + 1) * 128], start=False, stop=False)
                    for hf in range(2):
                        c = pr * 2 + hf
                        for d_ in range(2):
                            i_ = nc.tensor.matmul(ps[b][:, c * 64:(c + 1) * 64], lhsT=Sbf[:, d_, c, :],
                                                  rhs=QK[:, 2 * d_, c * 64:(c + 1) * 64], start=False,
                                                  stop=(hf == 1 and d_ == 1))
                return i_

            S.op("pe", fo, reads=("vtok", ("Am", 0), ("Am", 1), ("QK", sl, 0), ("QK", sl, 2)) +
                 tuple(("Sbf", d_, c) for d_ in range(2) for c in range(8)), writes=(PS(b),))
            S.op("act", lambda: nc.scalar.activation(out=osq[:, h, :], in_=ps[b][:], func=AF.Square),
                 reads=(PS(b),), writes=(("osq", h),))
            S.op("act", lambda: nc.scalar.activation(out=big[:, h, :], in_=ps[b][:], func=AF.Copy,
                                                     scale=sv[:, SV_GV + l:SV_GV + l + 1]),
                 reads=(PS(b), "sv"), writes=(("big", h),))
            ps_free(b)
            S.op("pool", lambda: nc.gpsimd.tensor_tensor(out=big[:, h, :], in0=big[:, h, :], in1=gs[:, h, :], op=ALU.mult),
                 reads=(("big", h), ("gs", h)), writes=(("big", h),))

        nxt = (lambda tg: prep_hT(src_d, i + 1, (tg,), part="rest")) if i + 1 < NT else (lambda tg: None)
        nxl = (lambda tg: prep_hT(src_d, i + 1, (tg,), part="load")) if i + 1 < NT else (lambda tg: None)
        pq1 = e_proj(l, 1)
        conv_in(0)
        e_elem(l, 1, pq1)
        pq2 = e_proj(l, 2)
        conv_in(1)
        prev_out = pending.pop() if pending else None

        def mid0():
            sgu_in(0)
            if prev_out:
                prev_out((0, 1), False)

        def mid1():
            sgu_bv(2)
            if prev_out:
                prev_out((2, 3), True)

        m_head(0, mid=mid0)
        sgu_in(1)
        e_elem(l, 2, pq2)
        pq3 = e_proj(l, 3)
        sgu_bv(0)
        sgu_bv(1)
        m_head(1, mid=mid1)
        sgu_bv(3)
        e_elem(l, 3, pq3)
        nxl(0)
        nxl(1)
        nxt(0)
        nxl(2)
        sgu_out(0)

        def mid2():
            nxt(1)
            nxl(3)

        m_head(2, mid=mid2)
        sgu_out(1)
        nxt(2)
        m_head(3, mid=lambda: nxt(3))
        rbuf = [(RC[:, 0, :], ("RC", 0)), (Ff[:], "Ff"), (Eb[:], "Eb"), (Fb[:, 1:T + 1], "Fb")]
        mb = []
        for h in range(4):
            b = ps_alloc()
            S.op("pe", lambda: nc.tensor.matmul(ps[b][:], lhsT=ONES128, rhs=osq[:, h, :], start=True, stop=True),
                 reads=("cbf", ("osq", h)), writes=(PS(b),))
            mb.append(b)

        cbanks = conv_mm(T)
        if True:
            for h in range(4):
                rb, rk = rbuf[h]
                S.op("act", lambda: nc.scalar.activation(out=rb, in_=ps[mb[h]][:], func=AF.Ln,
                                                         bias=sv[:, SV_EPS:SV_EPS + 1], scale=1.0),
                     reads=(PS(mb[h]), "sv_c"), writes=(rk,))
                ps_free(mb[h])
            for h in range(4):
                rb, rk = rbuf[h]
                S.op("act", lambda: nc.scalar.activation(out=rb, in_=rb, func=AF.Exp, scale=-0.5),
                     reads=(rk,), writes=(rk,))

        def hook_b():
            for h in range(4):
                rb, rk = rbuf[h]
                if h % 2 == 0:
                    S.op("dve", lambda: nc.vector.tensor_tensor(out=yT[:, 4 + h, 16:16 + T], in0=big[:, h, :], in1=rb,
                                                               op=ALU.mult),
                         reads=(("big", h), rk), writes=(("yT", 4 + h),))
                else:
                    S.op("pool", lambda: nc.gpsimd.tensor_tensor(out=yT[:, 4 + h, 16:16 + T], in0=big[:, h, :], in1=rb,
                                                                op=ALU.mult),
                         reads=(("big", h), rk), writes=(("yT", 4 + h),))

        conv_tail(l, T, cbanks, hooks={"a": (lambda: None), "b": hook_b})
        if i + 1 < NT:
            tile_start(l, i + 1)
        def do_out(wgs, carry):
            assert len(wgs) <= 2
            for wg in wgs:
                out_proj(l, src_d, dst_d, t0 - 16 + wg * 128, 128, wg * 128, part="load", slot=wg % 2)
            for wg in wgs:
                out_proj(l, src_d, dst_d, t0 - 16 + wg * 128, 128, wg * 128, part="rest", slot=wg % 2)
            if carry:
                S.op("pool", lambda: nc.gpsimd.tensor_copy(out=yT[:, 2:8, 0:16], in_=yT[:, 2:8, T:T + 16]),
                     reads=tuple(("yT", m) for m in range(2, 8)), writes=tuple(("yT", m) for m in range(2, 8)))

        S.op("pool", lambda: nc.gpsimd.tensor_copy(out=ybuf[:, :, 0:31], in_=ybuf[:, :, T:T + 31]),
             reads=("ybuf",), writes=("ybuf",))
        S.op("pool", lambda: nc.gpsimd.tensor_copy(out=ag[:, :, 0:16], in_=ag[:, :, T:T + 16]),
             reads=("ag",), writes=("ag",))
        if i + 1 < NT:
            pending.append(do_out)
        else:
            do_out((0, 1), False)
            do_out((2, 3), True)

    def epilogue(l, src_d, dst_d):
        S.op("pool", lambda: nc.gpsimd.memset(ybuf[:, :, 31:64], 0.0), reads=(), writes=("ybuf",))
        conv_tail(l, 16)
        out_proj(l, src_d, dst_d, L - 16, 16, 0)

    def layer(l, src_d, dst_d):
        load_weights(l)
        S.op("dve", lambda: nc.vector.memset(Sb2[:], 0.0), writes=tuple(("Sb", h, p_) for h in range(4) for p_ in range(2)))
        for i in range(NT - 1, -1, -1):
            pass1_tile(l, src_d, i)
        S.op("dve", lambda: nc.vector.memset(Gf2[:], 0.0), writes=tuple(("Gf", h, p_) for h in range(4) for p_ in range(2)))
        S.op("dve", lambda: nc.vector.memset(dcar[:], 1.0), writes=tuple(("dcar", h) for h in range(4)))
        S.op("pool", lambda: nc.gpsimd.memset(ybuf[:], 0.0), writes=("ybuf",))
        S.op("pool", lambda: nc.gpsimd.memset(ag[:], 0.0), writes=("ag",))
        S.op("pool", lambda: nc.gpsimd.memset(yT[:], 0.0), writes=tuple(("yT", m) for m in range(8)))
        prep_hT(src_d, 0)
        started.clear()
        for i in range(NT):
            pass2_tile(l, src_d, dst_d, i)
        epilogue(l, src_d, dst_d)

    class _Stop(Exception):
        pass

    def layer_staged(l, src_d, dst_d):
        if stage < 1:
            raise _Stop()
        load_weights(l)
        if stage < 2:
            raise _Stop()
        S.op("dve", lambda: nc.vector.memset(Sb2[:], 0.0), writes=tuple(("Sb", h, p_) for h in range(4) for p_ in range(2)))
        for i in range(NT - 1, -1, -1):
            pass1_tile(l, src_d, i)
            if stage < 3:
                raise _Stop()
        if stage < 4:
            raise _Stop()
        S.op("dve", lambda: nc.vector.memset(Gf2[:], 0.0), writes=tuple(("Gf", h, p_) for h in range(4) for p_ in range(2)))
        S.op("dve", lambda: nc.vector.memset(dcar[:], 1.0), writes=tuple(("dcar", h) for h in range(4)))
        S.op("pool", lambda: nc.gpsimd.memset(ybuf[:], 0.0), writes=("ybuf",))
        S.op("pool", lambda: nc.gpsimd.memset(ag[:], 0.0), writes=("ag",))
        S.op("pool", lambda: nc.gpsimd.memset(yT[:], 0.0), writes=tuple(("yT", m) for m in range(8)))
        prep_hT(src_d, 0)
        started.clear()
        for i in range(NT):
            pass2_tile(l, src_d, dst_d, i)
            if stage < 5:
                raise _Stop()
        if stage < 6:
            raise _Stop()
        epilogue(l, src_d, dst_d)
        if stage < 7:
            raise _Stop()

    if stage < 99:
        setup()
        try:
            layer_staged(0, x_d, y_d)
        except (_Stop, StopBuild):
            pass
        for e in ("pe", "act", "dve", "pool"):
            if S.cnt[e] > 0:
                nc.sync.wait_ge(S.sems[e], S.cnt[e])
        S.finish([k for k in S.cnt if k not in ("pe", "act", "dve", "pool")])
        es.close()
        build.stats = (S.nops, S.nwaits)
        return nc

    setup()
    layer(0, x_d, x1_d)
    for key_ in list(S.last_w.keys()):
        if isinstance(key_, tuple) and key_[0] == "xrow1":
            S.last_w[("xrow",) + key_[1:]] = S.last_w[key_]
    layer(1, x1_d, y_d)
    S.finish(out_keys)
    es.close()
    build.stats = (S.nops, S.nwaits)
    return nc


def make_consts():
    c = np.zeros((128, 640), np.float32)
    c[:, 0:128] = np.eye(128)
    s = np.arange(128)[:, None]
    t = np.arange(128)[None, :]
    same = (s // 64) == (t // 64)
    c[:, 128:256] = -(same & (s <= t)).astype(np.float32)
    c[:, 256:384] = -(same & (s >= t)).astype(np.float32)
    c[:, 384:512] = 1.0 / 128
    c[:, 512:640] = 1.0 / 256
    csm = np.full((128, 512), 1e-35, np.float32)
    csm[:, ::64] = 1.0
    return c.astype(ml_dtypes.bfloat16), csm


_WNAMES = ["norm_g", "w_in", "conv_w", "conv_b", "conv_ln_g", "conv_ln_b", "conv_pw", "sgu_ln_g", "sgu_ln_b",
           "sgu_w", "sgu_b", "hgrn_lb_fwd", "hgrn_lb_bwd", "hgrn_norm_g", "w_out", "final_norm_g"]


def kernel(x_prompt, x_sample, **w):
    L = SEQ_P
    nc = build(L)
    cbf, csm = make_consts()
    base = {k: np.ascontiguousarray(np.asarray(w[k], dtype=np.float32)) for k in _WNAMES}
    base["cbf"] = cbf
    base["csm"] = csm
    xp = np.asarray(x_prompt, dtype=np.float32)
    xsm = np.asarray(x_sample, dtype=np.float32)
    in_maps = []
    for c in range(8):
        m = dict(base)
        if c < 4:
            m["x"] = np.ascontiguousarray(xp[c])
        else:
            pad = np.zeros((L, D), np.float32)
            pad[:SEQ_S] = xsm[c - 4]
            m["x"] = pad
        in_maps.append(m)
    res = run_bass_kernel_spmd(nc, in_maps, core_ids=list(range(8)))
    yp = np.stack([np.asarray(res.results[c]["y"], dtype=np.float32) for c in range(4)], axis=0)
    ys = np.stack([np.asarray(res.results[c]["y"], dtype=np.float32)[:SEQ_S] for c in range(4, 8)], axis=0)
    return (yp, ys)
```
